# Optimizing a Trainium2 kernel written in Bass

```python
import jax, jax.numpy as jnp
from jax import lax
import numpy as np

D_MODEL = 4096
BATCH = 4
SEQ = 4096
DEPTH = 2
DEC_BATCH = 16
DEC_SEQ = 32
PAST_LEN = 2048

CHUNK = 64
MIX_DIM = D_MODEL
RWKV_HEAD = 64
RWKV_DIM = D_MODEL // 4
RWKV_HEADS = RWKV_DIM // RWKV_HEAD
RWKV_LORA = 64
RWKV_SHIFT_DIM = 3 * RWKV_DIM + 2 * RWKV_LORA
RWKV_GN_EPS = 64e-5
MLA_HEADS = 16
MLA_NOPE = 128
MLA_ROPE = 64
MLA_VHEAD = 128
MLA_DIM = MLA_HEADS * MLA_VHEAD
Q_LORA = D_MODEL // 4
KV_LORA = 512
ROPE_BASE = 10000.0
Q_BLOCK = 128
CONV_DIM = MIX_DIM - RWKV_DIM - MLA_DIM
CONV_W = 3
NORM_EPS = 1e-6
IN_COLS = (RWKV_SHIFT_DIM + RWKV_DIM + Q_LORA + KV_LORA + MLA_ROPE + MLA_DIM
           + 4 * CONV_DIM)

kernel_name = "hybrid_rwkv7_mla_shortconv_stream_step"


def rms_norm(x, g, eps=NORM_EPS):
    xf = x.astype(jnp.float32)
    y = xf * lax.rsqrt(jnp.mean(xf * xf, axis=-1, keepdims=True) + eps)
    return (y * g.astype(jnp.float32)).astype(x.dtype)


def rotary(x, pos):
    half = MLA_ROPE // 2
    freqs = ROPE_BASE ** (-jnp.arange(half, dtype=jnp.float32) / half)
    ang = pos.astype(jnp.float32)[:, None] * freqs[None, :]
    ang = ang.reshape((ang.shape[0],) + (1,) * (x.ndim - 3) + (half,))
    cos, sin = jnp.cos(ang), jnp.sin(ang)
    xf = x.astype(jnp.float32)
    x1, x2 = xf[..., :half], xf[..., half:]
    return jnp.concatenate([x1 * cos - x2 * sin, x2 * cos + x1 * sin], axis=-1).astype(x.dtype)


def split_columns(p):
    sizes = (RWKV_SHIFT_DIM, RWKV_DIM, Q_LORA, KV_LORA, MLA_ROPE, MLA_DIM,
             CONV_DIM, CONV_DIM, CONV_DIM, CONV_DIM)
    out, start = [], 0
    for s in sizes:
        out.append(p[..., start:start + s])
        start += s
    return out


def block_causal_attention(q_nope, q_rope, k_nope, k_rope, v, q_pos, k_pos):
    scale = (MLA_NOPE + MLA_ROPE) ** -0.5
    k_chunk = k_pos // CHUNK
    q_chunk = q_pos // CHUNK

    def one_block(args):
        qn, qr, qc = args
        s = (jnp.einsum('bqhd,bkhd->bhqk', qn, k_nope).astype(jnp.float32)
             + jnp.einsum('bqhd,bkd->bhqk', qr, k_rope).astype(jnp.float32)) * scale
        visible = k_chunk[None, :] <= qc[:, None]
        s = jnp.where(visible[None, None], s, -1e30)
        p = jax.nn.softmax(s, axis=-1)
        return jnp.einsum('bhqk,bkhd->bqhd', p.astype(v.dtype), v)

    B, T = q_nope.shape[:2]
    if T > Q_BLOCK and T % Q_BLOCK == 0:
        nb = T // Q_BLOCK

        def blocks(t):
            return jnp.moveaxis(t.reshape((B, nb, Q_BLOCK) + t.shape[2:]), 1, 0)

        out = lax.map(one_block, (blocks(q_nope), blocks(q_rope), q_chunk.reshape(nb, Q_BLOCK)))
        return jnp.moveaxis(out, 0, 1).reshape((B, T) + out.shape[3:])
    return one_block((q_nope, q_rope, q_chunk))


def rwkv7_scan(S0, r, w, k, v, kk, a):
    def step(S, inp):
        r_t, w_t, k_t, v_t, kk_t, a_t = inp
        sa = jnp.einsum('bhvk,bhk->bhv', S, -kk_t)
        S = (S * w_t[:, :, None, :] + sa[..., :, None] * (kk_t * a_t)[..., None, :]
             + v_t[..., :, None] * k_t[..., None, :])
        o = jnp.einsum('bhvk,bhk->bhv', S, r_t)
        return S, o

    xs = tuple(jnp.moveaxis(t, 1, 0) for t in (r, w, k, v, kk, a))
    S, o = lax.scan(step, S0, xs)
    return S, jnp.moveaxis(o, 0, 1)


def mixer_layer(x, c, rw_state, rw_shift, conv_buf, lat_past, kr_past,
                w_ada, b_ada, norm_g, w_in, rw_mu, rw_w0, rw_w2, rw_a0, rw_a2,
                rw_kk, rw_ka, rw_rk, rw_ln_g, rw_ln_b, mla_qnorm_g, mla_kvnorm_g,
                mla_w_uq, mla_w_uk, mla_w_uv, mla_qn_nope, mla_qn_rope,
                mla_kn_nope, mla_kn_rope, conv_w, conv_b, w_out):
    B, T, _ = x.shape
    P = lat_past.shape[1]
    f32 = jnp.float32
    q_pos = jnp.arange(P, P + T, dtype=jnp.int32)
    k_pos = jnp.arange(P + T, dtype=jnp.int32)

    mod = c @ w_ada + b_ada
    shift, scale, gate = jnp.split(mod, 3, axis=-1)
    h = rms_norm(x, norm_g) * (1 + scale[:, None]) + shift[:, None]
    proj = h @ w_in
    (rw_pre, rw_gate, cq, ckv, kr_raw, mla_gate,
     cv_b, cv_c, cv_x, cv_gate) = split_columns(proj)

    prev = jnp.concatenate([rw_shift[:, None].astype(rw_pre.dtype), rw_pre[:, :-1]], axis=1)
    xs = rw_pre + (prev - rw_pre) * rw_mu
    r = xs[..., :RWKV_DIM]
    k = xs[..., RWKV_DIM:2 * RWKV_DIM]
    v = xs[..., 2 * RWKV_DIM:3 * RWKV_DIM]
    wl = xs[..., 3 * RWKV_DIM:3 * RWKV_DIM + RWKV_LORA]
    al = xs[..., 3 * RWKV_DIM + RWKV_LORA:]
    w_log = -jax.nn.softplus(-(rw_w0 + jnp.tanh(wl) @ rw_w2).astype(f32)) - 0.5
    decay = jnp.exp(-jnp.exp(w_log))
    a = jax.nn.sigmoid((rw_a0 + al @ rw_a2).astype(f32))

    def heads(t):
        return t.astype(f32).reshape(B, T, RWKV_HEADS, RWKV_HEAD)

    kk = heads(k.astype(f32) * rw_kk.astype(f32))
    kk = kk * lax.rsqrt(jnp.sum(kk * kk, axis=-1, keepdims=True) + 1e-12)
    k_mod = k.astype(f32) * (1 + (a - 1) * rw_ka.astype(f32))
    rh, kh, vh, ah, dh = heads(r), heads(k_mod), heads(v), heads(a), heads(decay)
    new_rw_state, o = rwkv7_scan(rw_state.astype(f32), rh, dh, kh, vh, kk, ah)
    mu = jnp.mean(o, axis=-1, keepdims=True)
    var = jnp.mean(jnp.square(o - mu), axis=-1, keepdims=True)
    o = ((o - mu) * lax.rsqrt(var + RWKV_GN_EPS)).reshape(B, T, RWKV_DIM)
    o = o * rw_ln_g.astype(f32) + rw_ln_b.astype(f32)
    bonus = jnp.sum(rh * kh * rw_rk.astype(f32), axis=-1, keepdims=True) * vh
    rw_out = (o + bonus.reshape(B, T, RWKV_DIM)).astype(x.dtype) * jax.nn.silu(rw_gate)

    q = (rms_norm(cq, mla_qnorm_g) @ mla_w_uq).reshape(B, T, MLA_HEADS, MLA_NOPE + MLA_ROPE)
    q_nope = rms_norm(q[..., :MLA_NOPE], mla_qn_nope)
    q_rope = rotary(rms_norm(q[..., MLA_NOPE:], mla_qn_rope), q_pos)
    lat_new = rms_norm(ckv, mla_kvnorm_g)
    kr_new = rotary(rms_norm(kr_raw, mla_kn_rope), q_pos)
    lat_all = jnp.concatenate([lat_past.astype(lat_new.dtype), lat_new], axis=1)
    kr_all = jnp.concatenate([kr_past.astype(kr_new.dtype), kr_new], axis=1)
    k_nope = rms_norm((lat_all @ mla_w_uk).reshape(B, P + T, MLA_HEADS, MLA_NOPE), mla_kn_nope)
    v_mla = (lat_all @ mla_w_uv).reshape(B, P + T, MLA_HEADS, MLA_VHEAD)
    attn = block_causal_attention(q_nope, q_rope, k_nope, kr_all, v_mla, q_pos, k_pos)
    mla_out = attn.reshape(B, T, MLA_DIM) * jax.nn.silu(mla_gate)

    u = cv_c * cv_x
    up = jnp.concatenate([conv_buf.astype(u.dtype), u], axis=1)
    y = conv_b + up[:, 0:T] * conv_w[0]
    for j in range(1, CONV_W):
        y = y + up[:, j:j + T] * conv_w[j]
    cv_out = cv_b * y * jax.nn.silu(cv_gate)
    new_conv = up[:, T:]

    mix = jnp.concatenate([rw_out, mla_out, cv_out], axis=-1) @ w_out
    x_out = x + gate[:, None] * mix
    return x_out, lat_new, kr_new, new_rw_state.astype(x.dtype), rw_pre[:, -1], new_conv


def setup_inputs(seed: int = 0) -> dict:
    key = jax.random.key(seed)
    ks = iter(jax.random.split(key, 48))

    def nrm(shape, s):
        return s * jax.random.normal(next(ks), shape, jnp.float32)

    D = D_MODEL
    return {
        "x_prompt": nrm((BATCH, SEQ, D), 1.0),
        "x_sample": nrm((DEC_BATCH, DEC_SEQ, D), 1.0),
        "c_prompt": nrm((BATCH, D), 1.0),
        "c_sample": nrm((DEC_BATCH, D), 1.0),
        "cache_mla_latent": nrm((DEPTH, DEC_BATCH, PAST_LEN, KV_LORA), 1.0),
        "cache_mla_krope": nrm((DEPTH, DEC_BATCH, PAST_LEN, MLA_ROPE), 1.0),
        "state_rwkv": nrm((DEPTH, DEC_BATCH, RWKV_HEADS, RWKV_HEAD, RWKV_HEAD), 0.3),
        "state_rwkv_shift": nrm((DEPTH, DEC_BATCH, RWKV_SHIFT_DIM), 1.0),
        "state_conv": nrm((DEPTH, DEC_BATCH, CONV_W - 1, CONV_DIM), 1.0),
        "w_ada": nrm((DEPTH, D, 3 * D), 0.3 * D ** -0.5),
        "b_ada": nrm((DEPTH, 3 * D), 0.02),
        "norm_g": 1.0 + nrm((DEPTH, D), 0.02),
        "w_in": nrm((DEPTH, D, IN_COLS), D ** -0.5),
        "rw_mu": jax.random.uniform(next(ks), (DEPTH, RWKV_SHIFT_DIM), jnp.float32),
        "rw_w0": nrm((DEPTH, RWKV_DIM), 0.5),
        "rw_w2": nrm((DEPTH, RWKV_LORA, RWKV_DIM), 0.5 * RWKV_LORA ** -0.5),
        "rw_a0": nrm((DEPTH, RWKV_DIM), 0.5),
        "rw_a2": nrm((DEPTH, RWKV_LORA, RWKV_DIM), 0.5 * RWKV_LORA ** -0.5),
        "rw_kk": 0.85 + nrm((DEPTH, RWKV_DIM), 0.02),
        "rw_ka": 1.0 + nrm((DEPTH, RWKV_DIM), 0.02),
        "rw_rk": nrm((DEPTH, RWKV_HEADS, RWKV_HEAD), 0.1),
        "rw_ln_g": 1.0 + nrm((DEPTH, RWKV_DIM), 0.02),
        "rw_ln_b": nrm((DEPTH, RWKV_DIM), 0.02),
        "mla_qnorm_g": 1.0 + nrm((DEPTH, Q_LORA), 0.02),
        "mla_kvnorm_g": 1.0 + nrm((DEPTH, KV_LORA), 0.02),
        "mla_w_uq": nrm((DEPTH, Q_LORA, MLA_HEADS * (MLA_NOPE + MLA_ROPE)), Q_LORA ** -0.5),
        "mla_w_uk": nrm((DEPTH, KV_LORA, MLA_HEADS * MLA_NOPE), KV_LORA ** -0.5),
        "mla_w_uv": nrm((DEPTH, KV_LORA, MLA_HEADS * MLA_VHEAD), KV_LORA ** -0.5),
        "mla_qn_nope": 1.0 + nrm((DEPTH, MLA_NOPE), 0.02),
        "mla_qn_rope": 1.0 + nrm((DEPTH, MLA_ROPE), 0.02),
        "mla_kn_nope": 1.0 + nrm((DEPTH, MLA_NOPE), 0.02),
        "mla_kn_rope": 1.0 + nrm((DEPTH, MLA_ROPE), 0.02),
        "conv_w": nrm((DEPTH, CONV_W, CONV_DIM), CONV_W ** -0.5),
        "conv_b": nrm((DEPTH, CONV_DIM), 0.02),
        "w_out": nrm((DEPTH, MIX_DIM, D), MIX_DIM ** -0.5),
    }


def reference(x_prompt, x_sample, c_prompt, c_sample, cache_mla_latent, cache_mla_krope,
              state_rwkv, state_rwkv_shift, state_conv, w_ada, b_ada, norm_g, w_in,
              rw_mu, rw_w0, rw_w2, rw_a0, rw_a2, rw_kk, rw_ka, rw_rk, rw_ln_g, rw_ln_b,
              mla_qnorm_g, mla_kvnorm_g, mla_w_uq, mla_w_uk, mla_w_uv, mla_qn_nope,
              mla_qn_rope, mla_kn_nope, mla_kn_rope, conv_w, conv_b, w_out):
    yp, ys = x_prompt, x_sample
    Bp = x_prompt.shape[0]
    dt = x_prompt.dtype
    lat_p, kr_p, rw_p, sh_p, cv_p = [], [], [], [], []
    lat_s, kr_s, rw_s, sh_s, cv_s = [], [], [], [], []
    for l in range(DEPTH):
        lp = dict(w_ada=w_ada[l], b_ada=b_ada[l], norm_g=norm_g[l], w_in=w_in[l],
                  rw_mu=rw_mu[l], rw_w0=rw_w0[l], rw_w2=rw_w2[l], rw_a0=rw_a0[l],
                  rw_a2=rw_a2[l], rw_kk=rw_kk[l], rw_ka=rw_ka[l], rw_rk=rw_rk[l],
                  rw_ln_g=rw_ln_g[l], rw_ln_b=rw_ln_b[l], mla_qnorm_g=mla_qnorm_g[l],
                  mla_kvnorm_g=mla_kvnorm_g[l], mla_w_uq=mla_w_uq[l], mla_w_uk=mla_w_uk[l],
                  mla_w_uv=mla_w_uv[l], mla_qn_nope=mla_qn_nope[l], mla_qn_rope=mla_qn_rope[l],
                  mla_kn_nope=mla_kn_nope[l], mla_kn_rope=mla_kn_rope[l], conv_w=conv_w[l],
                  conv_b=conv_b[l], w_out=w_out[l])
        yp, lat, kr, rw, sh, cv = mixer_layer(
            yp, c_prompt,
            jnp.zeros((Bp, RWKV_HEADS, RWKV_HEAD, RWKV_HEAD), jnp.float32),
            jnp.zeros((Bp, RWKV_SHIFT_DIM), dt),
            jnp.zeros((Bp, CONV_W - 1, CONV_DIM), dt),
            jnp.zeros((Bp, 0, KV_LORA), dt),
            jnp.zeros((Bp, 0, MLA_ROPE), dt), **lp)
        lat_p.append(lat); kr_p.append(kr); rw_p.append(rw); sh_p.append(sh); cv_p.append(cv)
        ys, lat, kr, rw, sh, cv = mixer_layer(
            ys, c_sample, state_rwkv[l], state_rwkv_shift[l], state_conv[l],
            cache_mla_latent[l], cache_mla_krope[l], **lp)
        lat_s.append(lat); kr_s.append(kr); rw_s.append(rw); sh_s.append(sh); cv_s.append(cv)
    return (yp, ys,
            jnp.stack(lat_p), jnp.stack(kr_p), jnp.stack(rw_p), jnp.stack(sh_p), jnp.stack(cv_p),
            jnp.stack(lat_s), jnp.stack(kr_s), jnp.stack(rw_s), jnp.stack(sh_s), jnp.stack(cv_s))
```

```python
import contextlib
import math
import numpy as np
import concourse.bass as bass
import concourse.mybir as mybir
from concourse.bass_utils import run_bass_kernel_spmd

F32 = mybir.dt.float32
BF16 = mybir.dt.bfloat16
I32 = mybir.dt.int32
AF = mybir.ActivationFunctionType
ALU = mybir.AluOpType
AX = mybir.AxisListType

D = 4096
DC = 32
H = 16
L_FULL = 2
SEQ_FULL = 4096
PAST = 2048
SS = 32
NTP = 512
IN_COLS = 11968
OFF = dict(rw_pre=0, rw_gate=3200, cq=4224, ckv=5248, kr=5760, mla_gate=5824,
           cv_b=7872, cv_c=8896, cv_x=9920, cv_gate=10944)
QSCALE = 192.0 ** -0.5
C0 = math.exp(-0.5)
NORM_EPS = 1e-6
GN_EPS = 64e-5

PV = {}
_c = 0
for _n, _w in (("norm_g", 32), ("b_shift", 32), ("b_scale", 32), ("mu", 25), ("w0", 8), ("a0", 8),
               ("kk", 8), ("ka", 8), ("rk", 8), ("qng", 8), ("kvg", 4), ("qn_nope", 1), ("qn_rope", 1),
               ("kn_nope", 1), ("kn_rope", 1), ("conv_w", 24), ("conv_b", 8), ("ln_g", 8), ("ln_b", 8)):
    PV[_n] = _c
    _c += _w
NPV = _c

ENGS = ("pe", "act", "dve", "pool", "sp")
EPOCH = 20000
NDMA = 8
SAME_ENGINE_SYNC = True


class Op:
    __slots__ = ("eng", "fn", "deps", "is_dma", "sig", "dsem", "dval", "prevd")

    def __init__(self, eng, fn, is_dma):
        self.eng = eng
        self.fn = fn
        self.is_dma = is_dma
        self.deps = []
        self.sig = None
        self.dsem = None
        self.dval = 0
        self.prevd = None


class Prog:
    def __init__(self, nc):
        self.nc = nc
        self.ops = {e: [] for e in ENGS}
        self.lastw = {}
        self.readers = {}
        self.nsig = {e: 0 for e in ENGS}
        self.ndma = {e: 0 for e in ENGS}
        self.dma_last = {}

    def rekey(self, old_keys, new_keys):
        ops = []
        for k in old_keys:
            w = self.lastw.get(k)
            if w is not None:
                ops.append(w)
            ops.extend(self.readers.get(k, ()))
        for k in new_keys:
            self.lastw.pop(k, None)
            self.readers[k] = list(ops)

    def op(self, eng, fn, reads=(), writes=(), dma=False):
        o = Op(eng, fn, dma)
        psr = [k for k in reads if k.startswith("ps") and k not in writes]
        if psr:
            writes = list(writes) + psr
        deps = []
        for k in reads:
            w = self.lastw.get(k)
            if w is not None:
                deps.append(w)
        for k in writes:
            w = self.lastw.get(k)
            if w is not None:
                deps.append(w)
            deps.extend(self.readers.get(k, ()))
        seen = set()
        for d in deps:
            if id(d) in seen or d is o:
                continue
            seen.add(id(d))
            if d.is_dma:
                o.deps.append(d)
            elif d.eng == eng:
                if eng == "pe" or not SAME_ENGINE_SYNC:
                    continue
                o.deps.append(d)
                if d.sig is None:
                    d.sig = -1
            else:
                o.deps.append(d)
                if d.sig is None:
                    d.sig = -1
        for k in reads:
            self.readers.setdefault(k, []).append(o)
        for k in writes:
            self.lastw[k] = o
            self.readers[k] = []
        if dma:
            j = self.ndma[eng]
            self.ndma[eng] = j + 1
            slot = j % NDMA
            o.dsem = (eng, slot)
            o.dval = 16 * (j // NDMA + 1)
            o.prevd = self.dma_last.get((eng, slot))
            self.dma_last[(eng, slot)] = o
        self.ops[eng].append(o)
        return o

    def emit(self):
        nc = self.nc
        for e in ENGS:
            n = 0
            for o in self.ops[e]:
                if o.sig is not None and not o.is_dma:
                    o.sig = n
                    n += 1
            self.nsig[e] = n
        with contextlib.ExitStack() as st:
            esems = {}
            for e in ENGS:
                ne = (self.nsig[e] + EPOCH - 1) // EPOCH
                esems[e] = [st.enter_context(nc.semaphore(f"s_{e}_{i}")) for i in range(ne)]
            dsems = {}
            for e in ENGS:
                for s in range(min(NDMA, self.ndma[e])):
                    dsems[(e, s)] = st.enter_context(nc.semaphore(f"d_{e}_{s}"))
            block = st.enter_context(nc.Block())
            final_dma = dict(self.dma_last)

            def make(e):
                def body(eng):
                    seen_sig = {}
                    seen_dma = {}
                    for o in self.ops[e]:
                        waits = []
                        for d in o.deps:
                            if d.is_dma:
                                if seen_dma.get(d.dsem, 0) < d.dval:
                                    seen_dma[d.dsem] = d.dval
                                    waits.append((dsems[d.dsem], d.dval))
                            else:
                                if seen_sig.get(d.eng, -1) < d.sig:
                                    seen_sig[d.eng] = d.sig
                                    waits.append((esems[d.eng][d.sig // EPOCH], d.sig % EPOCH + 1))
                        if o.is_dma and o.prevd is not None:
                            d = o.prevd
                            if seen_dma.get(d.dsem, 0) < d.dval:
                                seen_dma[d.dsem] = d.dval
                                waits.append((dsems[d.dsem], d.dval))
                        for (s, v) in waits:
                            eng.wait_ge(s, v)
                        ins = o.fn(eng)
                        if o.is_dma:
                            ins.then_inc(dsems[o.dsem], 16)
                        elif o.sig is not None:
                            ins.then_inc(esems[e][o.sig // EPOCH], 1)
                    if e == "sp":
                        for k, d in final_dma.items():
                            if seen_dma.get(d.dsem, 0) < d.dval:
                                eng.wait_ge(dsems[d.dsem], d.dval)
                return body

            block.tensor(make("pe"))
            block.scalar(make("act"))
            block.vector(make("dve"))
            block.gpsimd(make("pool"))
            block.sync(make("sp"))


class Ctx:
    pass


def build(SEQ=SEQ_FULL, NL=L_FULL, phases=(1, 2, 3, 4, 5)):
    nc = bass.Bass("TRN2", target_bir_lowering=False)
    NPT = SEQ // NTP
    LKS = PAST + SS

    def din(name, shape, dt=F32):
        return nc.dram_tensor(name, list(shape), dt, kind="ExternalInput").ap()

    def dout(name, shape, dt=F32):
        return nc.dram_tensor(name, list(shape), dt, kind="ExternalOutput").ap()

    def dscr(name, shape, dt=F32):
        return nc.dram_tensor(name, list(shape), dt).ap()

    xp = din("xp", [SEQ, D])
    xs_in = din("xs", [2 * SS, D])
    cT_in = din("cT", [128, DC, 3])
    cache_lat = din("cache_lat", [NL, 2, PAST, 512])
    cache_kr = din("cache_kr", [NL, 2, PAST, 64])
    st_M = din("st_M", [NL, 2, 128, 8, 64])
    st_shift = din("st_shift", [NL, 2, 128, 25])
    st_conv = din("st_conv", [NL, 2, 128, 8, 2])
    w_ada = din("w_ada", [NL, D, 3 * D])
    b_gate = din("b_gate", [NL, 1, D])
    w_in = din("w_in", [NL, D, IN_COLS])
    w_out = din("w_out", [NL, D, D])
    w_uq = din("w_uq", [NL, 1024, 3072])
    w_uk = din("w_uk", [NL, 512, 2048])
    w_uv = din("w_uv", [NL, 512, 2048])
    w_w2 = din("rw_w2", [NL, 64, 1024])
    w_a2 = din("rw_a2", [NL, 64, 1024])
    pv_in = din("pv", [NL, 128, NPV])
    c_ident = din("c_ident", [128, 128])
    c_matt = din("c_matt", [128, 4, NTP])
    c_msu = din("c_msu", [128, 128])
    c_msl = din("c_msl", [128, 128])
    c_mu = din("c_mu", [128, 64])
    c_msu32 = din("c_msu32", [64, 64])
    c_msl32 = din("c_msl32", [64, 64])
    c_mu32 = din("c_mu32", [64, 32])
    c_bones = din("c_bones", [128, 128])
    c_pswap = din("c_pswap", [64, 64])
    c_pos = din("c_pos", [64, SEQ + 2 * SS])
    c_fidx = din("c_fidx", [64, 1])

    y_p = dout("y_p", [SEQ, D])
    y_s = dout("y_s", [2 * SS, D])
    lat_p = dout("lat_p", [NL, SEQ, 512])
    kr_p = dout("kr_p", [NL, SEQ, 64])
    rw_p = dout("rw_p", [NL, H, 64, 64])
    sh_p = dout("sh_p", [NL, 3200])
    cv_p = dout("cv_p", [NL, 2, 1024])
    lat_s = dout("lat_s", [NL, 2 * SS, 512])
    kr_s = dout("kr_s", [NL, 2 * SS, 64])
    rw_s = dout("rw_s", [NL, 2, H, 64, 64])
    sh_s = dout("sh_s", [NL, 2, 3200])
    cv_s = dout("cv_s", [NL, 2, 2, 1024])

    x1_p = dscr("x1_p", [SEQ, D])
    x1_s = dscr("x1_s", [2 * SS, D])
    gate_d = dscr("gate_d", [3, D])
    LK = [SEQ, LKS, LKS]
    kT_d = [[dscr(f"kT_{l}_{s}", [H, 128, LK[s]], BF16) for s in range(3)] for l in range(NL)]
    krT_d = [[dscr(f"krT_{l}_{s}", [64, LK[s]], BF16) for s in range(3)] for l in range(NL)]
    V_d = [[dscr(f"V_{l}_{s}", [LK[s], 2048], BF16) for s in range(3)] for l in range(NL)]

    st = contextlib.ExitStack()
    with st:
        def sb(name, shape, dt=F32):
            return st.enter_context(nc.sbuf_tensor("sb_" + name, list(shape), dt))

        P = Prog(nc)

        def _op(eng, fn, r, w, dma=False):
            return P.op(eng, fn, reads=r, writes=w, dma=dma)

        def mm(out, lhsT, rhs, start, stop, r, w):
            _op("pe", lambda e: e.matmul(out, lhsT=lhsT, rhs=rhs, start=start, stop=stop), r, w)

        def tr(out, in_, ident_ap, r, w):
            _op("pe", lambda e: e.transpose(out, in_, ident_ap), r, w)

        def act(out, in_, func, r, w, bias=None, scale=None, accum_out=None):
            kw = {}
            if bias is not None:
                kw["bias"] = bias
            if scale is not None:
                kw["scale"] = scale
            if accum_out is not None:
                kw["accum_out"] = accum_out
            _op("act", lambda e: e.activation(out=out, in_=in_, func=func, **kw), r, w)

        def ts(eng, out, in0, s1, s2, op0, op1, r, w):
            if op1 is None:
                _op(eng, lambda e: e.tensor_scalar(out=out, in0=in0, scalar1=s1, scalar2=None, op0=op0), r, w)
            else:
                _op(eng, lambda e: e.tensor_scalar(out=out, in0=in0, scalar1=s1, scalar2=s2, op0=op0, op1=op1), r, w)

        def tt(eng, out, in0, in1, op, r, w):
            _op(eng, lambda e: e.tensor_tensor(out=out, in0=in0, in1=in1, op=op), r, w)

        def stt(out, in0, scalar, in1, op0, op1, r, w):
            _op("dve", lambda e: e.scalar_tensor_tensor(out=out, in0=in0, scalar=scalar, in1=in1, op0=op0, op1=op1), r, w)

        def cp(eng, out, in_, r, w):
            if eng == "act":
                _op("act", lambda e: e.activation(out=out, in_=in_, func=AF.Copy), r, w)
            else:
                _op(eng, lambda e: e.tensor_copy(out=out, in_=in_), r, w)

        def memset(eng, ap, val, w):
            _op(eng, lambda e: e.memset(ap, val), (), w)

        def dma(eng, out, in_, r, w):
            _op(eng, lambda e: e.dma_start(out=out, in_=in_), r, w, dma=True)

        def red(out, in_, r, w):
            _op("dve", lambda e: e.tensor_reduce(out=out, in_=in_, axis=AX.X, op=ALU.add), r, w)

        def recip(out, in_, r, w):
            _op("dve", lambda e: e.reciprocal(out=out, in_=in_), r, w)

        banks = [st.enter_context(nc.psum_tensor(f"ps{i}", [128, 512], F32)) for i in range(8)]
        NROT = 6
        rot = [0]
        busy = [False] * 8

        class Bank:
            def __init__(self, i):
                self.i = i
                self.t = banks[i]
                self.k = f"ps{i}"

            def rel(self):
                busy[self.i] = False

            def v3(self, rows, a, b):
                return self.t[0:rows, 0:a * b].rearrange("p (a b) -> p a b", a=a)

        def nb():
            i = rot[0] % NROT
            rot[0] += 1
            assert not busy[i], f"psum bank {i} still busy"
            busy[i] = True
            return Bank(i)

        ACC0 = Bank(6)
        ACC1 = Bank(7)

        evq = [0]

        def ev_eng():
            evq[0] += 1
            return "act" if evq[0] % 2 else "dve"

        hT = sb("hT", [128, DC, NTP], BF16)
        mixT = sb("mixT", [128, DC, NTP], BF16)
        wbuf = [sb(f"wbuf{i}", [128, DC, 256], BF16) for i in range(2)]
        wq = [0]
        ident = sb("ident", [128, 128])
        onesb = sb("onesb", [128, 128], BF16)
        onesf = sb("onesf", [128, 128])
        bones = sb("bones", [128, 128])
        msu = sb("msu", [128, 128]); msl = sb("msl", [128, 128]); mup = sb("mup", [128, 64]); mun = sb("mun", [128, 64])
        msu32 = sb("msu32", [64, 64]); msl32 = sb("msl32", [64, 64]); mup32 = sb("mup32", [64, 32]); mun32 = sb("mun32", [64, 32])
        pswap = sb("pswap", [64, 64])
        fidx = sb("fidx", [64, 1]); freq = sb("freq", [64, 1])
        rmask64 = sb("rmask64", [128, 512]); rmask32 = sb("rmask32", [128, 256])
        pv = sb("pv", [128, NL, NPV])
        omka = sb("omka", [128, NL, 8])
        gq = sb("gq", [128, NL, 2])
        cTb = sb("cTb", [128, DC, 3], BF16)
        modT = sb("modT", [128, 64, 3])
        s1T = sb("s1T", [128, DC, 3])
        MblkA = sb("MblkA", [128, 8, 128]); MblkB = sb("MblkB", [128, 8, 128])
        Mblk = [MblkA, MblkA, MblkB]
        MKEY = ["Mblk0", "Mblk0", "Mblk2"]
        rwprev = [sb(f"rwprev{s}", [128, 25]) for s in range(3)]
        convh = [sb(f"convh{s}", [128, 8, 2]) for s in range(3)]
        ARENA_W = 23744
        arena = sb("arena", [128, ARENA_W])

        class Carver:
            def __init__(self, tag, prev):
                self.off = 0
                self.tag = tag
                self.keys = []
                self._prev = prev

            def get(self, name, free_shape, dt=F32):
                n = int(np.prod(free_shape))
                words = (n + 1) // 2 if dt == BF16 else n
                words = (words + 15) // 16 * 16
                assert self.off + words <= ARENA_W, f"arena overflow {self.tag}:{name} {self.off + words}"
                v = arena[:, self.off:self.off + words]
                self.off += words
                if dt == BF16:
                    v = v.bitcast(BF16)
                elif dt == I32:
                    v = v.bitcast(I32)
                v = v[:, 0:n]
                if len(free_shape) == 2:
                    v = v.rearrange("p (a b) -> p a b", a=free_shape[0])
                elif len(free_shape) == 3:
                    v = v.rearrange("p (a b c) -> p a b c", a=free_shape[0], b=free_shape[1])
                k = f"{self.tag}:{name}"
                self.keys.append(k)
                return v, k

        prev_keys = [[]]
        phase_ctr = [0]

        def new_phase(tag):
            phase_ctr[0] += 1
            return Carver(f"{tag}_{phase_ctr[0]}", list(prev_keys[0]))

        def seal(c):
            P.rekey(c._prev, c.keys)
            prev_keys[0] = list(c.keys)

        dma("sp", ident[:], c_ident, (), ["ident"])
        dma("sp", bones[:], c_bones, (), ["bones"])
        dma("sp", msu[:], c_msu, (), ["msu"]); dma("sp", msl[:], c_msl, (), ["msl"]); dma("sp", mup[:], c_mu, (), ["mup"])
        dma("sp", msu32[:], c_msu32, (), ["msu32"]); dma("sp", msl32[:], c_msl32, (), ["msl32"]); dma("sp", mup32[:], c_mu32, (), ["mup32"])
        dma("sp", pswap[:], c_pswap, (), ["pswap"])
        dma("sp", fidx[:], c_fidx, (), ["fidx"])
        dma("sp", pv[:], pv_in.rearrange("l p n -> p l n"), (), ["pv"])
        dma("pool", cTb[:], cT_in, (), ["cTb"])
        memset("pool", onesb[:], 1.0, ["onesb"])
        if len(phases) < 5:
            memset("pool", mixT[:], 0.0, ["mixT"])
        memset("pool", onesf[:], 1.0, ["onesf"])
        memset("pool", rmask64[:], 1.0, ["rmask64"])
        memset("pool", rmask32[:], 1.0, ["rmask32"])
        _op("pool", lambda e: e.memset(rmask64[:].rearrange("p (a c) -> p a c", c=64)[:, :, 0:1], 0.0), ["rmask64"], ["rmask64"])
        _op("pool", lambda e: e.memset(rmask32[:].rearrange("p (a c) -> p a c", c=32)[:, :, 0:1], 0.0), ["rmask32"], ["rmask32"])
        ts("dve", mun[:], mup[:], -1.0, None, ALU.mult, None, ["mup"], ["mun"])
        ts("dve", mun32[:], mup32[:], -1.0, None, ALU.mult, None, ["mup32"], ["mun32"])
        act(freq[:], fidx[:], AF.Exp, ["fidx"], ["freq"], scale=-math.log(10000.0) / 32.0)
        for l in range(NL):
            ts("dve", omka[:, l, :], pv[:, l, PV["ka"]:PV["ka"] + 8], -1.0, 1.0, ALU.mult, ALU.add, ["pv"], ["omka"])
            ts("dve", gq[:, l, 0:1], pv[:, l, PV["qn_nope"]:PV["qn_nope"] + 1], QSCALE, None, ALU.mult, None, ["pv"], ["gq"])
            ts("dve", gq[:, l, 1:2], pv[:, l, PV["qn_rope"]:PV["qn_rope"] + 1], QSCALE, None, ALU.mult, None, ["pv"], ["gq"])

        def pvc(l, name, j=0, n=1, rows=128):
            c0 = PV[name] + j
            return pv[0:rows, l, c0:c0 + n]

        def pvB(l, name, j, n, C, rows=128, p0=0):
            c0 = PV[name] + j
            return pv[p0:p0 + rows, l, c0:c0 + n].unsqueeze(2).broadcast_to([rows, n, C])

        def load_w(src_ap, nch=DC, ncols=256):
            i = wq[0] % 2
            wq[0] += 1
            wb = wbuf[i]
            dma("pool", wb[:, 0:nch, 0:ncols], src_ap.rearrange("(c p) n -> p c n", p=128), (), [f"wbuf{i}"])
            return wb, f"wbuf{i}"

        class Tile:
            pass

        tiles = []
        for i in range(NPT):
            t = Tile()
            t.NT = NTP; t.C = 64; t.sample = False; t.idx = i
            t.segs = [(0, 0, NTP, i * NTP)]
            t.subs = [(j * 128, 128) for j in range(NTP // 128)]
            t.pos0 = i * NTP
            tiles.append(t)
        t = Tile()
        t.NT = 2 * SS; t.C = 32; t.sample = True; t.idx = NPT
        t.segs = [(1, 0, SS, PAST), (2, SS, SS, PAST)]
        t.subs = [(0, 2 * SS)]
        t.pos0 = SEQ
        tiles.append(t)

        def layer_setup(l):
            A = new_phase("ls")
            grow, growk = A.get("grow", [D])
            bgrow, bgrowk = A.get("bgrow", [D])
            seal(A)
            dma("sp", bgrow[0:3, :], b_gate[l].broadcast_to([3, D]), (), [bgrowk])
            for cb in range(48):
                wb, wk = load_w(w_ada[l][:, cb * 256:(cb + 1) * 256])
                if cb < 32:
                    for half in range(2):
                        cc = cb * 2 + half
                        b = nb()
                        for ch in range(DC):
                            mm(b.t[:, 0:3], wb[:, ch, half * 128:(half + 1) * 128], cTb[:, ch, :], ch == 0, ch == DC - 1,
                               [wk, "cTb"], [b.k])
                        bname = "b_shift" if cc < 32 else "b_scale"
                        ts("dve", modT[:, cc, :], b.t[:, 0:3], pvc(l, bname, cc % 32), None, ALU.add, None,
                           [b.k, "pv"], ["modT"])
                        b.rel()
                else:
                    g0 = (cb - 32) * 256
                    b = nb()
                    for ch in range(DC):
                        mm(b.t[0:3, 0:256], cTb[:, ch, :], wb[:, ch, :], ch == 0, ch == DC - 1, [wk, "cTb"], [b.k])
                    tt("dve", grow[0:3, g0:g0 + 256], b.t[0:3, 0:256], bgrow[0:3, g0:g0 + 256], ALU.add, [b.k, bgrowk], [growk])
                    b.rel()
            dma("sp", gate_d, grow[0:3, :], [growk], ["gate_d"])
            ts("dve", s1T[:], modT[:, 32:64, :], 1.0, None, ALU.add, None, ["modT"], ["s1T"])
            tt("dve", s1T[:], s1T[:], pvB(l, "norm_g", 0, 32, 3), ALU.mult, ["s1T", "pv"], ["s1T"])
            for s in range(3):
                if s == 0:
                    memset("pool", Mblk[0][:], 0.0, [MKEY[0]])
                    memset("pool", rwprev[0][:], 0.0, ["rwprev0"])
                    memset("pool", convh[0][:], 0.0, ["convh0"])
                else:
                    if s == 2:
                        load_state_M(l, s)
                    dma("sp", rwprev[s][:], st_shift[l, s - 1], (), [f"rwprev{s}"])
                    dma("sp", convh[s][:], st_conv[l, s - 1], (), [f"convh{s}"])

        def load_state_M(l, s):
            memset("pool", Mblk[s][:], 0.0, [MKEY[s]])
            dma("sp", Mblk[s][0:64, :, 0:64], st_M[l, s - 1, 0:64], [MKEY[s]], [MKEY[s]])
            dma("sp", Mblk[s][64:128, :, 64:128], st_M[l, s - 1, 64:128], [MKEY[s]], [MKEY[s]])

        def in_proj_fm(l, t, col0, ncols, consume):
            NT = t.NT
            j = 0
            c = col0
            while c < col0 + ncols:
                bw = min(256, col0 + ncols - c)
                wb, wk = load_w(w_in[l][:, c:c + bw], DC, bw)
                for h0 in range(0, bw, 128):
                    wd = min(128, bw - h0)
                    b = nb()
                    for ch in range(DC):
                        mm(b.t[0:wd, 0:NT], wb[:, ch, h0:h0 + wd], hT[:, ch, 0:NT], ch == 0, ch == DC - 1,
                           [wk, "hT"], [b.k])
                    consume(j, wd, b)
                    b.rel()
                    j += 1
                c += bw

        def phase1(l, t):
            A = new_phase("p1")
            xb = [A.get(f"x{i}", [D]) for i in range(2)]
            junk, junkk = A.get("junk", [D], BF16)
            ssq, ssqk = A.get("ssq", [4])
            rstd, rstdk = A.get("rstd", [4])
            seal(A)
            src = (x1_s if l > 0 else xs_in) if t.sample else (x1_p if l > 0 else xp)
            srck = ["x1_s" if t.sample else "x1_p"] if l > 0 else []
            for si, (r0, rows) in enumerate(t.subs):
                xt, xk = xb[si % 2]
                g0 = r0 if t.sample else t.idx * NTP + r0
                dma("sp", xt[0:rows, :], src[g0:g0 + rows, :], srck, [xk])
                act(junk[0:rows, :], xt[0:rows, :], AF.Square, [xk], [junkk, ssqk], accum_out=ssq[0:rows, si:si + 1])
                ts("dve", rstd[0:rows, si:si + 1], ssq[0:rows, si:si + 1], 1.0 / D, NORM_EPS, ALU.mult, ALU.add, [ssqk], [rstdk])
                act(rstd[0:rows, si:si + 1], rstd[0:rows, si:si + 1], AF.Sqrt, [rstdk], [rstdk])
                recip(rstd[0:rows, si:si + 1], rstd[0:rows, si:si + 1], [rstdk], [rstdk])
                ts("dve", xt[0:rows, :], xt[0:rows, :], rstd[0:rows, si:si + 1], None, ALU.mult, None, [xk, rstdk], [xk])
                for c4 in range(DC // 4):
                    b = nb()
                    for q in range(4):
                        ch = c4 * 4 + q
                        tr(b.t[:, q * 128:q * 128 + rows], xt[0:rows, ch * 128:(ch + 1) * 128], ident[0:rows, 0:rows],
                           [xk, "ident"], [b.k])
                    e = ev_eng()
                    for q in range(4):
                        ch = c4 * 4 + q
                        for (s, o, n, p0) in t.segs:
                            lo = max(o, r0); hi = min(o + n, r0 + rows)
                            if lo >= hi:
                                continue
                            src_ps = b.t[:, q * 128 + lo - r0:q * 128 + hi - r0]
                            if e == "act":
                                act(hT[:, ch, lo:hi], src_ps, AF.Identity, [b.k, "s1T", "modT"], ["hT"],
                                    scale=s1T[:, ch, s:s + 1], bias=modT[:, ch, s:s + 1])
                            else:
                                ts("dve", hT[:, ch, lo:hi], src_ps, s1T[:, ch, s:s + 1], modT[:, ch, s:s + 1], ALU.mult, ALU.add,
                                   [b.k, "s1T", "modT"], ["hT"])
                    b.rel()

        def phase2(l, t):
            NT = t.NT
            A = new_phase("p2")
            nseg = len(t.segs)
            Ls = t.segs[0][2]
            csb = [A.get(f"csb{i}", [NT]) for i in range(2)]
            ub = [A.get(f"ub{i}", [nseg, Ls + 2]) for i in range(2)]
            yb = [A.get(f"yb{i}", [NT]) for i in range(2)]
            sg = [A.get(f"sg{i}", [NT]) for i in range(2)]
            seal(A)
            cw = PV["conv_w"]
            for cc2 in range(4):
                def cons_c(j, wd, b):
                    cp("act", csb[j][0][:, :], b.t[:, 0:NT], [b.k], [csb[j][1]])
                in_proj_fm(l, t, OFF["cv_c"] + cc2 * 256, 256, cons_c)

                def cons_x(j, wd, b):
                    cc = cc2 * 2 + j
                    u, uk = ub[j]
                    y, yk = yb[j]
                    for si, (s, o, n, p0) in enumerate(t.segs):
                        cp("pool", u[:, si, 0:2], convh[s][:, cc, :], [f"convh{s}"], [uk])
                        tt("dve", u[:, si, 2:2 + n], csb[j][0][:, o:o + n], b.t[:, o:o + n], ALU.mult, [b.k, csb[j][1], uk], [uk])
                        cp("pool", convh[s][:, cc, :], u[:, si, n:n + 2], [uk], [f"convh{s}"])
                        ts("dve", y[:, o:o + n], u[:, si, 2:2 + n], pv[:, l, cw + 16 + cc:cw + 17 + cc], pvc(l, "conv_b", cc), ALU.mult, ALU.add,
                           [uk, "pv"], [yk])
                        stt(y[:, o:o + n], u[:, si, 1:1 + n], pv[:, l, cw + 8 + cc:cw + 9 + cc], y[:, o:o + n], ALU.mult, ALU.add, [uk, yk, "pv"], [yk])
                        stt(y[:, o:o + n], u[:, si, 0:n], pv[:, l, cw + cc:cw + 1 + cc], y[:, o:o + n], ALU.mult, ALU.add, [uk, yk, "pv"], [yk])
                in_proj_fm(l, t, OFF["cv_x"] + cc2 * 256, 256, cons_x)

                def cons_b(j, wd, b):
                    y, yk = yb[j]
                    tt("dve", y[:, 0:NT], y[:, 0:NT], b.t[:, 0:NT], ALU.mult, [b.k, yk], [yk])
                in_proj_fm(l, t, OFF["cv_b"] + cc2 * 256, 256, cons_b)

                def cons_g(j, wd, b):
                    cc = cc2 * 2 + j
                    y, yk = yb[j]
                    act(sg[j][0][:, :], b.t[:, 0:NT], AF.Silu, [b.k], [sg[j][1]])
                    tt("pool", mixT[:, 24 + cc, 0:NT], y[:, 0:NT], sg[j][0][:, :], ALU.mult, [yk, sg[j][1]], ["mixT"])
                in_proj_fm(l, t, OFF["cv_gate"] + cc2 * 256, 256, cons_g)

        def phase3(l, t):
            NT, C = t.NT, t.C
            W = 2 * C
            nseg = len(t.segs)
            Ls = t.segs[0][2]
            nchk = Ls // C
            nlev = int(round(math.log2(C))) - 1
            MSU, MSL, MUP, MUN = (msu, msl, mup, mun) if C == 64 else (msu32, msl32, mup32, mun32)
            mkeys = ["msu", "msl", "mup", "mun"] if C == 64 else ["msu32", "msl32", "mup32", "mun32"]
            rmask, rmk = (rmask64, "rmask64") if C == 64 else (rmask32, "rmask32")
            A = new_phase("p3")
            pre, prek = A.get("pre", [25, nseg, Ls + 2], BF16)
            w2a2, w2k = A.get("w2a2", [1024])
            x24, x24k = A.get("x24", [C])
            sig, sigk = A.get("sig", [8, C])
            a_, ak = A.get("a", [8, C])
            cs, csk = A.get("cs", [8, C])
            G, Gk = A.get("G", [8, C])
            iG, iGk = A.get("iG", [8, C])
            Gp, Gpk = A.get("Gp", [8, C])
            xr, xrk = A.get("xr", [4, C]); xk_, xkk = A.get("xk", [4, C]); xv, xvk = A.get("xv", [4, C])
            kkn, kknk = A.get("kkn", [4, C]); kmod, kmodk = A.get("kmod", [4, C]); rt, rtk = A.get("rt", [4, C])
            t1, t1k = A.get("t1", [4, C]); t2, t2k = A.get("t2", [4, C]); bon, bonk = A.get("bon", [4, C])
            blk = {}
            for nm in ("kap", "bt", "kt", "kh", "bh", "vb"):
                blk[nm] = A.get("blk_" + nm, [4, 2, C])
            prod = {}
            for nm in ("Vb", "Kt", "Bt", "X0", "A0", "A2t", "X1", "A1", "Q"):
                prod[nm] = A.get("pr_" + nm, [4, 128])
            A2R, A2Rk = A.get("A2R", [4, C]); ARn, ARnk = A.get("ARn", [4, C])
            Oq, Oqk = A.get("Oq", [512]); cen, cenk = A.get("cen", [512]); sqv, sqvk = Oq, Oqk
            st8, st8k = A.get("st8", [16])
            sgt, sgtk = cen, cenk
            seal(A)
            import os
            P3 = int(os.environ.get("P3STOP", "99"))
            if P3 <= -3:
                return
            dma("sp", w2a2[0:64, :], w_w2[l], (), [w2k])
            dma("sp", w2a2[64:128, :], w_a2[l], (), [w2k])
            if P3 <= -2:
                return
            for nm in blk:
                memset("pool", blk[nm][0][:], 0.0, [blk[nm][1]])
            if P3 <= -1:
                return

            def cons_pre(j, wd, b):
                for si, (s, o, n, p0) in enumerate(t.segs):
                    CPF = int(os.environ.get("CPF", "7"))
                    if CPF & 1:
                        cp("pool", pre[:, j, si, 1:2], rwprev[s][:, j:j + 1], [f"rwprev{s}"], [prek])
                    if CPF & 2:
                        cp(ev_eng(), pre[:, j, si, 2:2 + n], b.t[:, o:o + n], [b.k], [prek])
                    if CPF & 4:
                        cp("dve", rwprev[s][:, j:j + 1], b.t[:, o + n - 1:o + n], [b.k], [f"rwprev{s}"])
            in_proj_fm(l, t, OFF["rw_pre"], 3200, cons_pre)
            if P3 <= 0:
                return

            def lerp(eng, out, j0, nj, si, c, outk):
                cur = pre[:, j0:j0 + nj, si, 2 + c * C:2 + (c + 1) * C]
                prv = pre[:, j0:j0 + nj, si, 1 + c * C:1 + (c + 1) * C]
                tt(eng, out, prv, cur, ALU.subtract, [prek], [outk])
                tt(eng, out, out, pvB(l, "mu", j0, nj, C), ALU.mult, [outk, "pv"], [outk])
                tt(eng, out, out, cur, ALU.add, [outk, prek], [outk])

            def v3(ap2d, rows, a, b_):
                return ap2d[0:rows, 0:a * b_].rearrange("p (a b) -> p a b", a=a)

            for si, (s, o, n, p0) in enumerate(t.segs):
                Mk = MKEY[s]
                M = Mblk[s]
                for c in range(nchk):
                    tok0 = o + c * C
                    lerp("pool", x24.unsqueeze(1), 24, 1, si, c, x24k)
                    act(x24[0:64, :], x24[0:64, :], AF.Tanh, [x24k], [x24k])
                    bw = nb(); ba = nb()
                    for j in range(8):
                        mm(bw.t[:, j * C:(j + 1) * C], w2a2[0:64, j * 128:(j + 1) * 128], x24[0:64, :], True, True, [w2k, x24k], [bw.k])
                    for j in range(8):
                        mm(ba.t[:, j * C:(j + 1) * C], w2a2[64:128, j * 128:(j + 1) * 128], x24[64:128, :], True, True, [w2k, x24k], [ba.k])
                    tt("dve", sig[:], bw.v3(128, 8, C), pvB(l, "w0", 0, 8, C), ALU.add, [bw.k, "pv"], [sigk])
                    bw.rel()
                    act(sig[:], sig[:], AF.Sigmoid, [sigk], [sigk])
                    tt("dve", a_[:], ba.v3(128, 8, C), pvB(l, "a0", 0, 8, C), ALU.add, [ba.k, "pv"], [ak])
                    ba.rel()
                    act(a_[:], a_[:], AF.Sigmoid, [ak], [ak])
                    sflat = sig[:].rearrange("p a b -> p (a b)")
                    cflat = cs[:].rearrange("p a b -> p (a b)")
                    _op("dve", lambda e, cflat=cflat, sflat=sflat: e.tensor_tensor_scan(out=cflat, data0=rmask[:, 0:8 * C], data1=sflat,
                                                                                       initial=0.0, op0=ALU.mult, op1=ALU.add),
                        [rmk, sigk], [csk])
                    act(G[:], cs[:], AF.Exp, [csk], [Gk], scale=-C0)
                    act(iG[:], cs[:], AF.Exp, [csk], [iGk], scale=C0)
                    tt("pool", Gp[:], cs[:], sig[:], ALU.subtract, [csk, sigk], [Gpk])
                    act(Gp[:], Gp[:], AF.Exp, [Gpk], [Gpk], scale=-C0)
                    if P3 <= 1:
                        continue
                    for q in range(2):
                        j0 = 4 * q
                        Sq = slice(j0, j0 + 4)
                        lerp("pool", xr[:], j0, 4, si, c, xrk)
                        lerp("pool", xk_[:], 8 + j0, 4, si, c, xkk)
                        lerp("pool", xv[:], 16 + j0, 4, si, c, xvk)
                        tt("dve", t1[:], xk_[:], pvB(l, "kk", j0, 4, C), ALU.mult, [xkk, "pv"], [t1k])
                        act(t2[:], t1[:], AF.Square, [t1k], [t2k])
                        bs = nb()
                        for jj in range(4):
                            mm(bs.t[:, jj * C:(jj + 1) * C], bones[:, :], t2[:, jj, :], True, True, ["bones", t2k], [bs.k])
                        ts("dve", t2[:], bs.v3(128, 4, C), 1e-12, None, ALU.add, None, [bs.k], [t2k])
                        bs.rel()
                        act(t2[:], t2[:], AF.Sqrt, [t2k], [t2k])
                        recip(t2[:], t2[:], [t2k], [t2k])
                        tt("dve", kkn[:], t1[:], t2[:], ALU.mult, [t1k, t2k], [kknk])
                        tt("dve", t1[:], a_[:, Sq, :], pvB(l, "ka", j0, 4, C), ALU.mult, [ak, "pv"], [t1k])
                        tt("dve", t1[:], t1[:], omka[:, l, j0:j0 + 4].unsqueeze(2).broadcast_to([128, 4, C]), ALU.add, [t1k, "omka"], [t1k])
                        tt("dve", kmod[:], xk_[:], t1[:], ALU.mult, [xkk, t1k], [kmodk])
                        tt("pool", t1[:], xr[:], kmod[:], ALU.mult, [xrk, kmodk], [t1k])
                        tt("pool", t1[:], t1[:], pvB(l, "rk", j0, 4, C), ALU.mult, [t1k, "pv"], [t1k])
                        br = nb()
                        for jj in range(4):
                            mm(br.t[:, jj * C:(jj + 1) * C], bones[:, :], t1[:, jj, :], True, True, ["bones", t1k], [br.k])
                        tt("dve", bon[:], br.v3(128, 4, C), xv[:], ALU.mult, [br.k, xvk], [bonk])
                        br.rel()
                        tt("pool", bon[:], bon[:], pvB(l, "ln_b", j0, 4, C), ALU.add, [bonk, "pv"], [bonk])
                        if P3 <= 2:
                            continue
                        tt("pool", rt[:], xr[:], G[:, Sq, :], ALU.mult, [xrk, Gk], [rtk])
                        tt("pool", t1[:], kkn[:], a_[:, Sq, :], ALU.mult, [kknk, ak], [t1k])
                        for hp in range(2):
                            ps_ = slice(64 * hp, 64 * hp + 64)
                            GC = G[ps_, Sq, C - 1:C].broadcast_to([64, 4, C])
                            e1 = "dve" if hp == 0 else "pool"
                            tt(e1, blk["kap"][0][ps_, :, hp, :], kkn[ps_], Gp[ps_, Sq, :], ALU.mult, [kknk, Gpk], [blk["kap"][1]])
                            tt(e1, blk["bt"][0][ps_, :, hp, :], t1[ps_], iG[ps_, Sq, :], ALU.mult, [t1k, iGk], [blk["bt"][1]])
                            tt(e1, blk["kt"][0][ps_, :, hp, :], kmod[ps_], iG[ps_, Sq, :], ALU.mult, [kmodk, iGk], [blk["kt"][1]])
                            tt(e1, blk["kh"][0][ps_, :, hp, :], blk["kt"][0][ps_, :, hp, :], GC, ALU.mult, [blk["kt"][1], Gk], [blk["kh"][1]])
                            tt(e1, blk["bh"][0][ps_, :, hp, :], blk["bt"][0][ps_, :, hp, :], GC, ALU.mult, [blk["bt"][1], Gk], [blk["bh"][1]])
                            cp(e1, blk["vb"][0][ps_, :, hp, :], xv[ps_], [xvk], [blk["vb"][1]])

                        def B(nm, jj):
                            return blk[nm][0][:, jj, :, :].rearrange("p a b -> p (a b)")

                        def PR(nm, jj, rows=W):
                            return prod[nm][0][0:rows, jj, 0:128]

                        def PRW(nm, jj):
                            return prod[nm][0][0:W, jj, 0:W]

                        if P3 <= 3:
                            continue
                        for nm_in, nm_out, neg in (("vb", "Vb", False), ("kh", "Kt", False), ("bh", "Bt", True)):
                            b = nb()
                            for jj in range(4):
                                tr(b.t[0:W, jj * 128:(jj + 1) * 128], B(nm_in, jj), ident[:, :], [blk[nm_in][1], "ident"], [b.k])
                            dst = prod[nm_out][0][0:W, :, :]
                            if neg:
                                ts("dve", dst, b.v3(W, 4, 128), -1.0, None, ALU.mult, None, [b.k], [prod[nm_out][1]])
                            else:
                                cp("act", dst, b.v3(W, 4, 128), [b.k], [prod[nm_out][1]])
                            b.rel()
                        for (lh, rh, outnm, mask, mk) in (("bt", "kap", "X0", MSU, mkeys[0]), ("kap", "bt", "A0", MSL, mkeys[1]),
                                                          ("kt", "kap", "A2t", MSU, mkeys[0])):
                            b = nb()
                            for jj in range(4):
                                mm(b.t[0:W, jj * W:(jj + 1) * W], B(lh, jj), B(rh, jj), True, True, [blk[lh][1], blk[rh][1]], [b.k])
                            tt("dve", prod[outnm][0][0:W, :, 0:W], b.v3(W, 4, W), mask[0:W, 0:W].unsqueeze(1).broadcast_to([W, 4, W]), ALU.mult,
                               [b.k, mk], [prod[outnm][1]])
                            b.rel()
                        for (lh, out_ap, outk, mask, mk) in (("kt", A2R, A2Rk, MUP, mkeys[2]), ("bt", ARn, ARnk, MUN, mkeys[3])):
                            b = nb()
                            for jj in range(4):
                                mm(b.t[0:W, jj * C:(jj + 1) * C], B(lh, jj), rt[:, jj, :], True, True, [blk[lh][1], rtk], [b.k])
                            tt("dve", out_ap[0:W, :, :], b.v3(W, 4, C), mask[0:W, 0:C].unsqueeze(1).broadcast_to([W, 4, C]), ALU.mult,
                               [b.k, mk], [outk])
                            b.rel()
                        if P3 <= 4:
                            continue
                        tt("pool", prod["Q"][0][0:W, :, 0:W], ident[0:W, 0:W].unsqueeze(1).broadcast_to([W, 4, W]), prod["X0"][0][0:W, :, 0:W],
                           ALU.subtract, ["ident", prod["X0"][1]], [prod["Q"][1]])
                        cur = ("X0", "A0"); nxt = ("X1", "A1")
                        for lev in range(nlev):
                            last = (lev == nlev - 1)
                            if not last:
                                b1 = nb()
                                for jj in range(4):
                                    mm(b1.t[0:W, jj * W:(jj + 1) * W], PRW(cur[1], jj), PRW(cur[0], jj), True, True,
                                       [prod[cur[0]][1], prod[cur[1]][1]], [b1.k])
                            b2 = nb()
                            for jj in range(4):
                                mm(b2.t[0:W, jj * W:(jj + 1) * W], PRW(cur[0], jj), PRW(cur[1], jj), True, True,
                                   [prod[cur[0]][1], prod[cur[1]][1]], [b2.k])
                            if not last:
                                cp("act", prod[nxt[0]][0][0:W, :, 0:W], b1.v3(W, 4, W), [b1.k], [prod[nxt[0]][1]])
                                b1.rel()
                            cp("dve", prod[nxt[1]][0][0:W, :, 0:W], b2.v3(W, 4, W), [b2.k], [prod[nxt[1]][1]])
                            b2.rel()
                            b3 = nb()
                            for jj in range(4):
                                mm(b3.t[0:W, jj * W:(jj + 1) * W], PRW(nxt[1], jj), PRW("Q", jj), True, True,
                                   [prod[nxt[1]][1], prod["Q"][1]], [b3.k])
                            tt("dve", prod["Q"][0][0:W, :, 0:W], b3.v3(W, 4, W), prod["Q"][0][0:W, :, 0:W], ALU.add,
                               [b3.k, prod["Q"][1]], [prod["Q"][1]])
                            b3.rel()
                            cur, nxt = nxt, cur
                        if P3 <= 5:
                            continue
                        RHSn, Un = cur[0], nxt[0]
                        b = nb()
                        for jj in range(4):
                            mm(b.t[0:W, jj * 128:(jj + 1) * 128], B("kap", jj), M[:, j0 + jj, :], True, False, [blk["kap"][1], Mk], [b.k])
                            mm(b.t[0:W, jj * 128:(jj + 1) * 128], PRW("A2t", jj), PR("Vb", jj), False, True, [prod["A2t"][1], prod["Vb"][1]], [b.k])
                        cp("act", prod[RHSn][0][0:W, :, :], b.v3(W, 4, 128), [b.k], [prod[RHSn][1]])
                        b.rel()
                        b = nb()
                        for jj in range(4):
                            mm(b.t[0:W, jj * 128:(jj + 1) * 128], PRW("Q", jj), PR(RHSn, jj), True, True, [prod["Q"][1], prod[RHSn][1]], [b.k])
                        cp("dve", prod[Un][0][0:W, :, :], b.v3(W, 4, 128), [b.k], [prod[Un][1]])
                        b.rel()
                        bO = nb()
                        for jj in range(4):
                            mm(bO.t[0:C, jj * 128:(jj + 1) * 128], rt[:, jj, :], M[:, j0 + jj, :], True, False, [rtk, Mk], [bO.k])
                            mm(bO.t[0:C, jj * 128:(jj + 1) * 128], A2R[0:W, jj, :], PR("Vb", jj), False, False, [A2Rk, prod["Vb"][1]], [bO.k])
                            mm(bO.t[0:C, jj * 128:(jj + 1) * 128], ARn[0:W, jj, :], PR(Un, jj), False, True, [ARnk, prod[Un][1]], [bO.k])
                        cp("act", Oq[0:C, :], bO.t[0:C, 0:512], [bO.k], [Oqk])
                        bO.rel()
                        bM = nb()
                        for jj in range(4):
                            mm(bM.t[:, jj * 128:(jj + 1) * 128], PR("Kt", jj), PR("Vb", jj), True, False, [prod["Kt"][1], prod["Vb"][1]], [bM.k])
                            mm(bM.t[:, jj * 128:(jj + 1) * 128], PR("Bt", jj), PR(Un, jj), False, True, [prod["Bt"][1], prod[Un][1]], [bM.k])
                        tt("pool", M[:, Sq, :], M[:, Sq, :], G[:, Sq, C - 1:C].broadcast_to([128, 4, 128]), ALU.mult, [Mk, Gk], [Mk])
                        tt("dve", M[:, Sq, :], M[:, Sq, :], bM.v3(128, 4, 128), ALU.add, [Mk, bM.k], [Mk])
                        bM.rel()
                        if P3 <= 6:
                            continue
                        O3 = Oq[0:C, :].rearrange("p (a b) -> p a b", a=8)
                        c3 = cen[0:C, :].rearrange("p (a b) -> p a b", a=8)
                        s3 = sqv[0:C, :].rearrange("p (a b) -> p a b", a=8)
                        red(st8[0:C, 0:8], O3, [Oqk], [st8k])
                        ts("dve", st8[0:C, 0:8], st8[0:C, 0:8], 1.0 / 64, None, ALU.mult, None, [st8k], [st8k])
                        tt("dve", c3, O3, st8[0:C, 0:8].unsqueeze(2).broadcast_to([C, 8, 64]), ALU.subtract, [Oqk, st8k], [cenk])
                        act(sqv[0:C, :], cen[0:C, :], AF.Square, [cenk], [sqvk])
                        red(st8[0:C, 8:16], s3, [sqvk], [st8k])
                        ts("dve", st8[0:C, 8:16], st8[0:C, 8:16], 1.0 / 64, GN_EPS, ALU.mult, ALU.add, [st8k], [st8k])
                        act(st8[0:C, 8:16], st8[0:C, 8:16], AF.Sqrt, [st8k], [st8k])
                        recip(st8[0:C, 8:16], st8[0:C, 8:16], [st8k], [st8k])
                        tt("dve", c3, c3, st8[0:C, 8:16].unsqueeze(2).broadcast_to([C, 8, 64]), ALU.mult, [cenk, st8k], [cenk])
                        bT = nb()
                        for jj in range(4):
                            tr(bT.t[:, jj * C:(jj + 1) * C], cen[0:C, jj * 128:(jj + 1) * 128], ident[0:C, 0:C], [cenk, "ident"], [bT.k])
                        tt("dve", t2[:], bT.v3(128, 4, C), pvB(l, "ln_g", j0, 4, C), ALU.mult, [bT.k, "pv"], [t2k])
                        bT.rel()
                        tt("pool", mixT[:, j0:j0 + 4, tok0:tok0 + C], t2[:], bon[:], ALU.add, [t2k, bonk], ["mixT"])

            def cons_gate(j, wd, b):
                act(sgt[:, 0:NT], b.t[:, 0:NT], AF.Silu, [b.k], [sgtk])
                tt("dve", mixT[:, j, 0:NT], mixT[:, j, 0:NT], sgt[:, 0:NT], ALU.mult, ["mixT", sgtk], ["mixT"])
            in_proj_fm(l, t, OFF["rw_gate"], 1024, cons_gate)

        def phase4(l, t):
            NT = t.NT
            A = new_phase("p4")
            cqb, cqbk = A.get("cqb", [8, NT], BF16)
            cosT, cosk = A.get("cos", [NT]); sinT, sink = A.get("sin", [NT])
            sq = [A.get(f"sq{i}", [NTP]) for i in range(2)]
            rs = [A.get(f"rs{i}", [NTP]) for i in range(2)]
            xrq, xrqk = A.get("xrq", [NT])
            mark = A.off
            ncommon = len(A.keys)
            latT, latTk = A.get("latT", [4, NTP])
            latb, latbk = A.get("latb", [4, NTP], BF16)
            krb, krbk = A.get("krb", [NTP], BF16)
            wuk, wukk = A.get("wuk", [4, 2048], BF16)
            wuvb = [A.get(f"wuv{i}", [4, 512], BF16) for i in range(2)]
            Vp = [A.get(f"Vp{i}", [512], BF16) for i in range(2)]
            knb = [A.get(f"knb{i}", [NTP], BF16) for i in range(2)]
            lato = [A.get(f"lato{i}", [512]) for i in range(2)]
            kro, krok = A.get("kro", [64])
            ctm, ctmk = A.get("ctm", [4, 512])
            ktm, ktmk = A.get("ktm", [4, 64])
            yy, yyk = A.get("yy", [NT]); kf, kfk = A.get("kf", [NT]); ki, kik = A.get("ki", [NT], I32)
            seal(A)
            sqi = [0]

            def next_sq():
                sqi[0] += 1
                return sq[sqi[0] % 2], rs[sqi[0] % 2]

            def rms_stats(b, rows, n, div, eps):
                (s_, sk), (r_, rk) = next_sq()
                act(s_[0:rows, 0:n], b.t[0:rows, 0:n], AF.Square, [b.k], [sk])
                b2 = nb()
                mm(b2.t[0:rows, 0:n], onesf[0:rows, 0:rows], s_[0:rows, 0:n], True, True, ["onesf", sk], [b2.k])
                ts("dve", r_[0:rows, 0:n], b2.t[0:rows, 0:n], 1.0 / div, eps, ALU.mult, ALU.add, [b2.k], [rk])
                b2.rel()
                act(r_[0:rows, 0:n], r_[0:rows, 0:n], AF.Sqrt, [rk], [rk])
                recip(r_[0:rows, 0:n], r_[0:rows, 0:n], [rk], [rk])
                return r_, rk

            pos, posk = yy, yyk
            dma("sp", pos[0:64, :], c_pos[:, t.pos0:t.pos0 + NT], (), [posk])
            ts("dve", yy[0:64, :], pos[0:64, :], freq[:, 0:1], 1.0 / (2 * math.pi), ALU.mult, ALU.mult, [posk, "freq"], [yyk])
            cp("dve", ki[0:64, :], yy[0:64, :], [yyk], [kik])
            cp("dve", kf[0:64, :], ki[0:64, :], [kik], [kfk])
            tt("dve", yy[0:64, :], yy[0:64, :], kf[0:64, :], ALU.subtract, [yyk, kfk], [yyk])
            act(sinT[0:64, :], yy[0:64, :], AF.Sin, [yyk], [sink], scale=math.pi)
            act(kf[0:64, :], yy[0:64, :], AF.Sin, [yyk], [kfk], scale=math.pi / 2)
            tt("dve", kf[0:64, :], kf[0:64, :], kf[0:64, :], ALU.mult, [kfk], [kfk])
            ts("dve", kf[0:64, :], kf[0:64, :], -2.0, 1.0, ALU.mult, ALU.add, [kfk], [kfk])
            tt("dve", cosT[0:64, :], sinT[0:64, :], sinT[0:64, :], ALU.mult, [sink], [cosk])
            ts("dve", cosT[0:64, :], cosT[0:64, :], -2.0, 1.0, ALU.mult, ALU.add, [cosk], [cosk])
            tt("dve", sinT[0:64, :], sinT[0:64, :], kf[0:64, :], ALU.mult, [sink, kfk], [sink])
            ts("dve", sinT[0:32, :], sinT[0:32, :], -2.0, None, ALU.mult, None, [sink], [sink])
            ts("dve", sinT[32:64, :], sinT[32:64, :], 2.0, None, ALU.mult, None, [sink], [sink])

            def rotary(out_ap, outk, x_ap, xk, n):
                b = nb()
                mm(b.t[0:64, 0:n], pswap[:, :], x_ap, True, True, ["pswap", xk], [b.k])
                (s_, sk), _ = next_sq()
                tt("dve", s_[0:64, 0:n], b.t[0:64, 0:n], sinT[0:64, 0:n], ALU.mult, [b.k, sink], [sk])
                b.rel()
                tt("dve", x_ap, x_ap, cosT[0:64, 0:n], ALU.mult, [xk, cosk], [xk])
                tt("dve", out_ap, x_ap, s_[0:64, 0:n], ALU.add, [xk, sk], [outk])

            def cons_cq(j, wd, b):
                cp("dve", cqb[:, j, :], b.t[:, 0:NT], [b.k], [cqbk])
                (s_, sk), _ = next_sq()
                act(s_[:, 0:NT], b.t[:, 0:NT], AF.Square, [b.k], [sk])
                mm(ACC0.t[:, 0:NT], onesf[:, :], s_[:, 0:NT], j == 0, j == 7, ["onesf", sk], [ACC0.k])
            in_proj_fm(l, t, OFF["cq"], 1024, cons_cq)
            _, (r_, rk) = next_sq()
            ts("dve", r_[:, 0:NT], ACC0.t[:, 0:NT], 1.0 / 1024, NORM_EPS, ALU.mult, ALU.add, [ACC0.k], [rk])
            act(r_[:, 0:NT], r_[:, 0:NT], AF.Sqrt, [rk], [rk])
            recip(r_[:, 0:NT], r_[:, 0:NT], [rk], [rk])
            for j in range(8):
                stt(cqb[:, j, :], cqb[:, j, :], pvc(l, "qng", j), r_[:, 0:NT], ALU.mult, ALU.mult, [cqbk, "pv", rk], [cqbk])

            def cons_ckv(j, wd, b):
                cp("dve", latT[:, j, 0:NT], b.t[:, 0:NT], [b.k], [latTk])
                (s_, sk), _ = next_sq()
                act(s_[:, 0:NT], b.t[:, 0:NT], AF.Square, [b.k], [sk])
                mm(ACC1.t[:, 0:NT], onesf[:, :], s_[:, 0:NT], j == 0, j == 3, ["onesf", sk], [ACC1.k])
            in_proj_fm(l, t, OFF["ckv"], 512, cons_ckv)
            _, (r_, rk) = next_sq()
            ts("dve", r_[:, 0:NT], ACC1.t[:, 0:NT], 1.0 / 512, NORM_EPS, ALU.mult, ALU.add, [ACC1.k], [rk])
            act(r_[:, 0:NT], r_[:, 0:NT], AF.Sqrt, [rk], [rk])
            recip(r_[:, 0:NT], r_[:, 0:NT], [rk], [rk])
            for j in range(4):
                stt(latT[:, j, 0:NT], latT[:, j, 0:NT], pvc(l, "kvg", j), r_[:, 0:NT], ALU.mult, ALU.mult, [latTk, "pv", rk], [latTk])
                cp("pool", latb[:, j, 0:NT], latT[:, j, 0:NT], [latTk], [latbk])
            lat_dst = lat_s if t.sample else lat_p
            kr_dst = kr_s if t.sample else kr_p
            for si, (r0, rows) in enumerate(t.subs):
                g0 = r0 if t.sample else t.idx * NTP + r0
                b = nb()
                for j in range(4):
                    tr(b.t[0:rows, j * 128:(j + 1) * 128], latT[:, j, r0:r0 + rows], ident[:, :], [latTk, "ident"], [b.k])
                lo_, lok = lato[si % 2]
                cp("act", lo_[0:rows, :], b.t[0:rows, 0:512], [b.k], [lok])
                b.rel()
                dma("sp", lat_dst[l, g0:g0 + rows, :], lo_[0:rows, :], [lok], [])

            def cons_kr(j, wd, b):
                r_, rk = rms_stats(b, 64, NT, 64, NORM_EPS)
                stt(xrq[0:64, :], b.t[0:64, 0:NT], pvc(l, "kn_rope", 0, 1, 64), r_[0:64, 0:NT], ALU.mult, ALU.mult, [b.k, "pv", rk], [xrqk])
            in_proj_fm(l, t, OFF["kr"], 64, cons_kr)
            rotary(xrq[0:64, :], xrqk, xrq[0:64, :], xrqk, NT)
            cp("pool", krb[0:64, 0:NT], xrq[0:64, :], [xrqk], [krbk])
            for si, (r0, rows) in enumerate(t.subs):
                g0 = r0 if t.sample else t.idx * NTP + r0
                b = nb()
                tr(b.t[0:rows, 0:64], xrq[0:64, r0:r0 + rows], ident[0:64, 0:64], [xrqk, "ident"], [b.k])
                cp("act", kro[0:rows, :], b.t[0:rows, 0:64], [b.k], [krok])
                b.rel()
                dma("sp", kr_dst[l, g0:g0 + rows, :], kro[0:rows, :], [krok], [])

            dma("pool", wuk[:], w_uk[l].rearrange("(c p) n -> p c n", p=128), (), [wukk])

            def emit_kv(s, lat_ap, n, key0):
                kd = f"kT_{l}_{s}"
                vd = f"V_{l}_{s}"
                for h in range(H):
                    b = nb()
                    for ch in range(4):
                        mm(b.t[:, 0:n], wuk[:, ch, h * 128:(h + 1) * 128], lat_ap[:, ch, :], ch == 0, ch == 3, [wukk, latbk], [b.k])
                    r_, rk = rms_stats(b, 128, n, 128, NORM_EPS)
                    kb_, kbk = knb[h % 2]
                    stt(kb_[:, 0:n], b.t[:, 0:n], pvc(l, "kn_nope"), r_[:, 0:n], ALU.mult, ALU.mult, [b.k, "pv", rk], [kbk])
                    b.rel()
                    dma("sp", kT_d[l][s][h, :, key0:key0 + n], kb_[:, 0:n], [kbk], [kd])
                for vbk in range(4):
                    wv, wvk = wuvb[vbk % 2]
                    dma("pool", wv[:], w_uv[l][:, vbk * 512:(vbk + 1) * 512].rearrange("(c p) n -> p c n", p=128), (), [wvk])
                    for tb in range((n + 127) // 128):
                        m = min(128, n - tb * 128)
                        b = nb()
                        for ch in range(4):
                            mm(b.t[0:m, 0:512], lat_ap[:, ch, tb * 128:tb * 128 + m], wv[:, ch, :], ch == 0, ch == 3, [latbk, wvk], [b.k])
                        vp, vpk = Vp[(vbk * 4 + tb) % 2]
                        cp(ev_eng(), vp[0:m, :], b.t[0:m, 0:512], [b.k], [vpk])
                        b.rel()
                        dma("sp", V_d[l][s][key0 + tb * 128:key0 + tb * 128 + m, vbk * 512:(vbk + 1) * 512], vp[0:m, :], [vpk], [vd])

            if t.sample:
                for (s, o, n, p0) in t.segs:
                    for blk_ in range(PAST // 512):
                        for sub in range(4):
                            r0 = blk_ * 512 + sub * 128
                            dma("sp", ctm[:, sub, :], cache_lat[l, s - 1, r0:r0 + 128, :], (), [ctmk])
                            dma("sp", ktm[:, sub, :], cache_kr[l, s - 1, r0:r0 + 128, :], (), [ktmk])
                        for sub in range(4):
                            b = nb()
                            for j in range(4):
                                tr(b.t[:, j * 128:(j + 1) * 128], ctm[:, sub, j * 128:(j + 1) * 128], ident[:, :], [ctmk, "ident"], [b.k])
                            for j in range(4):
                                cp(ev_eng(), latb[:, j, sub * 128:(sub + 1) * 128], b.t[:, j * 128:(j + 1) * 128], [b.k], [latbk])
                            b.rel()
                        b = nb()
                        for sub in range(4):
                            tr(b.t[0:64, sub * 128:(sub + 1) * 128], ktm[:, sub, :], ident[:, :], [ktmk, "ident"], [b.k])
                        cp("act", krb[0:64, 0:512], b.t[0:64, 0:512], [b.k], [krbk])
                        b.rel()
                        dma("sp", krT_d[l][s][:, blk_ * 512:(blk_ + 1) * 512], krb[0:64, 0:512], [krbk], [f"krT_{l}_{s}"])
                        emit_kv(s, latb[:, :, 0:512], 512, blk_ * 512)
                for j in range(4):
                    cp("pool", latb[:, j, 0:NT], latT[:, j, 0:NT], [latTk], [latbk])
                cp("pool", krb[0:64, 0:NT], xrq[0:64, :], [xrqk], [krbk])
            for (s, o, n, p0) in t.segs:
                dma("sp", krT_d[l][s][:, p0:p0 + n], krb[0:64, o:o + n], [krbk], [f"krT_{l}_{s}"])
                emit_kv(s, latb[:, :, o:o + n], n, p0)

            old_keys = list(A.keys)
            A.off = mark
            nk_max = max(p0 + n for (s, o, n, p0) in t.segs)
            nkb_max = (nk_max + 127) // 128
            nseg = len(t.segs)
            k0 = len(A.keys)
            knA = [A.get(f"kn{i}", [nk_max], BF16) for i in range(2)]
            VA = [A.get(f"V{i}", [nkb_max, 128], BF16) for i in range(2)]
            krA, krAk = A.get("krA", [nseg, nk_max], BF16)
            wuqh = [A.get(f"wuq{i}", [8, 192], BF16) for i in range(2)]
            qn, qnk = A.get("qn", [NT], BF16)
            qr, qrk = A.get("qr", [NT], BF16)
            PT = [A.get(f"PT{i}", [NTP], BF16) for i in range(4)]
            rec, reck = A.get("rec", [NT]); tq, tqk = A.get("tq", [NT])
            sgm, sgmk = A.get("sgm", [2, NT], BF16)
            xq2, xq2k = A.get("xq2", [NT])
            if not t.sample:
                matt, mattk = A.get("matt", [4, NTP], BF16)
            newk = A.keys[k0:]
            P.rekey(old_keys[ncommon:], newk)
            prev_keys[0] = list(A.keys[:ncommon]) + newk
            if not t.sample:
                dma("pool", matt[:], c_matt, (), [mattk])
            for si, (s, o, n, p0) in enumerate(t.segs):
                dma("sp", krA[0:64, si, 0:p0 + n], krT_d[l][s][:, 0:p0 + n], [f"krT_{l}_{s}"], [krAk])
            pti = [0]
            for h in range(H):
                if h % 2 == 0:
                    def cons_mg(j, wd, b):
                        act(sgm[:, j, :], b.t[:, 0:NT], AF.Silu, [b.k], [sgmk])
                    in_proj_fm(l, t, OFF["mla_gate"] + h * 128, 256, cons_mg)
                wq_, wqk = wuqh[h % 2]
                dma("pool", wq_[:], w_uq[l][:, h * 192:(h + 1) * 192].rearrange("(c p) n -> p c n", p=128), (), [wqk])
                b = nb()
                for ch in range(8):
                    mm(b.t[:, 0:NT], wq_[:, ch, 0:128], cqb[:, ch, :], ch == 0, ch == 7, [wqk, cqbk], [b.k])
                r_, rk = rms_stats(b, 128, NT, 128, NORM_EPS)
                stt(qn[:, :], b.t[:, 0:NT], gq[:, l, 0:1], r_[:, 0:NT], ALU.mult, ALU.mult, [b.k, "gq", rk], [qnk])
                b.rel()
                b = nb()
                for ch in range(8):
                    mm(b.t[0:64, 0:NT], wq_[:, ch, 128:192], cqb[:, ch, :], ch == 0, ch == 7, [wqk, cqbk], [b.k])
                r_, rk = rms_stats(b, 64, NT, 64, NORM_EPS)
                stt(xq2[0:64, :], b.t[0:64, 0:NT], gq[0:64, l, 1:2], r_[0:64, 0:NT], ALU.mult, ALU.mult, [b.k, "gq", rk], [xq2k])
                b.rel()
                rotary(qr[0:64, :], qrk, xq2[0:64, :], xq2k, NT)
                for si, (s, o, n, p0) in enumerate(t.segs):
                    nk = p0 + n
                    nkb = (nk + 127) // 128
                    nfull = nk // 128
                    kn, knk = knA[(h * nseg + si) % 2]
                    V, Vk = VA[(h * nseg + si) % 2]
                    dma("sp", kn[:, 0:nk], kT_d[l][s][h, :, 0:nk], [f"kT_{l}_{s}"], [knk])
                    if nfull > 0:
                        dma("sp", V[:, 0:nfull, :], V_d[l][s][0:nfull * 128, h * 128:(h + 1) * 128].rearrange("(b p) d -> p b d", p=128),
                            [f"V_{l}_{s}"], [Vk])
                    if nkb > nfull:
                        m_ = nk - nfull * 128
                        dma("sp", V[0:m_, nfull, :], V_d[l][s][nfull * 128:nk, h * 128:(h + 1) * 128], [f"V_{l}_{s}"], [Vk])
                    for kb in range(nkb):
                        m = min(128, nk - kb * 128)
                        diag = (not t.sample) and kb >= t.idx * 4
                        jd = kb - t.idx * 4 if diag else 0
                        qlo = jd * 128 if diag else 0
                        nq = n - qlo
                        b = nb()
                        mm(b.t[0:m, 0:nq], kn[:, kb * 128:kb * 128 + m], qn[:, o + qlo:o + n], True, False, [knk, qnk], [b.k])
                        mm(b.t[0:m, 0:nq], krA[0:64, si, kb * 128:kb * 128 + m], qr[0:64, o + qlo:o + n], False, True, [krAk, qrk], [b.k])
                        pt, ptk = PT[pti[0] % 4]
                        pti[0] += 1
                        act(pt[0:m, 0:nq], b.t[0:m, 0:nq], AF.Exp, [b.k], [ptk])
                        b.rel()
                        if diag:
                            tt("pool", pt[0:m, 0:nq], pt[0:m, 0:nq], matt[0:m, jd, qlo:NTP], ALU.mult, [ptk, mattk], [ptk])
                        mm(ACC0.t[:, qlo:n], V[0:m, kb, :], pt[0:m, 0:nq], kb == 0, kb == nkb - 1, [Vk, ptk], [ACC0.k])
                        mm(ACC1.t[:, qlo:n], onesb[0:m, :], pt[0:m, 0:nq], kb == 0, kb == nkb - 1, ["onesb", ptk], [ACC1.k])
                    recip(rec[:, 0:n], ACC1.t[:, 0:n], [ACC1.k], [reck])
                    tt("dve", tq[:, 0:n], ACC0.t[:, 0:n], rec[:, 0:n], ALU.mult, [ACC0.k, reck], [tqk])
                    tt("pool", mixT[:, 8 + h, o:o + n], tq[:, 0:n], sgm[:, h % 2, o:o + n], ALU.mult, [tqk, sgmk], ["mixT"])

        def phase5(l, t):
            NT = t.NT
            nsub = len(t.subs)
            rows = t.subs[0][1]
            A = new_phase("p5")
            gbc, gbck = A.get("gbc", [D])
            xq = [A.get(f"xq{i}", [nsub, 256]) for i in range(2)]
            oq = [A.get(f"oq{i}", [nsub, 256]) for i in range(2)]
            seal(A)
            last = (l == NL - 1)
            src = (x1_s if l > 0 else xs_in) if t.sample else (x1_p if l > 0 else xp)
            dst = (y_s if last else x1_s) if t.sample else (y_p if last else x1_p)
            srck = ["x1_s" if t.sample else "x1_p"] if l > 0 else []
            dstk = [] if last else ["x1_s" if t.sample else "x1_p"]
            if t.sample:
                for (s, o, n, p0) in t.segs:
                    dma("sp", gbc[o:o + n, :], gate_d[s:s + 1, :].broadcast_to([n, D]), ["gate_d"], [gbck])
            else:
                dma("sp", gbc[:, :], gate_d[0:1, :].broadcast_to([128, D]), ["gate_d"], [gbck])
            g0 = 0 if t.sample else t.idx * NTP
            for cb in range(16):
                c0 = cb * 256
                wb, wk = load_w(w_out[l][:, c0:c0 + 256])
                xt, xk = xq[cb % 2]
                ot, ok = oq[cb % 2]
                dma("sp", xt[0:rows, :, :], src[g0:g0 + NT, c0:c0 + 256].rearrange("(s p) c -> p s c", p=rows), srck, [xk])
                for si, (r0, rws) in enumerate(t.subs):
                    b = nb()
                    for ch in range(DC):
                        mm(b.t[0:rows, 0:256], mixT[:, ch, r0:r0 + rows], wb[:, ch, :], ch == 0, ch == DC - 1, [wk, "mixT"], [b.k])
                    tt("dve", ot[0:rows, si, :], b.t[0:rows, 0:256], gbc[0:rows, c0:c0 + 256], ALU.mult, [b.k, gbck], [ok])
                    b.rel()
                tt("pool", ot[0:rows, :, :], ot[0:rows, :, :], xt[0:rows, :, :], ALU.add, [ok, xk], [ok])
                dma("sp", dst[g0:g0 + NT, c0:c0 + 256].rearrange("(s p) c -> p s c", p=rows), ot[0:rows, :, :], [ok], dstk)

        def finish_seq(l, s):
            A = new_phase("lf")
            Ts, Tsk = A.get("Ts", [8, 128])
            shs, shsk = A.get("shs", [128])
            cvs, cvsk = A.get("cvs", [2, 128])
            seal(A)
            if True:
                b = nb()
                tr(b.t[0:25, 0:128], rwprev[s][:, 0:25], ident[:, :], [f"rwprev{s}", "ident"], [b.k])
                cp("act", shs[0:25, :], b.t[0:25, 0:128], [b.k], [shsk])
                b.rel()
                dsh = sh_p[l] if s == 0 else sh_s[l, s - 1]
                dma("sp", dsh.rearrange("(c p) -> c p", p=128), shs[0:25, :], [shsk], [])
                b = nb()
                for j in range(2):
                    tr(b.t[0:8, j * 128:(j + 1) * 128], convh[s][:, :, j], ident[:, :], [f"convh{s}", "ident"], [b.k])
                cp("act", cvs[0:8, :, :], b.v3(8, 2, 128), [b.k], [cvsk])
                b.rel()
                dcv = cv_p[l] if s == 0 else cv_s[l, s - 1]
                dma("sp", dcv.rearrange("j (c p) -> c j p", p=128), cvs[0:8, :, :], [cvsk], [])
                for q in range(2):
                    b = nb()
                    for jj in range(4):
                        tr(b.t[:, jj * 128:(jj + 1) * 128], Mblk[s][:, q * 4 + jj, :], ident[:, :], [MKEY[s], "ident"], [b.k])
                    cp("act", Ts[:, q * 4:q * 4 + 4, :], b.v3(128, 4, 128), [b.k], [Tsk])
                    b.rel()
                drw = rw_p[l] if s == 0 else rw_s[l, s - 1]
                dv = drw.rearrange("(j h) v k -> h v j k", h=2)
                dma("sp", dv[0], Ts[0:64, :, 0:64], [Tsk], [])
                dma("sp", dv[1], Ts[64:128, :, 64:128], [Tsk], [])

        for l in range(NL):
            layer_setup(l)
            for t in tiles:
                if t.sample:
                    finish_seq(l, 0)
                    load_state_M(l, 1)
                phase1(l, t)
                if 2 in phases:
                    phase2(l, t)
                if 3 in phases:
                    phase3(l, t)
                if 4 in phases:
                    phase4(l, t)
                phase5(l, t)
            finish_seq(l, 1)
            finish_seq(l, 2)
        nops = {e: len(P.ops[e]) for e in ENGS}
        print("ops per engine:", nops, flush=True)
        P.emit()
    return nc


def _cols(v):
    v = np.asarray(v, np.float32).reshape(-1)
    return np.ascontiguousarray(v.reshape(-1, 128).T)


def _consts(SEQ):
    c = {}
    c["c_ident"] = np.eye(128, dtype=np.float32)
    k = np.arange(128)[:, None]
    q = np.arange(NTP)[None, :]
    matt = np.zeros((128, 4, NTP), np.float32)
    for j in range(4):
        matt[:, j, :] = ((j * 128 + k) // 64 <= q // 64)
    c["c_matt"] = matt

    def blockdiag(m):
        n = m.shape[0]
        z = np.zeros((2 * n, 2 * n), np.float32)
        z[:n, :n] = m
        z[n:, n:] = m
        return z
    for C, suf in ((64, ""), (32, "32")):
        s = np.arange(C)[:, None]
        t = np.arange(C)[None, :]
        su = (s < t).astype(np.float32)
        sl = (s > t).astype(np.float32)
        mu = (s <= t).astype(np.float32)
        c["c_msu" + suf] = blockdiag(su)
        c["c_msl" + suf] = blockdiag(sl)
        c["c_mu" + suf] = np.concatenate([mu, mu], 0)
    bo = np.zeros((128, 128), np.float32)
    bo[:64, :64] = 1
    bo[64:, 64:] = 1
    c["c_bones"] = bo
    ps = np.zeros((64, 64), np.float32)
    for i in range(64):
        ps[(i + 32) % 64, i] = 1
    c["c_pswap"] = ps
    pos = np.concatenate([np.arange(SEQ), PAST + np.arange(SS), PAST + np.arange(SS)]).astype(np.float32)
    c["c_pos"] = np.ascontiguousarray(np.broadcast_to(pos[None, :], (64, SEQ + 2 * SS)))
    c["c_fidx"] = (np.arange(64) % 32).astype(np.float32)[:, None]
    return c


def _pack_pv(inp, NL):
    pv = np.zeros((NL, 128, NPV), np.float32)
    for l in range(NL):
        def put(name, arr, rows=128):
            a = np.asarray(arr, np.float32)
            pv[l, :a.shape[0], PV[name]:PV[name] + a.shape[1]] = a
        put("norm_g", _cols(inp["norm_g"][l]))
        put("b_shift", _cols(inp["b_ada"][l][0:D]))
        put("b_scale", _cols(inp["b_ada"][l][D:2 * D]))
        put("mu", _cols(inp["rw_mu"][l]))
        put("w0", _cols(inp["rw_w0"][l]))
        put("a0", _cols(inp["rw_a0"][l]))
        put("kk", _cols(inp["rw_kk"][l]))
        put("ka", _cols(inp["rw_ka"][l]))
        put("rk", _cols(inp["rw_rk"][l].reshape(-1)))
        put("qng", _cols(inp["mla_qnorm_g"][l]))
        put("kvg", _cols(inp["mla_kvnorm_g"][l]))
        put("qn_nope", _cols(inp["mla_qn_nope"][l]))
        put("qn_rope", np.asarray(inp["mla_qn_rope"][l], np.float32)[:, None])
        put("kn_nope", _cols(inp["mla_kn_nope"][l]))
        put("kn_rope", np.asarray(inp["mla_kn_rope"][l], np.float32)[:, None])
        cw = np.concatenate([_cols(inp["conv_w"][l][j]) for j in range(3)], 1)
        put("conv_w", cw)
        put("conv_b", _cols(inp["conv_b"][l]))
        put("ln_g", _cols(inp["rw_ln_g"][l]))
        put("ln_b", _cols(inp["rw_ln_b"][l]))
    return pv


_NC_CACHE = {}
RUNNER = None


def run(inputs, SEQ=SEQ_FULL, NL=L_FULL, phases=(1, 2, 3, 4, 5), n_cores=8):
    inp = {k: np.asarray(v) for k, v in inputs.items()}
    key = (SEQ, NL, tuple(phases))
    if key not in _NC_CACHE:
        _NC_CACHE[key] = build(SEQ, NL, phases)
    nc = _NC_CACHE[key]
    consts = _consts(SEQ)
    pv = _pack_pv(inp, NL)
    shared = dict(consts)
    shared["pv"] = pv
    f32 = lambda a: np.ascontiguousarray(a, dtype=np.float32)
    shared["w_ada"] = f32(inp["w_ada"][:NL])
    shared["b_gate"] = f32(inp["b_ada"][:NL, None, 2 * D:3 * D])
    shared["w_in"] = f32(inp["w_in"][:NL])
    shared["w_out"] = f32(inp["w_out"][:NL])
    shared["w_uq"] = f32(inp["mla_w_uq"][:NL])
    shared["w_uk"] = f32(inp["mla_w_uk"][:NL])
    shared["w_uv"] = f32(inp["mla_w_uv"][:NL])
    shared["rw_w2"] = f32(inp["rw_w2"][:NL])
    shared["rw_a2"] = f32(inp["rw_a2"][:NL])
    in_maps = []
    for i in range(n_cores):
        b = i % 4
        sbs = [2 * i, 2 * i + 1]
        m = dict(shared)
        m["xp"] = f32(inp["x_prompt"][b, :SEQ])
        m["xs"] = f32(inp["x_sample"][sbs].reshape(2 * SS, D))
        cs = np.stack([inp["c_prompt"][b], inp["c_sample"][sbs[0]], inp["c_sample"][sbs[1]]], 0)
        m["cT"] = f32(cs.reshape(3, DC, 128).transpose(2, 1, 0))
        m["cache_lat"] = f32(inp["cache_mla_latent"][:NL, sbs])
        m["cache_kr"] = f32(inp["cache_mla_krope"][:NL, sbs])
        S = inp["state_rwkv"][:NL, sbs]
        Mst = S.reshape(NL, 2, 8, 2, 64, 64).transpose(0, 1, 3, 5, 2, 4)
        m["st_M"] = f32(Mst.reshape(NL, 2, 128, 8, 64))
        sh = inp["state_rwkv_shift"][:NL, sbs]
        m["st_shift"] = f32(sh.reshape(NL, 2, 25, 128).transpose(0, 1, 3, 2))
        cv = inp["state_conv"][:NL, sbs]
        m["st_conv"] = f32(cv.reshape(NL, 2, 2, 8, 128).transpose(0, 1, 4, 3, 2))
        in_maps.append(m)
    if RUNNER is not None:
        return RUNNER(nc, in_maps)
    res = run_bass_kernel_spmd(nc, in_maps, core_ids=list(range(n_cores)))
    return res.results


def kernel(**inputs):
    r = run(inputs)
    NL = L_FULL
    B = 4
    f = lambda a: np.asarray(a, dtype=np.float32)
    y_p = np.stack([f(r[b]["y_p"]) for b in range(B)], 0)
    y_s = np.concatenate([f(r[i]["y_s"]).reshape(2, SS, D) for i in range(8)], 0)
    lat_p = np.stack([f(r[b]["lat_p"]) for b in range(B)], 1)
    kr_p = np.stack([f(r[b]["kr_p"]) for b in range(B)], 1)
    rw_p = np.stack([f(r[b]["rw_p"]) for b in range(B)], 1)
    sh_p = np.stack([f(r[b]["sh_p"]) for b in range(B)], 1)
    cv_p = np.stack([f(r[b]["cv_p"]) for b in range(B)], 1)
    lat_s = np.concatenate([f(r[i]["lat_s"]).reshape(NL, 2, SS, 512) for i in range(8)], 1)
    kr_s = np.concatenate([f(r[i]["kr_s"]).reshape(NL, 2, SS, 64) for i in range(8)], 1)
    rw_s = np.concatenate([f(r[i]["rw_s"]) for i in range(8)], 1)
    sh_s = np.concatenate([f(r[i]["sh_s"]) for i in range(8)], 1)
    cv_s = np.concatenate([f(r[i]["cv_s"]) for i in range(8)], 1)
    return (y_p, y_s, lat_p, kr_p, rw_p, sh_p, cv_p, lat_s, kr_s, rw_s, sh_s, cv_s)
```

```python
import contextlib
import math
import numpy as np
import concourse.bass as bass
import concourse.mybir as mybir
from concourse.bass_utils import run_bass_kernel_spmd

F32 = mybir.dt.float32
BF16 = mybir.dt.bfloat16
I32 = mybir.dt.int32
AF = mybir.ActivationFunctionType
ALU = mybir.AluOpType
AX = mybir.AxisListType

D = 4096
DC = 32
H = 16
L_FULL = 2
SEQ_FULL = 4096
PAST = 2048
SS = 32
NTP = 512
IN_COLS = 11968
OFF = dict(rw_pre=0, rw_gate=3200, cq=4224, ckv=5248, kr=5760, mla_gate=5824,
           cv_b=7872, cv_c=8896, cv_x=9920, cv_gate=10944)
QSCALE = 192.0 ** -0.5
C0 = math.exp(-0.5)
NORM_EPS = 1e-6
GN_EPS = 64e-5

PV = {}
_c = 0
for _n, _w in (("norm_g", 32), ("b_shift", 32), ("b_scale", 32), ("mu", 25), ("w0", 8), ("a0", 8),
               ("kk", 8), ("ka", 8), ("rk", 8), ("qng", 8), ("kvg", 4), ("qn_nope", 1), ("qn_rope", 1),
               ("kn_nope", 1), ("kn_rope", 1), ("conv_w", 24), ("conv_b", 8), ("ln_g", 8), ("ln_b", 8)):
    PV[_n] = _c
    _c += _w
NPV = _c

ENGS = ("pe", "act", "dve", "pool", "sp")
EPOCH = 20000
NDMA = 8
SAME_ENGINE_SYNC = True


class Op:
    __slots__ = ("eng", "fn", "deps", "is_dma", "sig", "dsem", "dval", "prevd")

    def __init__(self, eng, fn, is_dma):
        self.eng = eng
        self.fn = fn
        self.is_dma = is_dma
        self.deps = []
        self.sig = None
        self.dsem = None
        self.dval = 0
        self.prevd = None


class Prog:
    def __init__(self, nc):
        self.nc = nc
        self.ops = {e: [] for e in ENGS}
        self.lastw = {}
        self.readers = {}
        self.nsig = {e: 0 for e in ENGS}
        self.ndma = {e: 0 for e in ENGS}
        self.dma_last = {}

    def rekey(self, old_keys, new_keys):
        ops = []
        for k in old_keys:
            w = self.lastw.get(k)
            if w is not None:
                ops.append(w)
            ops.extend(self.readers.get(k, ()))
        for k in new_keys:
            self.lastw.pop(k, None)
            self.readers[k] = list(ops)

    def op(self, eng, fn, reads=(), writes=(), dma=False):
        o = Op(eng, fn, dma)
        psr = [k for k in reads if k.startswith("ps") and k not in writes]
        if psr:
            writes = list(writes) + psr
        deps = []
        for k in reads:
            w = self.lastw.get(k)
            if w is not None:
                deps.append(w)
        for k in writes:
            w = self.lastw.get(k)
            if w is not None:
                deps.append(w)
            deps.extend(self.readers.get(k, ()))
        seen = set()
        for d in deps:
            if id(d) in seen or d is o:
                continue
            seen.add(id(d))
            if d.is_dma:
                o.deps.append(d)
            elif d.eng == eng:
                if eng == "pe" or not SAME_ENGINE_SYNC:
                    continue
                o.deps.append(d)
                if d.sig is None:
                    d.sig = -1
            else:
                o.deps.append(d)
                if d.sig is None:
                    d.sig = -1
        for k in reads:
            self.readers.setdefault(k, []).append(o)
        for k in writes:
            self.lastw[k] = o
            self.readers[k] = []
        if dma:
            j = self.ndma[eng]
            self.ndma[eng] = j + 1
            slot = j % NDMA
            o.dsem = (eng, slot)
            o.dval = 16 * (j // NDMA + 1)
            o.prevd = self.dma_last.get((eng, slot))
            self.dma_last[(eng, slot)] = o
        self.ops[eng].append(o)
        return o

    def emit(self):
        nc = self.nc
        for e in ENGS:
            n = 0
            for o in self.ops[e]:
                if o.sig is not None and not o.is_dma:
                    o.sig = n
                    n += 1
            self.nsig[e] = n
        with contextlib.ExitStack() as st:
            esems = {}
            for e in ENGS:
                ne = (self.nsig[e] + EPOCH - 1) // EPOCH
                esems[e] = [st.enter_context(nc.semaphore(f"s_{e}_{i}")) for i in range(ne)]
            dsems = {}
            for e in ENGS:
                for s in range(min(NDMA, self.ndma[e])):
                    dsems[(e, s)] = st.enter_context(nc.semaphore(f"d_{e}_{s}"))
            block = st.enter_context(nc.Block())
            final_dma = dict(self.dma_last)

            def make(e):
                def body(eng):
                    seen_sig = {}
                    seen_dma = {}
                    for o in self.ops[e]:
                        waits = []
                        for d in o.deps:
                            if d.is_dma:
                                if seen_dma.get(d.dsem, 0) < d.dval:
                                    seen_dma[d.dsem] = d.dval
                                    waits.append((dsems[d.dsem], d.dval))
                            else:
                                if seen_sig.get(d.eng, -1) < d.sig:
                                    seen_sig[d.eng] = d.sig
                                    waits.append((esems[d.eng][d.sig // EPOCH], d.sig % EPOCH + 1))
                        if o.is_dma and o.prevd is not None:
                            d = o.prevd
                            if seen_dma.get(d.dsem, 0) < d.dval:
                                seen_dma[d.dsem] = d.dval
                                waits.append((dsems[d.dsem], d.dval))
                        for (s, v) in waits:
                            eng.wait_ge(s, v)
                        ins = o.fn(eng)
                        if o.is_dma:
                            ins.then_inc(dsems[o.dsem], 16)
                        elif o.sig is not None:
                            ins.then_inc(esems[e][o.sig // EPOCH], 1)
                    if e == "sp":
                        for k, d in final_dma.items():
                            if seen_dma.get(d.dsem, 0) < d.dval:
                                eng.wait_ge(dsems[d.dsem], d.dval)
                return body

            block.tensor(make("pe"))
            block.scalar(make("act"))
            block.vector(make("dve"))
            block.gpsimd(make("pool"))
            block.sync(make("sp"))


class Ctx:
    pass


def build(SEQ=SEQ_FULL, NL=L_FULL, phases=(1, 2, 3, 4, 5)):
    nc = bass.Bass("TRN2", target_bir_lowering=False)
    NPT = SEQ // NTP
    LKS = PAST + SS

    def din(name, shape, dt=F32):
        return nc.dram_tensor(name, list(shape), dt, kind="ExternalInput").ap()

    def dout(name, shape, dt=F32):
        return nc.dram_tensor(name, list(shape), dt, kind="ExternalOutput").ap()

    def dscr(name, shape, dt=F32):
        return nc.dram_tensor(name, list(shape), dt).ap()

    xp = din("xp", [SEQ, D])
    xs_in = din("xs", [2 * SS, D])
    cT_in = din("cT", [128, DC, 3])
    cache_lat = din("cache_lat", [NL, 2, PAST, 512])
    cache_kr = din("cache_kr", [NL, 2, PAST, 64])
    st_M = din("st_M", [NL, 2, 128, 8, 64])
    st_shift = din("st_shift", [NL, 2, 128, 25])
    st_conv = din("st_conv", [NL, 2, 128, 8, 2])
    w_ada = din("w_ada", [NL, D, 3 * D])
    b_gate = din("b_gate", [NL, 1, D])
    w_in = din("w_in", [NL, D, IN_COLS])
    w_out = din("w_out", [NL, D, D])
    w_uq = din("w_uq", [NL, 1024, 3072])
    w_uk = din("w_uk", [NL, 512, 2048])
    w_uv = din("w_uv", [NL, 512, 2048])
    w_w2 = din("rw_w2", [NL, 64, 1024])
    w_a2 = din("rw_a2", [NL, 64, 1024])
    pv_in = din("pv", [NL, 128, NPV])
    c_ident = din("c_ident", [128, 128])
    c_matt = din("c_matt", [128, 4, NTP])
    c_msu = din("c_msu", [128, 128])
    c_msl = din("c_msl", [128, 128])
    c_mu = din("c_mu", [128, 64])
    c_msu32 = din("c_msu32", [64, 64])
    c_msl32 = din("c_msl32", [64, 64])
    c_mu32 = din("c_mu32", [64, 32])
    c_bones = din("c_bones", [128, 128])
    c_pswap = din("c_pswap", [64, 64])
    c_pos = din("c_pos", [64, SEQ + 2 * SS])
    c_fidx = din("c_fidx", [64, 1])

    y_p = dout("y_p", [SEQ, D])
    y_s = dout("y_s", [2 * SS, D])
    lat_p = dout("lat_p", [NL, SEQ, 512])
    kr_p = dout("kr_p", [NL, SEQ, 64])
    rw_p = dout("rw_p", [NL, H, 64, 64])
    sh_p = dout("sh_p", [NL, 3200])
    cv_p = dout("cv_p", [NL, 2, 1024])
    lat_s = dout("lat_s", [NL, 2 * SS, 512])
    kr_s = dout("kr_s", [NL, 2 * SS, 64])
    rw_s = dout("rw_s", [NL, 2, H, 64, 64])
    sh_s = dout("sh_s", [NL, 2, 3200])
    cv_s = dout("cv_s", [NL, 2, 2, 1024])

    x1_p = dscr("x1_p", [SEQ, D])
    x1_s = dscr("x1_s", [2 * SS, D])
    gate_d = dscr("gate_d", [3, D])
    LK = [SEQ, LKS, LKS]
    kT_d = [[dscr(f"kT_{l}_{s}", [H, 128, LK[s]], BF16) for s in range(3)] for l in range(NL)]
    krT_d = [[dscr(f"krT_{l}_{s}", [64, LK[s]], BF16) for s in range(3)] for l in range(NL)]
    V_d = [[dscr(f"V_{l}_{s}", [LK[s], 2048], BF16) for s in range(3)] for l in range(NL)]

    def win_blocks():
        out = []
        for cc2 in range(4):
            for g in ("cv_c", "cv_x", "cv_b", "cv_gate"):
                out.append((OFF[g] + cc2 * 256, 256))
        for i in range(12):
            out.append((i * 256, 256))
        out.append((3072, 128))
        for i in range(4):
            out.append((OFF["cq"] + i * 256, 256))
        for i in range(2):
            out.append((OFF["ckv"] + i * 256, 256))
        out.append((OFF["kr"], 64))
        for h in range(0, H, 2):
            out.append((OFF["mla_gate"] + h * 128, 256))
        for i in range(4):
            out.append((OFF["rw_gate"] + i * 256, 256))
        return out
    WINB = win_blocks()
    WIN_IDX = {b: i for i, b in enumerate(WINB)}
    win_s = [dscr(f"win_s{l}", [len(WINB), 128, DC * 256], BF16) for l in range(NL)]
    wout_s = [dscr(f"wout_s{l}", [16, 128, DC * 256], BF16) for l in range(NL)]
    wuq_s = [dscr(f"wuq_s{l}", [H, 128, 8 * 192], BF16) for l in range(NL)]
    wuk_s = [dscr(f"wuk_s{l}", [128, 4 * 2048], BF16) for l in range(NL)]
    wuv_s = [dscr(f"wuv_s{l}", [4, 128, 4 * 512], BF16) for l in range(NL)]

    st = contextlib.ExitStack()
    with st:
        def sb(name, shape, dt=F32):
            return st.enter_context(nc.sbuf_tensor("sb_" + name, list(shape), dt))

        P = Prog(nc)

        def _op(eng, fn, r, w, dma=False):
            return P.op(eng, fn, reads=r, writes=w, dma=dma)

        def mm(out, lhsT, rhs, start, stop, r, w):
            _op("pe", lambda e: e.matmul(out, lhsT=lhsT, rhs=rhs, start=start, stop=stop), r, w)

        def tr(out, in_, ident_ap, r, w):
            _op("pe", lambda e: e.transpose(out, in_, ident_ap), r, w)

        def act(out, in_, func, r, w, bias=None, scale=None, accum_out=None):
            kw = {}
            if bias is not None:
                kw["bias"] = bias
            if scale is not None:
                kw["scale"] = scale
            if accum_out is not None:
                kw["accum_out"] = accum_out
            _op("act", lambda e: e.activation(out=out, in_=in_, func=func, **kw), r, w)

        def ts(eng, out, in0, s1, s2, op0, op1, r, w):
            if op1 is None:
                _op(eng, lambda e: e.tensor_scalar(out=out, in0=in0, scalar1=s1, scalar2=None, op0=op0), r, w)
            else:
                _op(eng, lambda e: e.tensor_scalar(out=out, in0=in0, scalar1=s1, scalar2=s2, op0=op0, op1=op1), r, w)

        def tt(eng, out, in0, in1, op, r, w):
            _op(eng, lambda e: e.tensor_tensor(out=out, in0=in0, in1=in1, op=op), r, w)

        def stt(out, in0, scalar, in1, op0, op1, r, w):
            _op("dve", lambda e: e.scalar_tensor_tensor(out=out, in0=in0, scalar=scalar, in1=in1, op0=op0, op1=op1), r, w)

        def cp(eng, out, in_, r, w):
            if eng == "act":
                _op("act", lambda e: e.activation(out=out, in_=in_, func=AF.Copy), r, w)
            else:
                _op(eng, lambda e: e.tensor_copy(out=out, in_=in_), r, w)

        def memset(eng, ap, val, w):
            _op(eng, lambda e: e.memset(ap, val), (), w)

        def dma(eng, out, in_, r, w):
            _op(eng, lambda e: e.dma_start(out=out, in_=in_), r, w, dma=True)

        def red(out, in_, r, w):
            _op("dve", lambda e: e.tensor_reduce(out=out, in_=in_, axis=AX.X, op=ALU.add), r, w)

        def recip(out, in_, r, w):
            _op("dve", lambda e: e.reciprocal(out=out, in_=in_), r, w)

        banks = [st.enter_context(nc.psum_tensor(f"ps{i}", [128, 512], F32)) for i in range(8)]
        NROT = 6
        rot = [0]
        busy = [False] * 8

        class Bank:
            def __init__(self, i):
                self.i = i
                self.t = banks[i]
                self.k = f"ps{i}"

            def rel(self):
                busy[self.i] = False

            def v3(self, rows, a, b):
                return self.t[0:rows, 0:a * b].rearrange("p (a b) -> p a b", a=a)

        def nb():
            i = rot[0] % NROT
            rot[0] += 1
            assert not busy[i], f"psum bank {i} still busy"
            busy[i] = True
            return Bank(i)

        ACC0 = Bank(6)
        ACC1 = Bank(7)

        evq = [0]

        def ev_eng():
            evq[0] += 1
            return "act" if evq[0] % 2 else "dve"

        hT = sb("hT", [128, DC, NTP], BF16)
        mixT = sb("mixT", [128, DC, NTP], BF16)
        wbuf = [sb(f"wbuf{i}", [128, DC, 256], BF16) for i in range(2)]
        wq = [0]
        ident = sb("ident", [128, 128])
        onesb = sb("onesb", [128, 128], BF16)
        onesf = sb("onesf", [128, 128])
        bones = sb("bones", [128, 128])
        msu = sb("msu", [128, 128]); msl = sb("msl", [128, 128]); mup = sb("mup", [128, 64]); mun = sb("mun", [128, 64])
        msu32 = sb("msu32", [64, 64]); msl32 = sb("msl32", [64, 64]); mup32 = sb("mup32", [64, 32]); mun32 = sb("mun32", [64, 32])
        pswap = sb("pswap", [64, 64])
        fidx = sb("fidx", [64, 1]); freq = sb("freq", [64, 1])
        rmask64 = sb("rmask64", [128, 512]); rmask32 = sb("rmask32", [128, 256])
        pv = sb("pv", [128, NL, NPV])
        omka = sb("omka", [128, NL, 8])
        gq = sb("gq", [128, NL, 2])
        cTb = sb("cTb", [128, DC, 3], BF16)
        modT = sb("modT", [128, 64, 3])
        s1T = sb("s1T", [128, DC, 3])
        MblkA = sb("MblkA", [128, 8, 128]); MblkB = sb("MblkB", [128, 8, 128])
        Mblk = [MblkA, MblkA, MblkB]
        MKEY = ["Mblk0", "Mblk0", "Mblk2"]
        rwprev = [sb(f"rwprev{s}", [128, 25]) for s in range(3)]
        convh = [sb(f"convh{s}", [128, 8, 2]) for s in range(3)]
        ARENA_W = 23744
        arena = sb("arena", [128, ARENA_W])

        class Carver:
            def __init__(self, tag, prev):
                self.off = 0
                self.tag = tag
                self.keys = []
                self._prev = prev

            def get(self, name, free_shape, dt=F32):
                n = int(np.prod(free_shape))
                words = (n + 1) // 2 if dt == BF16 else n
                words = (words + 15) // 16 * 16
                assert self.off + words <= ARENA_W, f"arena overflow {self.tag}:{name} {self.off + words}"
                v = arena[:, self.off:self.off + words]
                self.off += words
                if dt == BF16:
                    v = v.bitcast(BF16)
                elif dt == I32:
                    v = v.bitcast(I32)
                v = v[:, 0:n]
                if len(free_shape) == 2:
                    v = v.rearrange("p (a b) -> p a b", a=free_shape[0])
                elif len(free_shape) == 3:
                    v = v.rearrange("p (a b c) -> p a b c", a=free_shape[0], b=free_shape[1])
                k = f"{self.tag}:{name}"
                self.keys.append(k)
                return v, k

        prev_keys = [[]]
        phase_ctr = [0]

        def new_phase(tag):
            phase_ctr[0] += 1
            return Carver(f"{tag}_{phase_ctr[0]}", list(prev_keys[0]))

        def seal(c):
            P.rekey(c._prev, c.keys)
            prev_keys[0] = list(c.keys)

        dma("sp", ident[:], c_ident, (), ["ident"])
        dma("sp", bones[:], c_bones, (), ["bones"])
        dma("sp", msu[:], c_msu, (), ["msu"]); dma("sp", msl[:], c_msl, (), ["msl"]); dma("sp", mup[:], c_mu, (), ["mup"])
        dma("sp", msu32[:], c_msu32, (), ["msu32"]); dma("sp", msl32[:], c_msl32, (), ["msl32"]); dma("sp", mup32[:], c_mu32, (), ["mup32"])
        dma("sp", pswap[:], c_pswap, (), ["pswap"])
        dma("sp", fidx[:], c_fidx, (), ["fidx"])
        dma("sp", pv[:], pv_in.rearrange("l p n -> p l n"), (), ["pv"])
        dma("pool", cTb[:], cT_in, (), ["cTb"])
        memset("pool", onesb[:], 1.0, ["onesb"])
        if len(phases) < 5:
            memset("pool", mixT[:], 0.0, ["mixT"])
        memset("pool", onesf[:], 1.0, ["onesf"])
        memset("pool", rmask64[:], 1.0, ["rmask64"])
        memset("pool", rmask32[:], 1.0, ["rmask32"])
        _op("pool", lambda e: e.memset(rmask64[:].rearrange("p (a c) -> p a c", c=64)[:, :, 0:1], 0.0), ["rmask64"], ["rmask64"])
        _op("pool", lambda e: e.memset(rmask32[:].rearrange("p (a c) -> p a c", c=32)[:, :, 0:1], 0.0), ["rmask32"], ["rmask32"])
        ts("dve", mun[:], mup[:], -1.0, None, ALU.mult, None, ["mup"], ["mun"])
        ts("dve", mun32[:], mup32[:], -1.0, None, ALU.mult, None, ["mup32"], ["mun32"])
        act(freq[:], fidx[:], AF.Exp, ["fidx"], ["freq"], scale=-math.log(10000.0) / 32.0)
        for l in range(NL):
            ts("dve", omka[:, l, :], pv[:, l, PV["ka"]:PV["ka"] + 8], -1.0, 1.0, ALU.mult, ALU.add, ["pv"], ["omka"])
            ts("dve", gq[:, l, 0:1], pv[:, l, PV["qn_nope"]:PV["qn_nope"] + 1], QSCALE, None, ALU.mult, None, ["pv"], ["gq"])
            ts("dve", gq[:, l, 1:2], pv[:, l, PV["qn_rope"]:PV["qn_rope"] + 1], QSCALE, None, ALU.mult, None, ["pv"], ["gq"])

        def convert_weights(l):
            for i, (c, bw) in enumerate(WINB):
                dma("pool", win_s[l][i][:, 0:DC * bw].rearrange("p (c n) -> p c n", c=DC),
                    w_in[l][:, c:c + bw].rearrange("(c p) n -> p c n", p=128), (), [f"win_s{l}_{i}"])
            for i in range(16):
                dma("pool", wout_s[l][i].rearrange("p (c n) -> p c n", c=DC),
                    w_out[l][:, i * 256:(i + 1) * 256].rearrange("(c p) n -> p c n", p=128), (), [f"wout_s{l}_{i}"])
            for h in range(H):
                dma("pool", wuq_s[l][h].rearrange("p (c n) -> p c n", c=8),
                    w_uq[l][:, h * 192:(h + 1) * 192].rearrange("(c p) n -> p c n", p=128), (), [f"wuq_s{l}"])
            dma("pool", wuk_s[l].rearrange("p (c n) -> p c n", c=4), w_uk[l].rearrange("(c p) n -> p c n", p=128), (), [f"wuk_s{l}"])
            for i in range(4):
                dma("pool", wuv_s[l][i].rearrange("p (c n) -> p c n", c=4),
                    w_uv[l][:, i * 512:(i + 1) * 512].rearrange("(c p) n -> p c n", p=128), (), [f"wuv_s{l}"])

        def pvc(l, name, j=0, n=1, rows=128):
            c0 = PV[name] + j
            return pv[0:rows, l, c0:c0 + n]

        def pvB(l, name, j, n, C, rows=128, p0=0):
            c0 = PV[name] + j
            return pv[p0:p0 + rows, l, c0:c0 + n].unsqueeze(2).broadcast_to([rows, n, C])

        for l in range(NL):
            convert_weights(l)

        def load_w(src_ap, nch=DC, ncols=256):
            i = wq[0] % 2
            wq[0] += 1
            wb = wbuf[i]
            dma("pool", wb[:, 0:nch, 0:ncols], src_ap.rearrange("(c p) n -> p c n", p=128), (), [f"wbuf{i}"])
            return wb, f"wbuf{i}"

        def load_ws(scr_ap, scr_key, ncols=256):
            i = wq[0] % 2
            wq[0] += 1
            wb = wbuf[i]
            dma("sp", wb[:, 0:DC, 0:ncols], scr_ap[:, 0:DC * ncols].rearrange("p (c n) -> p c n", c=DC), [scr_key], [f"wbuf{i}"])
            return wb, f"wbuf{i}"

        class Tile:
            pass

        tiles = []
        for i in range(NPT):
            t = Tile()
            t.NT = NTP; t.C = 64; t.sample = False; t.idx = i
            t.segs = [(0, 0, NTP, i * NTP)]
            t.subs = [(j * 128, 128) for j in range(NTP // 128)]
            t.pos0 = i * NTP
            tiles.append(t)
        t = Tile()
        t.NT = 2 * SS; t.C = 32; t.sample = True; t.idx = NPT
        t.segs = [(1, 0, SS, PAST), (2, SS, SS, PAST)]
        t.subs = [(0, 2 * SS)]
        t.pos0 = SEQ
        tiles.append(t)

        def layer_setup(l):
            A = new_phase("ls")
            grow, growk = A.get("grow", [D])
            bgrow, bgrowk = A.get("bgrow", [D])
            seal(A)
            dma("sp", bgrow[0:3, :], b_gate[l].broadcast_to([3, D]), (), [bgrowk])
            for cb in range(48):
                wb, wk = load_w(w_ada[l][:, cb * 256:(cb + 1) * 256])
                if cb < 32:
                    for half in range(2):
                        cc = cb * 2 + half
                        b = nb()
                        for ch in range(DC):
                            mm(b.t[:, 0:3], wb[:, ch, half * 128:(half + 1) * 128], cTb[:, ch, :], ch == 0, ch == DC - 1,
                               [wk, "cTb"], [b.k])
                        bname = "b_shift" if cc < 32 else "b_scale"
                        ts("dve", modT[:, cc, :], b.t[:, 0:3], pvc(l, bname, cc % 32), None, ALU.add, None,
                           [b.k, "pv"], ["modT"])
                        b.rel()
                else:
                    g0 = (cb - 32) * 256
                    b = nb()
                    for ch in range(DC):
                        mm(b.t[0:3, 0:256], cTb[:, ch, :], wb[:, ch, :], ch == 0, ch == DC - 1, [wk, "cTb"], [b.k])
                    tt("dve", grow[0:3, g0:g0 + 256], b.t[0:3, 0:256], bgrow[0:3, g0:g0 + 256], ALU.add, [b.k, bgrowk], [growk])
                    b.rel()
            dma("pool", gate_d, grow[0:3, :], [growk], ["gate_d"])
            ts("dve", s1T[:], modT[:, 32:64, :], 1.0, None, ALU.add, None, ["modT"], ["s1T"])
            tt("dve", s1T[:], s1T[:], pvB(l, "norm_g", 0, 32, 3), ALU.mult, ["s1T", "pv"], ["s1T"])
            for s in range(3):
                if s == 0:
                    memset("pool", Mblk[0][:], 0.0, [MKEY[0]])
                    memset("pool", rwprev[0][:], 0.0, ["rwprev0"])
                    memset("pool", convh[0][:], 0.0, ["convh0"])
                else:
                    if s == 2:
                        load_state_M(l, s)
                    dma("sp", rwprev[s][:], st_shift[l, s - 1], (), [f"rwprev{s}"])
                    dma("sp", convh[s][:], st_conv[l, s - 1], (), [f"convh{s}"])

        def load_state_M(l, s):
            memset("pool", Mblk[s][:], 0.0, [MKEY[s]])
            dma("sp", Mblk[s][0:64, :, 0:64], st_M[l, s - 1, 0:64], [MKEY[s]], [MKEY[s]])
            dma("sp", Mblk[s][64:128, :, 64:128], st_M[l, s - 1, 64:128], [MKEY[s]], [MKEY[s]])

        def in_proj_fm(l, t, col0, ncols, consume):
            NT = t.NT
            j = 0
            c = col0
            while c < col0 + ncols:
                bw = min(256, col0 + ncols - c)
                bi = WIN_IDX[(c, bw)]
                wb, wk = load_ws(win_s[l][bi], f"win_s{l}_{bi}", bw)
                for h0 in range(0, bw, 128):
                    wd = min(128, bw - h0)
                    b = nb()
                    for ch in range(DC):
                        mm(b.t[0:wd, 0:NT], wb[:, ch, h0:h0 + wd], hT[:, ch, 0:NT], ch == 0, ch == DC - 1,
                           [wk, "hT"], [b.k])
                    consume(j, wd, b)
                    b.rel()
                    j += 1
                c += bw

        def phase1(l, t):
            A = new_phase("p1")
            xb = [A.get(f"x{i}", [D]) for i in range(2)]
            junk, junkk = A.get("junk", [D], BF16)
            ssq, ssqk = A.get("ssq", [4])
            rstd, rstdk = A.get("rstd", [4])
            seal(A)
            src = (x1_s if l > 0 else xs_in) if t.sample else (x1_p if l > 0 else xp)
            srck = ["x1_s" if t.sample else "x1_p"] if l > 0 else []
            for si, (r0, rows) in enumerate(t.subs):
                xt, xk = xb[si % 2]
                g0 = r0 if t.sample else t.idx * NTP + r0
                dma("sp", xt[0:rows, :], src[g0:g0 + rows, :], srck, [xk])
                act(junk[0:rows, :], xt[0:rows, :], AF.Square, [xk], [junkk, ssqk], accum_out=ssq[0:rows, si:si + 1])
                ts("dve", rstd[0:rows, si:si + 1], ssq[0:rows, si:si + 1], 1.0 / D, NORM_EPS, ALU.mult, ALU.add, [ssqk], [rstdk])
                act(rstd[0:rows, si:si + 1], rstd[0:rows, si:si + 1], AF.Sqrt, [rstdk], [rstdk])
                recip(rstd[0:rows, si:si + 1], rstd[0:rows, si:si + 1], [rstdk], [rstdk])
                ts("dve", xt[0:rows, :], xt[0:rows, :], rstd[0:rows, si:si + 1], None, ALU.mult, None, [xk, rstdk], [xk])
                for c4 in range(DC // 4):
                    b = nb()
                    for q in range(4):
                        ch = c4 * 4 + q
                        tr(b.t[:, q * 128:q * 128 + rows], xt[0:rows, ch * 128:(ch + 1) * 128], ident[0:rows, 0:rows],
                           [xk, "ident"], [b.k])
                    e = ev_eng()
                    for q in range(4):
                        ch = c4 * 4 + q
                        for (s, o, n, p0) in t.segs:
                            lo = max(o, r0); hi = min(o + n, r0 + rows)
                            if lo >= hi:
                                continue
                            src_ps = b.t[:, q * 128 + lo - r0:q * 128 + hi - r0]
                            if e == "act":
                                act(hT[:, ch, lo:hi], src_ps, AF.Identity, [b.k, "s1T", "modT"], ["hT"],
                                    scale=s1T[:, ch, s:s + 1], bias=modT[:, ch, s:s + 1])
                            else:
                                ts("dve", hT[:, ch, lo:hi], src_ps, s1T[:, ch, s:s + 1], modT[:, ch, s:s + 1], ALU.mult, ALU.add,
                                   [b.k, "s1T", "modT"], ["hT"])
                    b.rel()

        def phase2(l, t):
            NT = t.NT
            A = new_phase("p2")
            nseg = len(t.segs)
            Ls = t.segs[0][2]
            csb = [A.get(f"csb{i}", [NT]) for i in range(2)]
            ub = [A.get(f"ub{i}", [nseg, Ls + 2]) for i in range(2)]
            yb = [A.get(f"yb{i}", [NT]) for i in range(2)]
            sg = [A.get(f"sg{i}", [NT]) for i in range(2)]
            seal(A)
            cw = PV["conv_w"]
            for cc2 in range(4):
                def cons_c(j, wd, b):
                    cp("act", csb[j][0][:, :], b.t[:, 0:NT], [b.k], [csb[j][1]])
                in_proj_fm(l, t, OFF["cv_c"] + cc2 * 256, 256, cons_c)

                def cons_x(j, wd, b):
                    cc = cc2 * 2 + j
                    u, uk = ub[j]
                    y, yk = yb[j]
                    for si, (s, o, n, p0) in enumerate(t.segs):
                        cp("pool", u[:, si, 0:2], convh[s][:, cc, :], [f"convh{s}"], [uk])
                        tt("dve", u[:, si, 2:2 + n], csb[j][0][:, o:o + n], b.t[:, o:o + n], ALU.mult, [b.k, csb[j][1], uk], [uk])
                        cp("pool", convh[s][:, cc, :], u[:, si, n:n + 2], [uk], [f"convh{s}"])
                        ts("dve", y[:, o:o + n], u[:, si, 2:2 + n], pv[:, l, cw + 16 + cc:cw + 17 + cc], pvc(l, "conv_b", cc), ALU.mult, ALU.add,
                           [uk, "pv"], [yk])
                        stt(y[:, o:o + n], u[:, si, 1:1 + n], pv[:, l, cw + 8 + cc:cw + 9 + cc], y[:, o:o + n], ALU.mult, ALU.add, [uk, yk, "pv"], [yk])
                        stt(y[:, o:o + n], u[:, si, 0:n], pv[:, l, cw + cc:cw + 1 + cc], y[:, o:o + n], ALU.mult, ALU.add, [uk, yk, "pv"], [yk])
                in_proj_fm(l, t, OFF["cv_x"] + cc2 * 256, 256, cons_x)

                def cons_b(j, wd, b):
                    y, yk = yb[j]
                    tt("dve", y[:, 0:NT], y[:, 0:NT], b.t[:, 0:NT], ALU.mult, [b.k, yk], [yk])
                in_proj_fm(l, t, OFF["cv_b"] + cc2 * 256, 256, cons_b)

                def cons_g(j, wd, b):
                    cc = cc2 * 2 + j
                    y, yk = yb[j]
                    act(sg[j][0][:, :], b.t[:, 0:NT], AF.Silu, [b.k], [sg[j][1]])
                    tt("pool", mixT[:, 24 + cc, 0:NT], y[:, 0:NT], sg[j][0][:, :], ALU.mult, [yk, sg[j][1]], ["mixT"])
                in_proj_fm(l, t, OFF["cv_gate"] + cc2 * 256, 256, cons_g)

        def phase3(l, t):
            NT, C = t.NT, t.C
            W = 2 * C
            nseg = len(t.segs)
            Ls = t.segs[0][2]
            nchk = Ls // C
            nlev = int(round(math.log2(C))) - 1
            MSU, MSL, MUP, MUN = (msu, msl, mup, mun) if C == 64 else (msu32, msl32, mup32, mun32)
            mkeys = ["msu", "msl", "mup", "mun"] if C == 64 else ["msu32", "msl32", "mup32", "mun32"]
            rmask, rmk = (rmask64, "rmask64") if C == 64 else (rmask32, "rmask32")
            A = new_phase("p3")
            pre, prek = A.get("pre", [25, nseg, Ls + 2], BF16)
            w2a2, w2k = A.get("w2a2", [1024])
            x24, x24k = A.get("x24", [C])
            sig, sigk = A.get("sig", [8, C])
            a_, ak = A.get("a", [8, C])
            cs, csk = A.get("cs", [8, C])
            G, Gk = A.get("G", [8, C])
            iG, iGk = A.get("iG", [8, C])
            Gp, Gpk = A.get("Gp", [8, C])
            xr, xrk = A.get("xr", [4, C]); xk_, xkk = A.get("xk", [4, C]); xv, xvk = A.get("xv", [4, C])
            kkn, kknk = A.get("kkn", [4, C]); kmod, kmodk = A.get("kmod", [4, C]); rt, rtk = A.get("rt", [4, C])
            t1, t1k = A.get("t1", [4, C]); t2, t2k = A.get("t2", [4, C]); bon, bonk = A.get("bon", [4, C])
            blk = {}
            for nm in ("kap", "bt", "kt", "kh", "bh", "vb"):
                blk[nm] = A.get("blk_" + nm, [4, 2, C])
            prod = {}
            for nm in ("Vb", "Kt", "Bt", "X0", "A0", "A2t", "X1", "A1", "Q"):
                prod[nm] = A.get("pr_" + nm, [4, 128])
            A2R, A2Rk = A.get("A2R", [4, C]); ARn, ARnk = A.get("ARn", [4, C])
            Oq, Oqk = A.get("Oq", [512]); cen, cenk = A.get("cen", [512]); sqv, sqvk = Oq, Oqk
            st8, st8k = A.get("st8", [16])
            sgt, sgtk = cen, cenk
            seal(A)
            import os
            P3 = int(os.environ.get("P3STOP", "99"))
            if P3 <= -3:
                return
            dma("sp", w2a2[0:64, :], w_w2[l], (), [w2k])
            dma("sp", w2a2[64:128, :], w_a2[l], (), [w2k])
            if P3 <= -2:
                return
            for nm in blk:
                memset("pool", blk[nm][0][:], 0.0, [blk[nm][1]])
            if P3 <= -1:
                return

            def cons_pre(j, wd, b):
                for si, (s, o, n, p0) in enumerate(t.segs):
                    CPF = int(os.environ.get("CPF", "7"))
                    if CPF & 1:
                        cp("pool", pre[:, j, si, 1:2], rwprev[s][:, j:j + 1], [f"rwprev{s}"], [prek])
                    if CPF & 2:
                        cp(ev_eng(), pre[:, j, si, 2:2 + n], b.t[:, o:o + n], [b.k], [prek])
                    if CPF & 4:
                        cp("dve", rwprev[s][:, j:j + 1], b.t[:, o + n - 1:o + n], [b.k], [f"rwprev{s}"])
            in_proj_fm(l, t, OFF["rw_pre"], 3200, cons_pre)
            if P3 <= 0:
                return

            def lerp(eng, out, j0, nj, si, c, outk):
                cur = pre[:, j0:j0 + nj, si, 2 + c * C:2 + (c + 1) * C]
                prv = pre[:, j0:j0 + nj, si, 1 + c * C:1 + (c + 1) * C]
                tt(eng, out, prv, cur, ALU.subtract, [prek], [outk])
                tt(eng, out, out, pvB(l, "mu", j0, nj, C), ALU.mult, [outk, "pv"], [outk])
                tt(eng, out, out, cur, ALU.add, [outk, prek], [outk])

            def v3(ap2d, rows, a, b_):
                return ap2d[0:rows, 0:a * b_].rearrange("p (a b) -> p a b", a=a)

            for si, (s, o, n, p0) in enumerate(t.segs):
                Mk = MKEY[s]
                M = Mblk[s]
                for c in range(nchk):
                    tok0 = o + c * C
                    lerp("pool", x24.unsqueeze(1), 24, 1, si, c, x24k)
                    act(x24[0:64, :], x24[0:64, :], AF.Tanh, [x24k], [x24k])
                    bw = nb(); ba = nb()
                    for j in range(8):
                        mm(bw.t[:, j * C:(j + 1) * C], w2a2[0:64, j * 128:(j + 1) * 128], x24[0:64, :], True, True, [w2k, x24k], [bw.k])
                    for j in range(8):
                        mm(ba.t[:, j * C:(j + 1) * C], w2a2[64:128, j * 128:(j + 1) * 128], x24[64:128, :], True, True, [w2k, x24k], [ba.k])
                    tt("dve", sig[:], bw.v3(128, 8, C), pvB(l, "w0", 0, 8, C), ALU.add, [bw.k, "pv"], [sigk])
                    bw.rel()
                    act(sig[:], sig[:], AF.Sigmoid, [sigk], [sigk])
                    tt("dve", a_[:], ba.v3(128, 8, C), pvB(l, "a0", 0, 8, C), ALU.add, [ba.k, "pv"], [ak])
                    ba.rel()
                    act(a_[:], a_[:], AF.Sigmoid, [ak], [ak])
                    sflat = sig[:].rearrange("p a b -> p (a b)")
                    cflat = cs[:].rearrange("p a b -> p (a b)")
                    _op("dve", lambda e, cflat=cflat, sflat=sflat: e.tensor_tensor_scan(out=cflat, data0=rmask[:, 0:8 * C], data1=sflat,
                                                                                       initial=0.0, op0=ALU.mult, op1=ALU.add),
                        [rmk, sigk], [csk])
                    act(G[:], cs[:], AF.Exp, [csk], [Gk], scale=-C0)
                    act(iG[:], cs[:], AF.Exp, [csk], [iGk], scale=C0)
                    tt("pool", Gp[:], cs[:], sig[:], ALU.subtract, [csk, sigk], [Gpk])
                    act(Gp[:], Gp[:], AF.Exp, [Gpk], [Gpk], scale=-C0)
                    if P3 <= 1:
                        continue
                    for q in range(2):
                        j0 = 4 * q
                        Sq = slice(j0, j0 + 4)
                        lerp("pool", xr[:], j0, 4, si, c, xrk)
                        lerp("pool", xk_[:], 8 + j0, 4, si, c, xkk)
                        lerp("pool", xv[:], 16 + j0, 4, si, c, xvk)
                        tt("dve", t1[:], xk_[:], pvB(l, "kk", j0, 4, C), ALU.mult, [xkk, "pv"], [t1k])
                        act(t2[:], t1[:], AF.Square, [t1k], [t2k])
                        bs = nb()
                        for jj in range(4):
                            mm(bs.t[:, jj * C:(jj + 1) * C], bones[:, :], t2[:, jj, :], True, True, ["bones", t2k], [bs.k])
                        ts("dve", t2[:], bs.v3(128, 4, C), 1e-12, None, ALU.add, None, [bs.k], [t2k])
                        bs.rel()
                        act(t2[:], t2[:], AF.Sqrt, [t2k], [t2k])
                        recip(t2[:], t2[:], [t2k], [t2k])
                        tt("dve", kkn[:], t1[:], t2[:], ALU.mult, [t1k, t2k], [kknk])
                        tt("dve", t1[:], a_[:, Sq, :], pvB(l, "ka", j0, 4, C), ALU.mult, [ak, "pv"], [t1k])
                        tt("dve", t1[:], t1[:], omka[:, l, j0:j0 + 4].unsqueeze(2).broadcast_to([128, 4, C]), ALU.add, [t1k, "omka"], [t1k])
                        tt("dve", kmod[:], xk_[:], t1[:], ALU.mult, [xkk, t1k], [kmodk])
                        tt("pool", t1[:], xr[:], kmod[:], ALU.mult, [xrk, kmodk], [t1k])
                        tt("pool", t1[:], t1[:], pvB(l, "rk", j0, 4, C), ALU.mult, [t1k, "pv"], [t1k])
                        br = nb()
                        for jj in range(4):
                            mm(br.t[:, jj * C:(jj + 1) * C], bones[:, :], t1[:, jj, :], True, True, ["bones", t1k], [br.k])
                        tt("dve", bon[:], br.v3(128, 4, C), xv[:], ALU.mult, [br.k, xvk], [bonk])
                        br.rel()
                        tt("pool", bon[:], bon[:], pvB(l, "ln_b", j0, 4, C), ALU.add, [bonk, "pv"], [bonk])
                        if P3 <= 2:
                            continue
                        tt("pool", rt[:], xr[:], G[:, Sq, :], ALU.mult, [xrk, Gk], [rtk])
                        tt("pool", t1[:], kkn[:], a_[:, Sq, :], ALU.mult, [kknk, ak], [t1k])
                        for hp in range(2):
                            ps_ = slice(64 * hp, 64 * hp + 64)
                            GC = G[ps_, Sq, C - 1:C].broadcast_to([64, 4, C])
                            e1 = "dve" if hp == 0 else "pool"
                            tt(e1, blk["kap"][0][ps_, :, hp, :], kkn[ps_], Gp[ps_, Sq, :], ALU.mult, [kknk, Gpk], [blk["kap"][1]])
                            tt(e1, blk["bt"][0][ps_, :, hp, :], t1[ps_], iG[ps_, Sq, :], ALU.mult, [t1k, iGk], [blk["bt"][1]])
                            tt(e1, blk["kt"][0][ps_, :, hp, :], kmod[ps_], iG[ps_, Sq, :], ALU.mult, [kmodk, iGk], [blk["kt"][1]])
                            tt(e1, blk["kh"][0][ps_, :, hp, :], blk["kt"][0][ps_, :, hp, :], GC, ALU.mult, [blk["kt"][1], Gk], [blk["kh"][1]])
                            tt(e1, blk["bh"][0][ps_, :, hp, :], blk["bt"][0][ps_, :, hp, :], GC, ALU.mult, [blk["bt"][1], Gk], [blk["bh"][1]])
                            cp(e1, blk["vb"][0][ps_, :, hp, :], xv[ps_], [xvk], [blk["vb"][1]])

                        def B(nm, jj):
                            return blk[nm][0][:, jj, :, :].rearrange("p a b -> p (a b)")

                        def PR(nm, jj, rows=W):
                            return prod[nm][0][0:rows, jj, 0:128]

                        def PRW(nm, jj):
                            return prod[nm][0][0:W, jj, 0:W]

                        if P3 <= 3:
                            continue
                        for nm_in, nm_out, neg in (("vb", "Vb", False), ("kh", "Kt", False), ("bh", "Bt", True)):
                            b = nb()
                            for jj in range(4):
                                tr(b.t[0:W, jj * 128:(jj + 1) * 128], B(nm_in, jj), ident[:, :], [blk[nm_in][1], "ident"], [b.k])
                            dst = prod[nm_out][0][0:W, :, :]
                            if neg:
                                ts("dve", dst, b.v3(W, 4, 128), -1.0, None, ALU.mult, None, [b.k], [prod[nm_out][1]])
                            else:
                                cp("act", dst, b.v3(W, 4, 128), [b.k], [prod[nm_out][1]])
                            b.rel()
                        for (lh, rh, outnm, mask, mk) in (("bt", "kap", "X0", MSU, mkeys[0]), ("kap", "bt", "A0", MSL, mkeys[1]),
                                                          ("kt", "kap", "A2t", MSU, mkeys[0])):
                            b = nb()
                            for jj in range(4):
                                mm(b.t[0:W, jj * W:(jj + 1) * W], B(lh, jj), B(rh, jj), True, True, [blk[lh][1], blk[rh][1]], [b.k])
                            tt("dve", prod[outnm][0][0:W, :, 0:W], b.v3(W, 4, W), mask[0:W, 0:W].unsqueeze(1).broadcast_to([W, 4, W]), ALU.mult,
                               [b.k, mk], [prod[outnm][1]])
                            b.rel()
                        for (lh, out_ap, outk, mask, mk) in (("kt", A2R, A2Rk, MUP, mkeys[2]), ("bt", ARn, ARnk, MUN, mkeys[3])):
                            b = nb()
                            for jj in range(4):
                                mm(b.t[0:W, jj * C:(jj + 1) * C], B(lh, jj), rt[:, jj, :], True, True, [blk[lh][1], rtk], [b.k])
                            tt("dve", out_ap[0:W, :, :], b.v3(W, 4, C), mask[0:W, 0:C].unsqueeze(1).broadcast_to([W, 4, C]), ALU.mult,
                               [b.k, mk], [outk])
                            b.rel()
                        if P3 <= 4:
                            continue
                        tt("pool", prod["Q"][0][0:W, :, 0:W], ident[0:W, 0:W].unsqueeze(1).broadcast_to([W, 4, W]), prod["X0"][0][0:W, :, 0:W],
                           ALU.subtract, ["ident", prod["X0"][1]], [prod["Q"][1]])
                        cur = ("X0", "A0"); nxt = ("X1", "A1")
                        for lev in range(nlev):
                            last = (lev == nlev - 1)
                            if not last:
                                b1 = nb()
                                for jj in range(4):
                                    mm(b1.t[0:W, jj * W:(jj + 1) * W], PRW(cur[1], jj), PRW(cur[0], jj), True, True,
                                       [prod[cur[0]][1], prod[cur[1]][1]], [b1.k])
                            b2 = nb()
                            for jj in range(4):
                                mm(b2.t[0:W, jj * W:(jj + 1) * W], PRW(cur[0], jj), PRW(cur[1], jj), True, True,
                                   [prod[cur[0]][1], prod[cur[1]][1]], [b2.k])
                            if not last:
                                cp("act", prod[nxt[0]][0][0:W, :, 0:W], b1.v3(W, 4, W), [b1.k], [prod[nxt[0]][1]])
                                b1.rel()
                            cp("dve", prod[nxt[1]][0][0:W, :, 0:W], b2.v3(W, 4, W), [b2.k], [prod[nxt[1]][1]])
                            b2.rel()
                            b3 = nb()
                            for jj in range(4):
                                mm(b3.t[0:W, jj * W:(jj + 1) * W], PRW(nxt[1], jj), PRW("Q", jj), True, True,
                                   [prod[nxt[1]][1], prod["Q"][1]], [b3.k])
                            tt("dve", prod["Q"][0][0:W, :, 0:W], b3.v3(W, 4, W), prod["Q"][0][0:W, :, 0:W], ALU.add,
                               [b3.k, prod["Q"][1]], [prod["Q"][1]])
                            b3.rel()
                            cur, nxt = nxt, cur
                        if P3 <= 5:
                            continue
                        RHSn, Un = cur[0], nxt[0]
                        b = nb()
                        for jj in range(4):
                            mm(b.t[0:W, jj * 128:(jj + 1) * 128], B("kap", jj), M[:, j0 + jj, :], True, False, [blk["kap"][1], Mk], [b.k])
                            mm(b.t[0:W, jj * 128:(jj + 1) * 128], PRW("A2t", jj), PR("Vb", jj), False, True, [prod["A2t"][1], prod["Vb"][1]], [b.k])
                        cp("act", prod[RHSn][0][0:W, :, :], b.v3(W, 4, 128), [b.k], [prod[RHSn][1]])
                        b.rel()
                        b = nb()
                        for jj in range(4):
                            mm(b.t[0:W, jj * 128:(jj + 1) * 128], PRW("Q", jj), PR(RHSn, jj), True, True, [prod["Q"][1], prod[RHSn][1]], [b.k])
                        cp("dve", prod[Un][0][0:W, :, :], b.v3(W, 4, 128), [b.k], [prod[Un][1]])
                        b.rel()
                        bO = nb()
                        for jj in range(4):
                            mm(bO.t[0:C, jj * 128:(jj + 1) * 128], rt[:, jj, :], M[:, j0 + jj, :], True, False, [rtk, Mk], [bO.k])
                            mm(bO.t[0:C, jj * 128:(jj + 1) * 128], A2R[0:W, jj, :], PR("Vb", jj), False, False, [A2Rk, prod["Vb"][1]], [bO.k])
                            mm(bO.t[0:C, jj * 128:(jj + 1) * 128], ARn[0:W, jj, :], PR(Un, jj), False, True, [ARnk, prod[Un][1]], [bO.k])
                        cp("act", Oq[0:C, :], bO.t[0:C, 0:512], [bO.k], [Oqk])
                        bO.rel()
                        bM = nb()
                        for jj in range(4):
                            mm(bM.t[:, jj * 128:(jj + 1) * 128], PR("Kt", jj), PR("Vb", jj), True, False, [prod["Kt"][1], prod["Vb"][1]], [bM.k])
                            mm(bM.t[:, jj * 128:(jj + 1) * 128], PR("Bt", jj), PR(Un, jj), False, True, [prod["Bt"][1], prod[Un][1]], [bM.k])
                        tt("pool", M[:, Sq, :], M[:, Sq, :], G[:, Sq, C - 1:C].broadcast_to([128, 4, 128]), ALU.mult, [Mk, Gk], [Mk])
                        tt("dve", M[:, Sq, :], M[:, Sq, :], bM.v3(128, 4, 128), ALU.add, [Mk, bM.k], [Mk])
                        bM.rel()
                        if P3 <= 6:
                            continue
                        O3 = Oq[0:C, :].rearrange("p (a b) -> p a b", a=8)
                        c3 = cen[0:C, :].rearrange("p (a b) -> p a b", a=8)
                        s3 = sqv[0:C, :].rearrange("p (a b) -> p a b", a=8)
                        red(st8[0:C, 0:8], O3, [Oqk], [st8k])
                        ts("dve", st8[0:C, 0:8], st8[0:C, 0:8], 1.0 / 64, None, ALU.mult, None, [st8k], [st8k])
                        tt("dve", c3, O3, st8[0:C, 0:8].unsqueeze(2).broadcast_to([C, 8, 64]), ALU.subtract, [Oqk, st8k], [cenk])
                        act(sqv[0:C, :], cen[0:C, :], AF.Square, [cenk], [sqvk])
                        red(st8[0:C, 8:16], s3, [sqvk], [st8k])
                        ts("dve", st8[0:C, 8:16], st8[0:C, 8:16], 1.0 / 64, GN_EPS, ALU.mult, ALU.add, [st8k], [st8k])
                        act(st8[0:C, 8:16], st8[0:C, 8:16], AF.Sqrt, [st8k], [st8k])
                        recip(st8[0:C, 8:16], st8[0:C, 8:16], [st8k], [st8k])
                        tt("dve", c3, c3, st8[0:C, 8:16].unsqueeze(2).broadcast_to([C, 8, 64]), ALU.mult, [cenk, st8k], [cenk])
                        bT = nb()
                        for jj in range(4):
                            tr(bT.t[:, jj * C:(jj + 1) * C], cen[0:C, jj * 128:(jj + 1) * 128], ident[0:C, 0:C], [cenk, "ident"], [bT.k])
                        tt("dve", t2[:], bT.v3(128, 4, C), pvB(l, "ln_g", j0, 4, C), ALU.mult, [bT.k, "pv"], [t2k])
                        bT.rel()
                        tt("pool", mixT[:, j0:j0 + 4, tok0:tok0 + C], t2[:], bon[:], ALU.add, [t2k, bonk], ["mixT"])

            def cons_gate(j, wd, b):
                act(sgt[:, 0:NT], b.t[:, 0:NT], AF.Silu, [b.k], [sgtk])
                tt("dve", mixT[:, j, 0:NT], mixT[:, j, 0:NT], sgt[:, 0:NT], ALU.mult, ["mixT", sgtk], ["mixT"])
            in_proj_fm(l, t, OFF["rw_gate"], 1024, cons_gate)

        def phase4(l, t):
            NT = t.NT
            A = new_phase("p4")
            cqb, cqbk = A.get("cqb", [8, NT], BF16)
            cosT, cosk = A.get("cos", [NT]); sinT, sink = A.get("sin", [NT])
            sq = [A.get(f"sq{i}", [NTP]) for i in range(2)]
            rs = [A.get(f"rs{i}", [NTP]) for i in range(2)]
            xrq, xrqk = A.get("xrq", [NT])
            mark = A.off
            ncommon = len(A.keys)
            latT, latTk = A.get("latT", [4, NTP])
            latb, latbk = A.get("latb", [4, NTP], BF16)
            krb, krbk = A.get("krb", [NTP], BF16)
            wuk, wukk = A.get("wuk", [4, 2048], BF16)
            wuvb = [A.get(f"wuv{i}", [4, 512], BF16) for i in range(2)]
            Vp = [A.get(f"Vp{i}", [512], BF16) for i in range(2)]
            knb = [A.get(f"knb{i}", [NTP], BF16) for i in range(2)]
            lato = [A.get(f"lato{i}", [512]) for i in range(2)]
            kro, krok = A.get("kro", [64])
            ctm, ctmk = A.get("ctm", [4, 512])
            ktm, ktmk = A.get("ktm", [4, 64])
            yy, yyk = A.get("yy", [NT]); kf, kfk = A.get("kf", [NT]); ki, kik = A.get("ki", [NT], I32)
            seal(A)
            sqi = [0]

            def next_sq():
                sqi[0] += 1
                return sq[sqi[0] % 2], rs[sqi[0] % 2]

            def rms_stats(b, rows, n, div, eps):
                (s_, sk), (r_, rk) = next_sq()
                act(s_[0:rows, 0:n], b.t[0:rows, 0:n], AF.Square, [b.k], [sk])
                b2 = nb()
                mm(b2.t[0:rows, 0:n], onesf[0:rows, 0:rows], s_[0:rows, 0:n], True, True, ["onesf", sk], [b2.k])
                ts("dve", r_[0:rows, 0:n], b2.t[0:rows, 0:n], 1.0 / div, eps, ALU.mult, ALU.add, [b2.k], [rk])
                b2.rel()
                act(r_[0:rows, 0:n], r_[0:rows, 0:n], AF.Sqrt, [rk], [rk])
                recip(r_[0:rows, 0:n], r_[0:rows, 0:n], [rk], [rk])
                return r_, rk

            pos, posk = yy, yyk
            dma("sp", pos[0:64, :], c_pos[:, t.pos0:t.pos0 + NT], (), [posk])
            ts("dve", yy[0:64, :], pos[0:64, :], freq[:, 0:1], 1.0 / (2 * math.pi), ALU.mult, ALU.mult, [posk, "freq"], [yyk])
            cp("dve", ki[0:64, :], yy[0:64, :], [yyk], [kik])
            cp("dve", kf[0:64, :], ki[0:64, :], [kik], [kfk])
            tt("dve", yy[0:64, :], yy[0:64, :], kf[0:64, :], ALU.subtract, [yyk, kfk], [yyk])
            act(sinT[0:64, :], yy[0:64, :], AF.Sin, [yyk], [sink], scale=math.pi)
            act(kf[0:64, :], yy[0:64, :], AF.Sin, [yyk], [kfk], scale=math.pi / 2)
            tt("dve", kf[0:64, :], kf[0:64, :], kf[0:64, :], ALU.mult, [kfk], [kfk])
            ts("dve", kf[0:64, :], kf[0:64, :], -2.0, 1.0, ALU.mult, ALU.add, [kfk], [kfk])
            tt("dve", cosT[0:64, :], sinT[0:64, :], sinT[0:64, :], ALU.mult, [sink], [cosk])
            ts("dve", cosT[0:64, :], cosT[0:64, :], -2.0, 1.0, ALU.mult, ALU.add, [cosk], [cosk])
            tt("dve", sinT[0:64, :], sinT[0:64, :], kf[0:64, :], ALU.mult, [sink, kfk], [sink])
            ts("dve", sinT[0:32, :], sinT[0:32, :], -2.0, None, ALU.mult, None, [sink], [sink])
            ts("dve", sinT[32:64, :], sinT[32:64, :], 2.0, None, ALU.mult, None, [sink], [sink])

            def rotary(out_ap, outk, x_ap, xk, n):
                b = nb()
                mm(b.t[0:64, 0:n], pswap[:, :], x_ap, True, True, ["pswap", xk], [b.k])
                (s_, sk), _ = next_sq()
                tt("dve", s_[0:64, 0:n], b.t[0:64, 0:n], sinT[0:64, 0:n], ALU.mult, [b.k, sink], [sk])
                b.rel()
                tt("dve", x_ap, x_ap, cosT[0:64, 0:n], ALU.mult, [xk, cosk], [xk])
                tt("dve", out_ap, x_ap, s_[0:64, 0:n], ALU.add, [xk, sk], [outk])

            def cons_cq(j, wd, b):
                cp("dve", cqb[:, j, :], b.t[:, 0:NT], [b.k], [cqbk])
                (s_, sk), _ = next_sq()
                act(s_[:, 0:NT], b.t[:, 0:NT], AF.Square, [b.k], [sk])
                mm(ACC0.t[:, 0:NT], onesf[:, :], s_[:, 0:NT], j == 0, j == 7, ["onesf", sk], [ACC0.k])
            in_proj_fm(l, t, OFF["cq"], 1024, cons_cq)
            _, (r_, rk) = next_sq()
            ts("dve", r_[:, 0:NT], ACC0.t[:, 0:NT], 1.0 / 1024, NORM_EPS, ALU.mult, ALU.add, [ACC0.k], [rk])
            act(r_[:, 0:NT], r_[:, 0:NT], AF.Sqrt, [rk], [rk])
            recip(r_[:, 0:NT], r_[:, 0:NT], [rk], [rk])
            for j in range(8):
                stt(cqb[:, j, :], cqb[:, j, :], pvc(l, "qng", j), r_[:, 0:NT], ALU.mult, ALU.mult, [cqbk, "pv", rk], [cqbk])

            def cons_ckv(j, wd, b):
                cp("dve", latT[:, j, 0:NT], b.t[:, 0:NT], [b.k], [latTk])
                (s_, sk), _ = next_sq()
                act(s_[:, 0:NT], b.t[:, 0:NT], AF.Square, [b.k], [sk])
                mm(ACC1.t[:, 0:NT], onesf[:, :], s_[:, 0:NT], j == 0, j == 3, ["onesf", sk], [ACC1.k])
            in_proj_fm(l, t, OFF["ckv"], 512, cons_ckv)
            _, (r_, rk) = next_sq()
            ts("dve", r_[:, 0:NT], ACC1.t[:, 0:NT], 1.0 / 512, NORM_EPS, ALU.mult, ALU.add, [ACC1.k], [rk])
            act(r_[:, 0:NT], r_[:, 0:NT], AF.Sqrt, [rk], [rk])
            recip(r_[:, 0:NT], r_[:, 0:NT], [rk], [rk])
            for j in range(4):
                stt(latT[:, j, 0:NT], latT[:, j, 0:NT], pvc(l, "kvg", j), r_[:, 0:NT], ALU.mult, ALU.mult, [latTk, "pv", rk], [latTk])
                cp("pool", latb[:, j, 0:NT], latT[:, j, 0:NT], [latTk], [latbk])
            lat_dst = lat_s if t.sample else lat_p
            kr_dst = kr_s if t.sample else kr_p
            for si, (r0, rows) in enumerate(t.subs):
                g0 = r0 if t.sample else t.idx * NTP + r0
                b = nb()
                for j in range(4):
                    tr(b.t[0:rows, j * 128:(j + 1) * 128], latT[:, j, r0:r0 + rows], ident[:, :], [latTk, "ident"], [b.k])
                lo_, lok = lato[si % 2]
                cp("act", lo_[0:rows, :], b.t[0:rows, 0:512], [b.k], [lok])
                b.rel()
                dma("pool", lat_dst[l, g0:g0 + rows, :], lo_[0:rows, :], [lok], [])

            def cons_kr(j, wd, b):
                r_, rk = rms_stats(b, 64, NT, 64, NORM_EPS)
                stt(xrq[0:64, :], b.t[0:64, 0:NT], pvc(l, "kn_rope", 0, 1, 64), r_[0:64, 0:NT], ALU.mult, ALU.mult, [b.k, "pv", rk], [xrqk])
            in_proj_fm(l, t, OFF["kr"], 64, cons_kr)
            rotary(xrq[0:64, :], xrqk, xrq[0:64, :], xrqk, NT)
            cp("pool", krb[0:64, 0:NT], xrq[0:64, :], [xrqk], [krbk])
            for si, (r0, rows) in enumerate(t.subs):
                g0 = r0 if t.sample else t.idx * NTP + r0
                b = nb()
                tr(b.t[0:rows, 0:64], xrq[0:64, r0:r0 + rows], ident[0:64, 0:64], [xrqk, "ident"], [b.k])
                cp("act", kro[0:rows, :], b.t[0:rows, 0:64], [b.k], [krok])
                b.rel()
                dma("pool", kr_dst[l, g0:g0 + rows, :], kro[0:rows, :], [krok], [])

            dma("sp", wuk[:], wuk_s[l].rearrange("p (c n) -> p c n", c=4), [f"wuk_s{l}"], [wukk])

            def emit_kv(s, lat_ap, n, key0):
                kd = f"kT_{l}_{s}"
                vd = f"V_{l}_{s}"
                for h in range(H):
                    b = nb()
                    for ch in range(4):
                        mm(b.t[:, 0:n], wuk[:, ch, h * 128:(h + 1) * 128], lat_ap[:, ch, :], ch == 0, ch == 3, [wukk, latbk], [b.k])
                    r_, rk = rms_stats(b, 128, n, 128, NORM_EPS)
                    kb_, kbk = knb[h % 2]
                    stt(kb_[:, 0:n], b.t[:, 0:n], pvc(l, "kn_nope"), r_[:, 0:n], ALU.mult, ALU.mult, [b.k, "pv", rk], [kbk])
                    b.rel()
                    dma("pool", kT_d[l][s][h, :, key0:key0 + n], kb_[:, 0:n], [kbk], [kd])
                for vbk in range(4):
                    wv, wvk = wuvb[vbk % 2]
                    dma("sp", wv[:], wuv_s[l][vbk].rearrange("p (c n) -> p c n", c=4), [f"wuv_s{l}"], [wvk])
                    for tb in range((n + 127) // 128):
                        m = min(128, n - tb * 128)
                        b = nb()
                        for ch in range(4):
                            mm(b.t[0:m, 0:512], lat_ap[:, ch, tb * 128:tb * 128 + m], wv[:, ch, :], ch == 0, ch == 3, [latbk, wvk], [b.k])
                        vp, vpk = Vp[(vbk * 4 + tb) % 2]
                        cp(ev_eng(), vp[0:m, :], b.t[0:m, 0:512], [b.k], [vpk])
                        b.rel()
                        dma("pool", V_d[l][s][key0 + tb * 128:key0 + tb * 128 + m, vbk * 512:(vbk + 1) * 512], vp[0:m, :], [vpk], [vd])

            if t.sample:
                for (s, o, n, p0) in t.segs:
                    for blk_ in range(PAST // 512):
                        for sub in range(4):
                            r0 = blk_ * 512 + sub * 128
                            dma("sp", ctm[:, sub, :], cache_lat[l, s - 1, r0:r0 + 128, :], (), [ctmk])
                            dma("sp", ktm[:, sub, :], cache_kr[l, s - 1, r0:r0 + 128, :], (), [ktmk])
                        for sub in range(4):
                            b = nb()
                            for j in range(4):
                                tr(b.t[:, j * 128:(j + 1) * 128], ctm[:, sub, j * 128:(j + 1) * 128], ident[:, :], [ctmk, "ident"], [b.k])
                            for j in range(4):
                                cp(ev_eng(), latb[:, j, sub * 128:(sub + 1) * 128], b.t[:, j * 128:(j + 1) * 128], [b.k], [latbk])
                            b.rel()
                        b = nb()
                        for sub in range(4):
                            tr(b.t[0:64, sub * 128:(sub + 1) * 128], ktm[:, sub, :], ident[:, :], [ktmk, "ident"], [b.k])
                        cp("act", krb[0:64, 0:512], b.t[0:64, 0:512], [b.k], [krbk])
                        b.rel()
                        dma("pool", krT_d[l][s][:, blk_ * 512:(blk_ + 1) * 512], krb[0:64, 0:512], [krbk], [f"krT_{l}_{s}"])
                        emit_kv(s, latb[:, :, 0:512], 512, blk_ * 512)
                for j in range(4):
                    cp("pool", latb[:, j, 0:NT], latT[:, j, 0:NT], [latTk], [latbk])
                cp("pool", krb[0:64, 0:NT], xrq[0:64, :], [xrqk], [krbk])
            for (s, o, n, p0) in t.segs:
                dma("pool", krT_d[l][s][:, p0:p0 + n], krb[0:64, o:o + n], [krbk], [f"krT_{l}_{s}"])
                emit_kv(s, latb[:, :, o:o + n], n, p0)

            old_keys = list(A.keys)
            A.off = mark
            nk_max = max(p0 + n for (s, o, n, p0) in t.segs)
            nkb_max = (nk_max + 127) // 128
            nseg = len(t.segs)
            k0 = len(A.keys)
            knA = [A.get(f"kn{i}", [nk_max], BF16) for i in range(2)]
            VA = [A.get(f"V{i}", [nkb_max, 128], BF16) for i in range(2)]
            krA, krAk = A.get("krA", [nseg, nk_max], BF16)
            wuqh = [A.get(f"wuq{i}", [8, 192], BF16) for i in range(2)]
            qn, qnk = A.get("qn", [NT], BF16)
            qr, qrk = A.get("qr", [NT], BF16)
            PT = [A.get(f"PT{i}", [NTP], BF16) for i in range(4)]
            rec, reck = A.get("rec", [NT]); tq, tqk = A.get("tq", [NT])
            sgm, sgmk = A.get("sgm", [2, NT], BF16)
            xq2, xq2k = A.get("xq2", [NT])
            if not t.sample:
                matt, mattk = A.get("matt", [4, NTP], BF16)
            newk = A.keys[k0:]
            P.rekey(old_keys[ncommon:], newk)
            prev_keys[0] = list(A.keys[:ncommon]) + newk
            if not t.sample:
                dma("pool", matt[:], c_matt, (), [mattk])
            for si, (s, o, n, p0) in enumerate(t.segs):
                dma("sp", krA[0:64, si, 0:p0 + n], krT_d[l][s][:, 0:p0 + n], [f"krT_{l}_{s}"], [krAk])
            pti = [0]
            for h in range(H):
                if h % 2 == 0:
                    def cons_mg(j, wd, b):
                        act(sgm[:, j, :], b.t[:, 0:NT], AF.Silu, [b.k], [sgmk])
                    in_proj_fm(l, t, OFF["mla_gate"] + h * 128, 256, cons_mg)
                wq_, wqk = wuqh[h % 2]
                if h == 0:
                    dma("sp", wq_[:], wuq_s[l][0].rearrange("p (c n) -> p c n", c=8), [f"wuq_s{l}"], [wqk])
                if h + 1 < H:
                    dma("sp", wuqh[(h + 1) % 2][0][:], wuq_s[l][h + 1].rearrange("p (c n) -> p c n", c=8), [f"wuq_s{l}"], [wuqh[(h + 1) % 2][1]])
                b = nb()
                for ch in range(8):
                    mm(b.t[:, 0:NT], wq_[:, ch, 0:128], cqb[:, ch, :], ch == 0, ch == 7, [wqk, cqbk], [b.k])
                r_, rk = rms_stats(b, 128, NT, 128, NORM_EPS)
                stt(qn[:, :], b.t[:, 0:NT], gq[:, l, 0:1], r_[:, 0:NT], ALU.mult, ALU.mult, [b.k, "gq", rk], [qnk])
                b.rel()
                b = nb()
                for ch in range(8):
                    mm(b.t[0:64, 0:NT], wq_[:, ch, 128:192], cqb[:, ch, :], ch == 0, ch == 7, [wqk, cqbk], [b.k])
                r_, rk = rms_stats(b, 64, NT, 64, NORM_EPS)
                stt(xq2[0:64, :], b.t[0:64, 0:NT], gq[0:64, l, 1:2], r_[0:64, 0:NT], ALU.mult, ALU.mult, [b.k, "gq", rk], [xq2k])
                b.rel()
                rotary(qr[0:64, :], qrk, xq2[0:64, :], xq2k, NT)
                for si, (s, o, n, p0) in enumerate(t.segs):
                    nk = p0 + n
                    nkb = (nk + 127) // 128
                    nfull = nk // 128
                    kn, knk = knA[(h * nseg + si) % 2]
                    V, Vk = VA[(h * nseg + si) % 2]
                    dma("sp", kn[:, 0:nk], kT_d[l][s][h, :, 0:nk], [f"kT_{l}_{s}"], [knk])
                    if nfull > 0:
                        dma("sp", V[:, 0:nfull, :], V_d[l][s][0:nfull * 128, h * 128:(h + 1) * 128].rearrange("(b p) d -> p b d", p=128),
                            [f"V_{l}_{s}"], [Vk])
                    if nkb > nfull:
                        m_ = nk - nfull * 128
                        dma("sp", V[0:m_, nfull, :], V_d[l][s][nfull * 128:nk, h * 128:(h + 1) * 128], [f"V_{l}_{s}"], [Vk])
                    for kb in range(nkb):
                        m = min(128, nk - kb * 128)
                        diag = (not t.sample) and kb >= t.idx * 4
                        jd = kb - t.idx * 4 if diag else 0
                        qlo = jd * 128 if diag else 0
                        nq = n - qlo
                        b = nb()
                        mm(b.t[0:m, 0:nq], kn[:, kb * 128:kb * 128 + m], qn[:, o + qlo:o + n], True, False, [knk, qnk], [b.k])
                        mm(b.t[0:m, 0:nq], krA[0:64, si, kb * 128:kb * 128 + m], qr[0:64, o + qlo:o + n], False, True, [krAk, qrk], [b.k])
                        pt, ptk = PT[pti[0] % 4]
                        pti[0] += 1
                        act(pt[0:m, 0:nq], b.t[0:m, 0:nq], AF.Exp, [b.k], [ptk])
                        b.rel()
                        if diag:
                            tt("pool", pt[0:m, 0:nq], pt[0:m, 0:nq], matt[0:m, jd, qlo:NTP], ALU.mult, [ptk, mattk], [ptk])
                        mm(ACC0.t[:, qlo:n], V[0:m, kb, :], pt[0:m, 0:nq], kb == 0, kb == nkb - 1, [Vk, ptk], [ACC0.k])
                        mm(ACC1.t[:, qlo:n], onesb[0:m, :], pt[0:m, 0:nq], kb == 0, kb == nkb - 1, ["onesb", ptk], [ACC1.k])
                    recip(rec[:, 0:n], ACC1.t[:, 0:n], [ACC1.k], [reck])
                    tt("dve", tq[:, 0:n], ACC0.t[:, 0:n], rec[:, 0:n], ALU.mult, [ACC0.k, reck], [tqk])
                    tt("pool", mixT[:, 8 + h, o:o + n], tq[:, 0:n], sgm[:, h % 2, o:o + n], ALU.mult, [tqk, sgmk], ["mixT"])

        def phase5(l, t):
            NT = t.NT
            nsub = len(t.subs)
            rows = t.subs[0][1]
            A = new_phase("p5")
            gbc, gbck = A.get("gbc", [D])
            xq = [A.get(f"xq{i}", [nsub, 256]) for i in range(2)]
            oq = [A.get(f"oq{i}", [nsub, 256]) for i in range(2)]
            seal(A)
            last = (l == NL - 1)
            src = (x1_s if l > 0 else xs_in) if t.sample else (x1_p if l > 0 else xp)
            dst = (y_s if last else x1_s) if t.sample else (y_p if last else x1_p)
            srck = ["x1_s" if t.sample else "x1_p"] if l > 0 else []
            dstk = [] if last else ["x1_s" if t.sample else "x1_p"]
            if t.sample:
                for (s, o, n, p0) in t.segs:
                    dma("sp", gbc[o:o + n, :], gate_d[s:s + 1, :].broadcast_to([n, D]), ["gate_d"], [gbck])
            else:
                dma("sp", gbc[:, :], gate_d[0:1, :].broadcast_to([128, D]), ["gate_d"], [gbck])
            g0 = 0 if t.sample else t.idx * NTP
            for cb in range(16):
                c0 = cb * 256
                wb, wk = load_ws(wout_s[l][cb], f"wout_s{l}_{cb}")
                xt, xk = xq[cb % 2]
                ot, ok = oq[cb % 2]
                dma("sp", xt[0:rows, :, :], src[g0:g0 + NT, c0:c0 + 256].rearrange("(s p) c -> p s c", p=rows), srck, [xk])
                for si, (r0, rws) in enumerate(t.subs):
                    b = nb()
                    for ch in range(DC):
                        mm(b.t[0:rows, 0:256], mixT[:, ch, r0:r0 + rows], wb[:, ch, :], ch == 0, ch == DC - 1, [wk, "mixT"], [b.k])
                    tt("dve", ot[0:rows, si, :], b.t[0:rows, 0:256], gbc[0:rows, c0:c0 + 256], ALU.mult, [b.k, gbck], [ok])
                    b.rel()
                tt("pool", ot[0:rows, :, :], ot[0:rows, :, :], xt[0:rows, :, :], ALU.add, [ok, xk], [ok])
                dma("pool", dst[g0:g0 + NT, c0:c0 + 256].rearrange("(s p) c -> p s c", p=rows), ot[0:rows, :, :], [ok], dstk)

        def finish_seq(l, s):
            A = new_phase("lf")
            Ts, Tsk = A.get("Ts", [8, 128])
            shs, shsk = A.get("shs", [128])
            cvs, cvsk = A.get("cvs", [2, 128])
            seal(A)
            if True:
                b = nb()
                tr(b.t[0:25, 0:128], rwprev[s][:, 0:25], ident[:, :], [f"rwprev{s}", "ident"], [b.k])
                cp("act", shs[0:25, :], b.t[0:25, 0:128], [b.k], [shsk])
                b.rel()
                dsh = sh_p[l] if s == 0 else sh_s[l, s - 1]
                dma("pool", dsh.rearrange("(c p) -> c p", p=128), shs[0:25, :], [shsk], [])
                b = nb()
                for j in range(2):
                    tr(b.t[0:8, j * 128:(j + 1) * 128], convh[s][:, :, j], ident[:, :], [f"convh{s}", "ident"], [b.k])
                cp("act", cvs[0:8, :, :], b.v3(8, 2, 128), [b.k], [cvsk])
                b.rel()
                dcv = cv_p[l] if s == 0 else cv_s[l, s - 1]
                dma("pool", dcv.rearrange("j (c p) -> c j p", p=128), cvs[0:8, :, :], [cvsk], [])
                for q in range(2):
                    b = nb()
                    for jj in range(4):
                        tr(b.t[:, jj * 128:(jj + 1) * 128], Mblk[s][:, q * 4 + jj, :], ident[:, :], [MKEY[s], "ident"], [b.k])
                    cp("act", Ts[:, q * 4:q * 4 + 4, :], b.v3(128, 4, 128), [b.k], [Tsk])
                    b.rel()
                drw = rw_p[l] if s == 0 else rw_s[l, s - 1]
                dv = drw.rearrange("(j h) v k -> h v j k", h=2)
                dma("pool", dv[0], Ts[0:64, :, 0:64], [Tsk], [])
                dma("pool", dv[1], Ts[64:128, :, 64:128], [Tsk], [])

        for l in range(NL):
            layer_setup(l)
            for t in tiles:
                if t.sample:
                    finish_seq(l, 0)
                    load_state_M(l, 1)
                phase1(l, t)
                if 2 in phases:
                    phase2(l, t)
                if 3 in phases:
                    phase3(l, t)
                if 4 in phases:
                    phase4(l, t)
                phase5(l, t)
            finish_seq(l, 1)
            finish_seq(l, 2)
        nops = {e: len(P.ops[e]) for e in ENGS}
        print("ops per engine:", nops, flush=True)
        P.emit()
    return nc


def _cols(v):
    v = np.asarray(v, np.float32).reshape(-1)
    return np.ascontiguousarray(v.reshape(-1, 128).T)


def _consts(SEQ):
    c = {}
    c["c_ident"] = np.eye(128, dtype=np.float32)
    k = np.arange(128)[:, None]
    q = np.arange(NTP)[None, :]
    matt = np.zeros((128, 4, NTP), np.float32)
    for j in range(4):
        matt[:, j, :] = ((j * 128 + k) // 64 <= q // 64)
    c["c_matt"] = matt

    def blockdiag(m):
        n = m.shape[0]
        z = np.zeros((2 * n, 2 * n), np.float32)
        z[:n, :n] = m
        z[n:, n:] = m
        return z
    for C, suf in ((64, ""), (32, "32")):
        s = np.arange(C)[:, None]
        t = np.arange(C)[None, :]
        su = (s < t).astype(np.float32)
        sl = (s > t).astype(np.float32)
        mu = (s <= t).astype(np.float32)
        c["c_msu" + suf] = blockdiag(su)
        c["c_msl" + suf] = blockdiag(sl)
        c["c_mu" + suf] = np.concatenate([mu, mu], 0)
    bo = np.zeros((128, 128), np.float32)
    bo[:64, :64] = 1
    bo[64:, 64:] = 1
    c["c_bones"] = bo
    ps = np.zeros((64, 64), np.float32)
    for i in range(64):
        ps[(i + 32) % 64, i] = 1
    c["c_pswap"] = ps
    pos = np.concatenate([np.arange(SEQ), PAST + np.arange(SS), PAST + np.arange(SS)]).astype(np.float32)
    c["c_pos"] = np.ascontiguousarray(np.broadcast_to(pos[None, :], (64, SEQ + 2 * SS)))
    c["c_fidx"] = (np.arange(64) % 32).astype(np.float32)[:, None]
    return c


def _pack_pv(inp, NL):
    pv = np.zeros((NL, 128, NPV), np.float32)
    for l in range(NL):
        def put(name, arr, rows=128):
            a = np.asarray(arr, np.float32)
            pv[l, :a.shape[0], PV[name]:PV[name] + a.shape[1]] = a
        put("norm_g", _cols(inp["norm_g"][l]))
        put("b_shift", _cols(inp["b_ada"][l][0:D]))
        put("b_scale", _cols(inp["b_ada"][l][D:2 * D]))
        put("mu", _cols(inp["rw_mu"][l]))
        put("w0", _cols(inp["rw_w0"][l]))
        put("a0", _cols(inp["rw_a0"][l]))
        put("kk", _cols(inp["rw_kk"][l]))
        put("ka", _cols(inp["rw_ka"][l]))
        put("rk", _cols(inp["rw_rk"][l].reshape(-1)))
        put("qng", _cols(inp["mla_qnorm_g"][l]))
        put("kvg", _cols(inp["mla_kvnorm_g"][l]))
        put("qn_nope", _cols(inp["mla_qn_nope"][l]))
        put("qn_rope", np.asarray(inp["mla_qn_rope"][l], np.float32)[:, None])
        put("kn_nope", _cols(inp["mla_kn_nope"][l]))
        put("kn_rope", np.asarray(inp["mla_kn_rope"][l], np.float32)[:, None])
        cw = np.concatenate([_cols(inp["conv_w"][l][j]) for j in range(3)], 1)
        put("conv_w", cw)
        put("conv_b", _cols(inp["conv_b"][l]))
        put("ln_g", _cols(inp["rw_ln_g"][l]))
        put("ln_b", _cols(inp["rw_ln_b"][l]))
    return pv


_NC_CACHE = {}
RUNNER = None


def run(inputs, SEQ=SEQ_FULL, NL=L_FULL, phases=(1, 2, 3, 4, 5), n_cores=8):
    inp = {k: np.asarray(v) for k, v in inputs.items()}
    key = (SEQ, NL, tuple(phases))
    if key not in _NC_CACHE:
        _NC_CACHE[key] = build(SEQ, NL, phases)
    nc = _NC_CACHE[key]
    consts = _consts(SEQ)
    pv = _pack_pv(inp, NL)
    shared = dict(consts)
    shared["pv"] = pv
    f32 = lambda a: np.ascontiguousarray(a, dtype=np.float32)
    shared["w_ada"] = f32(inp["w_ada"][:NL])
    shared["b_gate"] = f32(inp["b_ada"][:NL, None, 2 * D:3 * D])
    shared["w_in"] = f32(inp["w_in"][:NL])
    shared["w_out"] = f32(inp["w_out"][:NL])
    shared["w_uq"] = f32(inp["mla_w_uq"][:NL])
    shared["w_uk"] = f32(inp["mla_w_uk"][:NL])
    shared["w_uv"] = f32(inp["mla_w_uv"][:NL])
    shared["rw_w2"] = f32(inp["rw_w2"][:NL])
    shared["rw_a2"] = f32(inp["rw_a2"][:NL])
    in_maps = []
    for i in range(n_cores):
        b = i % 4
        sbs = [2 * i, 2 * i + 1]
        m = dict(shared)
        m["xp"] = f32(inp["x_prompt"][b, :SEQ])
        m["xs"] = f32(inp["x_sample"][sbs].reshape(2 * SS, D))
        cs = np.stack([inp["c_prompt"][b], inp["c_sample"][sbs[0]], inp["c_sample"][sbs[1]]], 0)
        m["cT"] = f32(cs.reshape(3, DC, 128).transpose(2, 1, 0))
        m["cache_lat"] = f32(inp["cache_mla_latent"][:NL, sbs])
        m["cache_kr"] = f32(inp["cache_mla_krope"][:NL, sbs])
        S = inp["state_rwkv"][:NL, sbs]
        Mst = S.reshape(NL, 2, 8, 2, 64, 64).transpose(0, 1, 3, 5, 2, 4)
        m["st_M"] = f32(Mst.reshape(NL, 2, 128, 8, 64))
        sh = inp["state_rwkv_shift"][:NL, sbs]
        m["st_shift"] = f32(sh.reshape(NL, 2, 25, 128).transpose(0, 1, 3, 2))
        cv = inp["state_conv"][:NL, sbs]
        m["st_conv"] = f32(cv.reshape(NL, 2, 2, 8, 128).transpose(0, 1, 4, 3, 2))
        in_maps.append(m)
    if RUNNER is not None:
        return RUNNER(nc, in_maps)
    res = run_bass_kernel_spmd(nc, in_maps, core_ids=list(range(n_cores)))
    return res.results


def kernel(**inputs):
    r = run(inputs)
    NL = L_FULL
    B = 4
    f = lambda a: np.asarray(a, dtype=np.float32)
    y_p = np.stack([f(r[b]["y_p"]) for b in range(B)], 0)
    y_s = np.concatenate([f(r[i]["y_s"]).reshape(2, SS, D) for i in range(8)], 0)
    lat_p = np.stack([f(r[b]["lat_p"]) for b in range(B)], 1)
    kr_p = np.stack([f(r[b]["kr_p"]) for b in range(B)], 1)
    rw_p = np.stack([f(r[b]["rw_p"]) for b in range(B)], 1)
    sh_p = np.stack([f(r[b]["sh_p"]) for b in range(B)], 1)
    cv_p = np.stack([f(r[b]["cv_p"]) for b in range(B)], 1)
    lat_s = np.concatenate([f(r[i]["lat_s"]).reshape(NL, 2, SS, 512) for i in range(8)], 1)
    kr_s = np.concatenate([f(r[i]["kr_s"]).reshape(NL, 2, SS, 64) for i in range(8)], 1)
    rw_s = np.concatenate([f(r[i]["rw_s"]) for i in range(8)], 1)
    sh_s = np.concatenate([f(r[i]["sh_s"]) for i in range(8)], 1)
    cv_s = np.concatenate([f(r[i]["cv_s"]) for i in range(8)], 1)
    return (y_p, y_s, lat_p, kr_p, rw_p, sh_p, cv_p, lat_s, kr_s, rw_s, sh_s, cv_s)
```

```python
import contextlib
import math
import numpy as np
import concourse.bass as bass
import concourse.mybir as mybir
from concourse.bass_utils import run_bass_kernel_spmd

F32 = mybir.dt.float32
BF16 = mybir.dt.bfloat16
I32 = mybir.dt.int32
AF = mybir.ActivationFunctionType
ALU = mybir.AluOpType
AX = mybir.AxisListType

D = 4096
DC = 32
H = 16
L_FULL = 2
SEQ_FULL = 4096
PAST = 2048
SS = 32
NTP = 512
IN_COLS = 11968
OFF = dict(rw_pre=0, rw_gate=3200, cq=4224, ckv=5248, kr=5760, mla_gate=5824,
           cv_b=7872, cv_c=8896, cv_x=9920, cv_gate=10944)
QSCALE = 192.0 ** -0.5
C0 = math.exp(-0.5)
NORM_EPS = 1e-6
GN_EPS = 64e-5

PV = {}
_c = 0
for _n, _w in (("norm_g", 32), ("b_shift", 32), ("b_scale", 32), ("mu", 25), ("w0", 8), ("a0", 8),
               ("kk", 8), ("ka", 8), ("rk", 8), ("qng", 8), ("kvg", 4), ("qn_nope", 1), ("qn_rope", 1),
               ("kn_nope", 1), ("kn_rope", 1), ("conv_w", 24), ("conv_b", 8), ("ln_g", 8), ("ln_b", 8)):
    PV[_n] = _c
    _c += _w
NPV = _c

ENGS = ("pe", "act", "dve", "pool", "sp")
EPOCH = 20000
NDMA = 8
import os as _os
SAME_ENGINE_SYNC = _os.environ.get('SES', '1') == '1'


class Op:
    __slots__ = ("eng", "fn", "deps", "is_dma", "sig", "dsem", "dval", "prevd")

    def __init__(self, eng, fn, is_dma):
        self.eng = eng
        self.fn = fn
        self.is_dma = is_dma
        self.deps = []
        self.sig = None
        self.dsem = None
        self.dval = 0
        self.prevd = None


class Prog:
    def __init__(self, nc):
        self.nc = nc
        self.ops = {e: [] for e in ENGS}
        self.lastw = {}
        self.readers = {}
        self.nsig = {e: 0 for e in ENGS}
        self.ndma = {e: 0 for e in ENGS}
        self.dma_last = {}

    def rekey(self, old_keys, new_keys):
        ops = []
        for k in old_keys:
            w = self.lastw.get(k)
            if w is not None:
                ops.append(w)
            ops.extend(self.readers.get(k, ()))
        for k in new_keys:
            self.lastw.pop(k, None)
            self.readers[k] = list(ops)

    def op(self, eng, fn, reads=(), writes=(), dma=False):
        o = Op(eng, fn, dma)
        psr = [k for k in reads if k.startswith("ps") and k not in writes]
        if psr:
            writes = list(writes) + psr
        deps = []
        for k in reads:
            w = self.lastw.get(k)
            if w is not None:
                deps.append(w)
        for k in writes:
            w = self.lastw.get(k)
            if w is not None:
                deps.append(w)
            deps.extend(self.readers.get(k, ()))
        seen = set()
        for d in deps:
            if id(d) in seen or d is o:
                continue
            seen.add(id(d))
            if d.is_dma:
                o.deps.append(d)
            elif d.eng == eng:
                if eng == "pe" or not SAME_ENGINE_SYNC:
                    continue
                o.deps.append(d)
                if d.sig is None:
                    d.sig = -1
            else:
                o.deps.append(d)
                if d.sig is None:
                    d.sig = -1
        for k in reads:
            self.readers.setdefault(k, []).append(o)
        for k in writes:
            self.lastw[k] = o
            self.readers[k] = []
        if dma:
            j = self.ndma[eng]
            self.ndma[eng] = j + 1
            slot = j % NDMA
            o.dsem = (eng, slot)
            o.dval = 16 * (j // NDMA + 1)
            o.prevd = self.dma_last.get((eng, slot))
            self.dma_last[(eng, slot)] = o
        self.ops[eng].append(o)
        return o

    def emit(self):
        nc = self.nc
        for e in ENGS:
            n = 0
            for o in self.ops[e]:
                if o.sig is not None and not o.is_dma:
                    o.sig = n
                    n += 1
            self.nsig[e] = n
        with contextlib.ExitStack() as st:
            esems = {}
            for e in ENGS:
                ne = (self.nsig[e] + EPOCH - 1) // EPOCH
                esems[e] = [st.enter_context(nc.semaphore(f"s_{e}_{i}")) for i in range(ne)]
            dsems = {}
            for e in ENGS:
                for s in range(min(NDMA, self.ndma[e])):
                    dsems[(e, s)] = st.enter_context(nc.semaphore(f"d_{e}_{s}"))
            block = st.enter_context(nc.Block())
            final_dma = dict(self.dma_last)

            def make(e):
                def body(eng):
                    seen_sig = {}
                    seen_dma = {}
                    for o in self.ops[e]:
                        waits = []
                        for d in o.deps:
                            if d.is_dma:
                                if seen_dma.get(d.dsem, 0) < d.dval:
                                    seen_dma[d.dsem] = d.dval
                                    waits.append((dsems[d.dsem], d.dval))
                            else:
                                if seen_sig.get(d.eng, -1) < d.sig:
                                    seen_sig[d.eng] = d.sig
                                    waits.append((esems[d.eng][d.sig // EPOCH], d.sig % EPOCH + 1))
                        if o.is_dma and o.prevd is not None:
                            d = o.prevd
                            if seen_dma.get(d.dsem, 0) < d.dval:
                                seen_dma[d.dsem] = d.dval
                                waits.append((dsems[d.dsem], d.dval))
                        for (s, v) in waits:
                            eng.wait_ge(s, v)
                        ins = o.fn(eng)
                        if o.is_dma:
                            ins.then_inc(dsems[o.dsem], 16)
                        elif o.sig is not None:
                            ins.then_inc(esems[e][o.sig // EPOCH], 1)
                    if e == "sp":
                        for k, d in final_dma.items():
                            if seen_dma.get(d.dsem, 0) < d.dval:
                                eng.wait_ge(dsems[d.dsem], d.dval)
                return body

            block.tensor(make("pe"))
            block.scalar(make("act"))
            block.vector(make("dve"))
            block.gpsimd(make("pool"))
            block.sync(make("sp"))


class Ctx:
    pass


def build(SEQ=SEQ_FULL, NL=L_FULL, phases=(1, 2, 3, 4, 5)):
    nc = bass.Bass("TRN2", target_bir_lowering=False)
    NPT = SEQ // NTP
    LKS = PAST + SS

    def din(name, shape, dt=F32):
        return nc.dram_tensor(name, list(shape), dt, kind="ExternalInput").ap()

    def dout(name, shape, dt=F32):
        return nc.dram_tensor(name, list(shape), dt, kind="ExternalOutput").ap()

    def dscr(name, shape, dt=F32):
        return nc.dram_tensor(name, list(shape), dt).ap()

    xp = din("xp", [SEQ, D])
    xs_in = din("xs", [2 * SS, D])
    cT_in = din("cT", [128, DC, 3])
    cache_lat = din("cache_lat", [NL, 2, PAST, 512])
    cache_kr = din("cache_kr", [NL, 2, PAST, 64])
    st_M = din("st_M", [NL, 2, 128, 8, 64])
    st_shift = din("st_shift", [NL, 2, 128, 25])
    st_conv = din("st_conv", [NL, 2, 128, 8, 2])
    w_ada = din("w_ada", [NL, D, 3 * D])
    b_gate = din("b_gate", [NL, 1, D])
    w_in = din("w_in", [NL, D, IN_COLS])
    w_out = din("w_out", [NL, D, D])
    w_uq = din("w_uq", [NL, 1024, 3072])
    w_uk = din("w_uk", [NL, 512, 2048])
    w_uv = din("w_uv", [NL, 512, 2048])
    w_w2 = din("rw_w2", [NL, 64, 1024])
    w_a2 = din("rw_a2", [NL, 64, 1024])
    pv_in = din("pv", [NL, 128, NPV])
    c_ident = din("c_ident", [128, 128])
    c_matt = din("c_matt", [128, 4, NTP])
    c_msu = din("c_msu", [128, 128])
    c_msl = din("c_msl", [128, 128])
    c_mu = din("c_mu", [128, 64])
    c_msu32 = din("c_msu32", [64, 64])
    c_msl32 = din("c_msl32", [64, 64])
    c_mu32 = din("c_mu32", [64, 32])
    c_bones = din("c_bones", [128, 128])
    c_pswap = din("c_pswap", [64, 64])
    c_pos = din("c_pos", [64, SEQ + 2 * SS])
    c_fidx = din("c_fidx", [64, 1])

    y_p = dout("y_p", [SEQ, D])
    y_s = dout("y_s", [2 * SS, D])
    lat_p = dout("lat_p", [NL, SEQ, 512])
    kr_p = dout("kr_p", [NL, SEQ, 64])
    rw_p = dout("rw_p", [NL, H, 64, 64])
    sh_p = dout("sh_p", [NL, 3200])
    cv_p = dout("cv_p", [NL, 2, 1024])
    lat_s = dout("lat_s", [NL, 2 * SS, 512])
    kr_s = dout("kr_s", [NL, 2 * SS, 64])
    rw_s = dout("rw_s", [NL, 2, H, 64, 64])
    sh_s = dout("sh_s", [NL, 2, 3200])
    cv_s = dout("cv_s", [NL, 2, 2, 1024])

    x1_p = dscr("x1_p", [SEQ, D])
    x1_s = dscr("x1_s", [2 * SS, D])
    gate_d = dscr("gate_d", [3, D])
    LK = [SEQ, LKS, LKS]
    kT_d = [[dscr(f"kT_{l}_{s}", [H, 128, LK[s]], BF16) for s in range(3)] for l in range(NL)]
    krT_d = [[dscr(f"krT_{l}_{s}", [64, LK[s]], BF16) for s in range(3)] for l in range(NL)]
    V_d = [[dscr(f"V_{l}_{s}", [LK[s], 2048], BF16) for s in range(3)] for l in range(NL)]

    def win_blocks():
        out = []
        for cc2 in range(4):
            for g in ("cv_c", "cv_x", "cv_b", "cv_gate"):
                out.append((OFF[g] + cc2 * 256, 256))
        for i in range(12):
            out.append((i * 256, 256))
        out.append((3072, 128))
        for i in range(4):
            out.append((OFF["cq"] + i * 256, 256))
        for i in range(2):
            out.append((OFF["ckv"] + i * 256, 256))
        out.append((OFF["kr"], 64))
        for h in range(0, H, 2):
            out.append((OFF["mla_gate"] + h * 128, 256))
        for i in range(4):
            out.append((OFF["rw_gate"] + i * 256, 256))
        return out
    WINB = win_blocks()
    WIN_IDX = {b: i for i, b in enumerate(WINB)}
    win_s = [dscr(f"win_s{l}", [len(WINB), 128, DC * 256], BF16) for l in range(NL)]
    wout_s = [dscr(f"wout_s{l}", [16, 128, DC * 256], BF16) for l in range(NL)]
    wuq_s = [dscr(f"wuq_s{l}", [H, 128, 8 * 192], BF16) for l in range(NL)]
    wuk_s = [dscr(f"wuk_s{l}", [128, 4 * 2048], BF16) for l in range(NL)]
    wuv_s = [dscr(f"wuv_s{l}", [4, 128, 4 * 512], BF16) for l in range(NL)]

    st = contextlib.ExitStack()
    with st:
        def sb(name, shape, dt=F32):
            return st.enter_context(nc.sbuf_tensor("sb_" + name, list(shape), dt))

        P = Prog(nc)

        def _op(eng, fn, r, w, dma=False):
            return P.op(eng, fn, reads=r, writes=w, dma=dma)

        def mm(out, lhsT, rhs, start, stop, r, w):
            _op("pe", lambda e: e.matmul(out, lhsT=lhsT, rhs=rhs, start=start, stop=stop), r, w)

        def tr(out, in_, ident_ap, r, w):
            _op("pe", lambda e: e.transpose(out, in_, ident_ap), r, w)

        def act(out, in_, func, r, w, bias=None, scale=None, accum_out=None):
            kw = {}
            if bias is not None:
                kw["bias"] = bias
            if scale is not None:
                kw["scale"] = scale
            if accum_out is not None:
                kw["accum_out"] = accum_out
            _op("act", lambda e: e.activation(out=out, in_=in_, func=func, **kw), r, w)

        def ts(eng, out, in0, s1, s2, op0, op1, r, w):
            if op1 is None:
                _op(eng, lambda e: e.tensor_scalar(out=out, in0=in0, scalar1=s1, scalar2=None, op0=op0), r, w)
            else:
                _op(eng, lambda e: e.tensor_scalar(out=out, in0=in0, scalar1=s1, scalar2=s2, op0=op0, op1=op1), r, w)

        def tt(eng, out, in0, in1, op, r, w):
            _op(eng, lambda e: e.tensor_tensor(out=out, in0=in0, in1=in1, op=op), r, w)

        def stt(out, in0, scalar, in1, op0, op1, r, w):
            _op("dve", lambda e: e.scalar_tensor_tensor(out=out, in0=in0, scalar=scalar, in1=in1, op0=op0, op1=op1), r, w)

        def cp(eng, out, in_, r, w):
            if eng == "act":
                _op("act", lambda e: e.activation(out=out, in_=in_, func=AF.Copy), r, w)
            else:
                _op(eng, lambda e: e.tensor_copy(out=out, in_=in_), r, w)

        def memset(eng, ap, val, w):
            _op(eng, lambda e: e.memset(ap, val), (), w)

        def dma(eng, out, in_, r, w):
            _op(eng, lambda e: e.dma_start(out=out, in_=in_), r, w, dma=True)

        def red(out, in_, r, w):
            _op("dve", lambda e: e.tensor_reduce(out=out, in_=in_, axis=AX.X, op=ALU.add), r, w)

        def recip(out, in_, r, w):
            _op("dve", lambda e: e.reciprocal(out=out, in_=in_), r, w)

        banks = [st.enter_context(nc.psum_tensor(f"ps{i}", [128, 512], F32)) for i in range(8)]
        NROT = 6
        rot = [0]
        busy = [False] * 8

        class Bank:
            def __init__(self, i):
                self.i = i
                self.t = banks[i]
                self.k = f"ps{i}"

            def rel(self):
                busy[self.i] = False

            def v3(self, rows, a, b):
                return self.t[0:rows, 0:a * b].rearrange("p (a b) -> p a b", a=a)

        def nb():
            i = rot[0] % NROT
            rot[0] += 1
            assert not busy[i], f"psum bank {i} still busy"
            busy[i] = True
            return Bank(i)

        ACC0 = Bank(6)
        ACC1 = Bank(7)

        evq = [0]

        def ev_eng():
            evq[0] += 1
            return "act" if evq[0] % 2 else "dve"

        hT = sb("hT", [128, DC, NTP], BF16)
        mixT = sb("mixT", [128, DC, NTP], BF16)
        wbuf = [sb(f"wbuf{i}", [128, DC, 256], BF16) for i in range(2)]
        wq = [0]
        ident = sb("ident", [128, 128])
        onesb = sb("onesb", [128, 128], BF16)
        onesf = sb("onesf", [128, 128])
        bones = sb("bones", [128, 128])
        msu = sb("msu", [128, 128]); msl = sb("msl", [128, 128]); mup = sb("mup", [128, 64]); mun = sb("mun", [128, 64])
        msu32 = sb("msu32", [64, 64]); msl32 = sb("msl32", [64, 64]); mup32 = sb("mup32", [64, 32]); mun32 = sb("mun32", [64, 32])
        pswap = sb("pswap", [64, 64])
        fidx = sb("fidx", [64, 1]); freq = sb("freq", [64, 1])
        rmask64 = sb("rmask64", [128, 512]); rmask32 = sb("rmask32", [128, 256])
        pv = sb("pv", [128, NL, NPV])
        omka = sb("omka", [128, NL, 8])
        gq = sb("gq", [128, NL, 2])
        cTb = sb("cTb", [128, DC, 3], BF16)
        modT = sb("modT", [128, 64, 3])
        s1T = sb("s1T", [128, DC, 3])
        MblkA = sb("MblkA", [128, 8, 128]); MblkB = sb("MblkB", [128, 8, 128])
        Mblk = [MblkA, MblkA, MblkB]
        MKEY = ["Mblk0", "Mblk0", "Mblk2"]
        rwprev = [sb(f"rwprev{s}", [128, 25]) for s in range(3)]
        convh = [sb(f"convh{s}", [128, 8, 2]) for s in range(3)]
        ARENA_W = 23744
        arena = sb("arena", [128, ARENA_W])

        class Carver:
            def __init__(self, tag, prev):
                self.off = 0
                self.tag = tag
                self.keys = []
                self._prev = prev

            def get(self, name, free_shape, dt=F32):
                n = int(np.prod(free_shape))
                words = (n + 1) // 2 if dt == BF16 else n
                words = (words + 15) // 16 * 16
                assert self.off + words <= ARENA_W, f"arena overflow {self.tag}:{name} {self.off + words}"
                v = arena[:, self.off:self.off + words]
                self.off += words
                if dt == BF16:
                    v = v.bitcast(BF16)
                elif dt == I32:
                    v = v.bitcast(I32)
                v = v[:, 0:n]
                if len(free_shape) == 2:
                    v = v.rearrange("p (a b) -> p a b", a=free_shape[0])
                elif len(free_shape) == 3:
                    v = v.rearrange("p (a b c) -> p a b c", a=free_shape[0], b=free_shape[1])
                k = f"{self.tag}:{name}"
                self.keys.append(k)
                return v, k

        prev_keys = [[]]
        phase_ctr = [0]

        def new_phase(tag):
            phase_ctr[0] += 1
            return Carver(f"{tag}_{phase_ctr[0]}", list(prev_keys[0]))

        def seal(c):
            P.rekey(c._prev, c.keys)
            prev_keys[0] = list(c.keys)

        dma("sp", ident[:], c_ident, (), ["ident"])
        dma("sp", bones[:], c_bones, (), ["bones"])
        dma("sp", msu[:], c_msu, (), ["msu"]); dma("sp", msl[:], c_msl, (), ["msl"]); dma("sp", mup[:], c_mu, (), ["mup"])
        dma("sp", msu32[:], c_msu32, (), ["msu32"]); dma("sp", msl32[:], c_msl32, (), ["msl32"]); dma("sp", mup32[:], c_mu32, (), ["mup32"])
        dma("sp", pswap[:], c_pswap, (), ["pswap"])
        dma("sp", fidx[:], c_fidx, (), ["fidx"])
        dma("sp", pv[:], pv_in.rearrange("l p n -> p l n"), (), ["pv"])
        dma("pool", cTb[:], cT_in, (), ["cTb"])
        memset("pool", onesb[:], 1.0, ["onesb"])
        if len(phases) < 5:
            memset("pool", mixT[:], 0.0, ["mixT"])
        memset("pool", onesf[:], 1.0, ["onesf"])
        memset("pool", rmask64[:], 1.0, ["rmask64"])
        memset("pool", rmask32[:], 1.0, ["rmask32"])
        _op("pool", lambda e: e.memset(rmask64[:].rearrange("p (a c) -> p a c", c=64)[:, :, 0:1], 0.0), ["rmask64"], ["rmask64"])
        _op("pool", lambda e: e.memset(rmask32[:].rearrange("p (a c) -> p a c", c=32)[:, :, 0:1], 0.0), ["rmask32"], ["rmask32"])
        ts("dve", mun[:], mup[:], -1.0, None, ALU.mult, None, ["mup"], ["mun"])
        ts("dve", mun32[:], mup32[:], -1.0, None, ALU.mult, None, ["mup32"], ["mun32"])
        act(freq[:], fidx[:], AF.Exp, ["fidx"], ["freq"], scale=-math.log(10000.0) / 32.0)
        for l in range(NL):
            ts("dve", omka[:, l, :], pv[:, l, PV["ka"]:PV["ka"] + 8], -1.0, 1.0, ALU.mult, ALU.add, ["pv"], ["omka"])
            ts("dve", gq[:, l, 0:1], pv[:, l, PV["qn_nope"]:PV["qn_nope"] + 1], QSCALE, None, ALU.mult, None, ["pv"], ["gq"])
            ts("dve", gq[:, l, 1:2], pv[:, l, PV["qn_rope"]:PV["qn_rope"] + 1], QSCALE, None, ALU.mult, None, ["pv"], ["gq"])

        def convert_weights(l):
            for i, (c, bw) in enumerate(WINB):
                dma("pool", win_s[l][i][:, 0:DC * bw].rearrange("p (c n) -> p c n", c=DC),
                    w_in[l][:, c:c + bw].rearrange("(c p) n -> p c n", p=128), (), [f"win_s{l}_{i}"])
            for i in range(16):
                dma("pool", wout_s[l][i].rearrange("p (c n) -> p c n", c=DC),
                    w_out[l][:, i * 256:(i + 1) * 256].rearrange("(c p) n -> p c n", p=128), (), [f"wout_s{l}_{i}"])
            for h in range(H):
                dma("pool", wuq_s[l][h].rearrange("p (c n) -> p c n", c=8),
                    w_uq[l][:, h * 192:(h + 1) * 192].rearrange("(c p) n -> p c n", p=128), (), [f"wuq_s{l}"])
            dma("pool", wuk_s[l].rearrange("p (c n) -> p c n", c=4), w_uk[l].rearrange("(c p) n -> p c n", p=128), (), [f"wuk_s{l}"])
            for i in range(4):
                dma("pool", wuv_s[l][i].rearrange("p (c n) -> p c n", c=4),
                    w_uv[l][:, i * 512:(i + 1) * 512].rearrange("(c p) n -> p c n", p=128), (), [f"wuv_s{l}"])

        def pvc(l, name, j=0, n=1, rows=128):
            c0 = PV[name] + j
            return pv[0:rows, l, c0:c0 + n]

        def pvB(l, name, j, n, C, rows=128, p0=0):
            c0 = PV[name] + j
            return pv[p0:p0 + rows, l, c0:c0 + n].unsqueeze(2).broadcast_to([rows, n, C])

        for l in range(NL):
            convert_weights(l)

        def load_w(src_ap, nch=DC, ncols=256):
            i = wq[0] % 2
            wq[0] += 1
            wb = wbuf[i]
            dma("pool", wb[:, 0:nch, 0:ncols], src_ap.rearrange("(c p) n -> p c n", p=128), (), [f"wbuf{i}"])
            return wb, f"wbuf{i}"

        def load_ws(scr_ap, scr_key, ncols=256):
            i = wq[0] % 2
            wq[0] += 1
            wb = wbuf[i]
            dma("sp", wb[:, 0:DC, 0:ncols], scr_ap[:, 0:DC * ncols].rearrange("p (c n) -> p c n", c=DC), [scr_key], [f"wbuf{i}"])
            return wb, f"wbuf{i}"

        class Tile:
            pass

        tiles = []
        for i in range(NPT):
            t = Tile()
            t.NT = NTP; t.C = 64; t.sample = False; t.idx = i
            t.segs = [(0, 0, NTP, i * NTP)]
            t.subs = [(j * 128, 128) for j in range(NTP // 128)]
            t.pos0 = i * NTP
            tiles.append(t)
        t = Tile()
        t.NT = 2 * SS; t.C = 32; t.sample = True; t.idx = NPT
        t.segs = [(1, 0, SS, PAST), (2, SS, SS, PAST)]
        t.subs = [(0, 2 * SS)]
        t.pos0 = SEQ
        tiles.append(t)

        def layer_setup(l):
            A = new_phase("ls")
            grow, growk = A.get("grow", [D])
            bgrow, bgrowk = A.get("bgrow", [D])
            seal(A)
            dma("sp", bgrow[0:3, :], b_gate[l].broadcast_to([3, D]), (), [bgrowk])
            for cb in range(48):
                wb, wk = load_w(w_ada[l][:, cb * 256:(cb + 1) * 256])
                if cb < 32:
                    for half in range(2):
                        cc = cb * 2 + half
                        b = nb()
                        for ch in range(DC):
                            mm(b.t[:, 0:3], wb[:, ch, half * 128:(half + 1) * 128], cTb[:, ch, :], ch == 0, ch == DC - 1,
                               [wk, "cTb"], [b.k])
                        bname = "b_shift" if cc < 32 else "b_scale"
                        ts("dve", modT[:, cc, :], b.t[:, 0:3], pvc(l, bname, cc % 32), None, ALU.add, None,
                           [b.k, "pv"], ["modT"])
                        b.rel()
                else:
                    g0 = (cb - 32) * 256
                    b = nb()
                    for ch in range(DC):
                        mm(b.t[0:3, 0:256], cTb[:, ch, :], wb[:, ch, :], ch == 0, ch == DC - 1, [wk, "cTb"], [b.k])
                    tt("dve", grow[0:3, g0:g0 + 256], b.t[0:3, 0:256], bgrow[0:3, g0:g0 + 256], ALU.add, [b.k, bgrowk], [growk])
                    b.rel()
            dma("pool", gate_d, grow[0:3, :], [growk], ["gate_d"])
            ts("dve", s1T[:], modT[:, 32:64, :], 1.0, None, ALU.add, None, ["modT"], ["s1T"])
            tt("dve", s1T[:], s1T[:], pvB(l, "norm_g", 0, 32, 3), ALU.mult, ["s1T", "pv"], ["s1T"])
            for s in range(3):
                if s == 0:
                    memset("pool", Mblk[0][:], 0.0, [MKEY[0]])
                    memset("pool", rwprev[0][:], 0.0, ["rwprev0"])
                    memset("pool", convh[0][:], 0.0, ["convh0"])
                else:
                    if s == 2:
                        load_state_M(l, s)
                    dma("sp", rwprev[s][:], st_shift[l, s - 1], (), [f"rwprev{s}"])
                    dma("sp", convh[s][:], st_conv[l, s - 1], (), [f"convh{s}"])

        def load_state_M(l, s):
            memset("pool", Mblk[s][:], 0.0, [MKEY[s]])
            dma("sp", Mblk[s][0:64, :, 0:64], st_M[l, s - 1, 0:64], [MKEY[s]], [MKEY[s]])
            dma("sp", Mblk[s][64:128, :, 64:128], st_M[l, s - 1, 64:128], [MKEY[s]], [MKEY[s]])

        def in_proj_fm(l, t, col0, ncols, consume):
            NT = t.NT
            j = 0
            c = col0
            while c < col0 + ncols:
                bw = min(256, col0 + ncols - c)
                bi = WIN_IDX[(c, bw)]
                wb, wk = load_ws(win_s[l][bi], f"win_s{l}_{bi}", bw)
                for h0 in range(0, bw, 128):
                    wd = min(128, bw - h0)
                    b = nb()
                    for ch in range(DC):
                        mm(b.t[0:wd, 0:NT], wb[:, ch, h0:h0 + wd], hT[:, ch, 0:NT], ch == 0, ch == DC - 1,
                           [wk, "hT"], [b.k])
                    consume(j, wd, b)
                    b.rel()
                    j += 1
                c += bw

        def phase1(l, t):
            A = new_phase("p1")
            xb = [A.get(f"x{i}", [D]) for i in range(2)]
            junk, junkk = A.get("junk", [D], BF16)
            ssq, ssqk = A.get("ssq", [4])
            rstd, rstdk = A.get("rstd", [4])
            seal(A)
            src = (x1_s if l > 0 else xs_in) if t.sample else (x1_p if l > 0 else xp)
            srck = ["x1_s" if t.sample else "x1_p"] if l > 0 else []
            for si, (r0, rows) in enumerate(t.subs):
                xt, xk = xb[si % 2]
                g0 = r0 if t.sample else t.idx * NTP + r0
                dma("sp", xt[0:rows, :], src[g0:g0 + rows, :], srck, [xk])
                act(junk[0:rows, :], xt[0:rows, :], AF.Square, [xk], [junkk, ssqk], accum_out=ssq[0:rows, si:si + 1])
                ts("dve", rstd[0:rows, si:si + 1], ssq[0:rows, si:si + 1], 1.0 / D, NORM_EPS, ALU.mult, ALU.add, [ssqk], [rstdk])
                act(rstd[0:rows, si:si + 1], rstd[0:rows, si:si + 1], AF.Sqrt, [rstdk], [rstdk])
                recip(rstd[0:rows, si:si + 1], rstd[0:rows, si:si + 1], [rstdk], [rstdk])
                ts("dve", xt[0:rows, :], xt[0:rows, :], rstd[0:rows, si:si + 1], None, ALU.mult, None, [xk, rstdk], [xk])
                for c4 in range(DC // 4):
                    b = nb()
                    for q in range(4):
                        ch = c4 * 4 + q
                        tr(b.t[:, q * 128:q * 128 + rows], xt[0:rows, ch * 128:(ch + 1) * 128], ident[0:rows, 0:rows],
                           [xk, "ident"], [b.k])
                    e = ev_eng()
                    for q in range(4):
                        ch = c4 * 4 + q
                        for (s, o, n, p0) in t.segs:
                            lo = max(o, r0); hi = min(o + n, r0 + rows)
                            if lo >= hi:
                                continue
                            src_ps = b.t[:, q * 128 + lo - r0:q * 128 + hi - r0]
                            if e == "act":
                                act(hT[:, ch, lo:hi], src_ps, AF.Identity, [b.k, "s1T", "modT"], ["hT"],
                                    scale=s1T[:, ch, s:s + 1], bias=modT[:, ch, s:s + 1])
                            else:
                                ts("dve", hT[:, ch, lo:hi], src_ps, s1T[:, ch, s:s + 1], modT[:, ch, s:s + 1], ALU.mult, ALU.add,
                                   [b.k, "s1T", "modT"], ["hT"])
                    b.rel()

        def phase2(l, t):
            NT = t.NT
            A = new_phase("p2")
            nseg = len(t.segs)
            Ls = t.segs[0][2]
            csb = [A.get(f"csb{i}", [NT]) for i in range(2)]
            ub = [A.get(f"ub{i}", [nseg, Ls + 2]) for i in range(2)]
            yb = [A.get(f"yb{i}", [NT]) for i in range(2)]
            sg = [A.get(f"sg{i}", [NT]) for i in range(2)]
            seal(A)
            cw = PV["conv_w"]
            for cc2 in range(4):
                def cons_c(j, wd, b):
                    cp("act", csb[j][0][:, :], b.t[:, 0:NT], [b.k], [csb[j][1]])
                in_proj_fm(l, t, OFF["cv_c"] + cc2 * 256, 256, cons_c)

                def cons_x(j, wd, b):
                    cc = cc2 * 2 + j
                    u, uk = ub[j]
                    y, yk = yb[j]
                    for si, (s, o, n, p0) in enumerate(t.segs):
                        cp("pool", u[:, si, 0:2], convh[s][:, cc, :], [f"convh{s}"], [uk])
                        tt("dve", u[:, si, 2:2 + n], csb[j][0][:, o:o + n], b.t[:, o:o + n], ALU.mult, [b.k, csb[j][1], uk], [uk])
                        cp("pool", convh[s][:, cc, :], u[:, si, n:n + 2], [uk], [f"convh{s}"])
                        ts("dve", y[:, o:o + n], u[:, si, 2:2 + n], pv[:, l, cw + 16 + cc:cw + 17 + cc], pvc(l, "conv_b", cc), ALU.mult, ALU.add,
                           [uk, "pv"], [yk])
                        stt(y[:, o:o + n], u[:, si, 1:1 + n], pv[:, l, cw + 8 + cc:cw + 9 + cc], y[:, o:o + n], ALU.mult, ALU.add, [uk, yk, "pv"], [yk])
                        stt(y[:, o:o + n], u[:, si, 0:n], pv[:, l, cw + cc:cw + 1 + cc], y[:, o:o + n], ALU.mult, ALU.add, [uk, yk, "pv"], [yk])
                in_proj_fm(l, t, OFF["cv_x"] + cc2 * 256, 256, cons_x)

                def cons_b(j, wd, b):
                    y, yk = yb[j]
                    tt("dve", y[:, 0:NT], y[:, 0:NT], b.t[:, 0:NT], ALU.mult, [b.k, yk], [yk])
                in_proj_fm(l, t, OFF["cv_b"] + cc2 * 256, 256, cons_b)

                def cons_g(j, wd, b):
                    cc = cc2 * 2 + j
                    y, yk = yb[j]
                    act(sg[j][0][:, :], b.t[:, 0:NT], AF.Silu, [b.k], [sg[j][1]])
                    tt("pool", mixT[:, 24 + cc, 0:NT], y[:, 0:NT], sg[j][0][:, :], ALU.mult, [yk, sg[j][1]], ["mixT"])
                in_proj_fm(l, t, OFF["cv_gate"] + cc2 * 256, 256, cons_g)

        def phase3(l, t):
            NT, C = t.NT, t.C
            W = 2 * C
            nseg = len(t.segs)
            Ls = t.segs[0][2]
            nchk = Ls // C
            nlev = int(round(math.log2(C))) - 1
            MSU, MSL, MUP, MUN = (msu, msl, mup, mun) if C == 64 else (msu32, msl32, mup32, mun32)
            mkeys = ["msu", "msl", "mup", "mun"] if C == 64 else ["msu32", "msl32", "mup32", "mun32"]
            rmask, rmk = (rmask64, "rmask64") if C == 64 else (rmask32, "rmask32")
            A = new_phase("p3")
            pre, prek = A.get("pre", [25, nseg, Ls + 2], BF16)
            w2a2, w2k = A.get("w2a2", [1024])
            x24, x24k = A.get("x24", [C])
            sig, sigk = A.get("sig", [8, C])
            a_, ak = A.get("a", [8, C])
            cs, csk = A.get("cs", [8, C])
            G, Gk = A.get("G", [8, C])
            iG, iGk = A.get("iG", [8, C])
            Gp, Gpk = A.get("Gp", [8, C])
            xr, xrk = A.get("xr", [4, C]); xk_, xkk = A.get("xk", [4, C]); xv, xvk = A.get("xv", [4, C])
            kkn, kknk = A.get("kkn", [4, C]); kmod, kmodk = A.get("kmod", [4, C]); rt, rtk = A.get("rt", [4, C])
            t1, t1k = A.get("t1", [4, C]); t2, t2k = A.get("t2", [4, C]); bon, bonk = A.get("bon", [4, C])
            blk = {}
            for nm in ("kap", "bt", "kt", "kh", "bh", "vb"):
                blk[nm] = A.get("blk_" + nm, [4, 2, C])
            prod = {}
            for nm in ("Vb", "Kt", "Bt", "X0", "A0", "A2t", "X1", "A1", "Q"):
                prod[nm] = A.get("pr_" + nm, [4, 128])
            A2R, A2Rk = A.get("A2R", [4, C]); ARn, ARnk = A.get("ARn", [4, C])
            Oq, Oqk = A.get("Oq", [512]); cen, cenk = A.get("cen", [512]); sqv, sqvk = Oq, Oqk
            st8, st8k = A.get("st8", [16])
            sgt, sgtk = cen, cenk
            seal(A)
            import os
            P3 = int(os.environ.get("P3STOP", "99"))
            if P3 <= -3:
                return
            dma("sp", w2a2[0:64, :], w_w2[l], (), [w2k])
            dma("sp", w2a2[64:128, :], w_a2[l], (), [w2k])
            if P3 <= -2:
                return
            for nm in blk:
                memset("pool", blk[nm][0][:], 0.0, [blk[nm][1]])
            if P3 <= -1:
                return

            def cons_pre(j, wd, b):
                for si, (s, o, n, p0) in enumerate(t.segs):
                    CPF = int(os.environ.get("CPF", "7"))
                    if CPF & 1:
                        cp("pool", pre[:, j, si, 1:2], rwprev[s][:, j:j + 1], [f"rwprev{s}"], [prek])
                    if CPF & 2:
                        cp(ev_eng(), pre[:, j, si, 2:2 + n], b.t[:, o:o + n], [b.k], [prek])
                    if CPF & 4:
                        cp("dve", rwprev[s][:, j:j + 1], b.t[:, o + n - 1:o + n], [b.k], [f"rwprev{s}"])
            in_proj_fm(l, t, OFF["rw_pre"], 3200, cons_pre)
            if P3 <= 0:
                return

            def lerp(eng, out, j0, nj, si, c, outk):
                cur = pre[:, j0:j0 + nj, si, 2 + c * C:2 + (c + 1) * C]
                prv = pre[:, j0:j0 + nj, si, 1 + c * C:1 + (c + 1) * C]
                tt(eng, out, prv, cur, ALU.subtract, [prek], [outk])
                tt(eng, out, out, pvB(l, "mu", j0, nj, C), ALU.mult, [outk, "pv"], [outk])
                tt(eng, out, out, cur, ALU.add, [outk, prek], [outk])

            def v3(ap2d, rows, a, b_):
                return ap2d[0:rows, 0:a * b_].rearrange("p (a b) -> p a b", a=a)

            for si, (s, o, n, p0) in enumerate(t.segs):
                Mk = MKEY[s]
                M = Mblk[s]
                for c in range(nchk):
                    tok0 = o + c * C
                    lerp("pool", x24.unsqueeze(1), 24, 1, si, c, x24k)
                    act(x24[0:64, :], x24[0:64, :], AF.Tanh, [x24k], [x24k])
                    bw = nb(); ba = nb()
                    for j in range(8):
                        mm(bw.t[:, j * C:(j + 1) * C], w2a2[0:64, j * 128:(j + 1) * 128], x24[0:64, :], True, True, [w2k, x24k], [bw.k])
                    for j in range(8):
                        mm(ba.t[:, j * C:(j + 1) * C], w2a2[64:128, j * 128:(j + 1) * 128], x24[64:128, :], True, True, [w2k, x24k], [ba.k])
                    tt("dve", sig[:], bw.v3(128, 8, C), pvB(l, "w0", 0, 8, C), ALU.add, [bw.k, "pv"], [sigk])
                    bw.rel()
                    act(sig[:], sig[:], AF.Sigmoid, [sigk], [sigk])
                    tt("dve", a_[:], ba.v3(128, 8, C), pvB(l, "a0", 0, 8, C), ALU.add, [ba.k, "pv"], [ak])
                    ba.rel()
                    act(a_[:], a_[:], AF.Sigmoid, [ak], [ak])
                    sflat = sig[:].rearrange("p a b -> p (a b)")
                    cflat = cs[:].rearrange("p a b -> p (a b)")
                    _op("dve", lambda e, cflat=cflat, sflat=sflat: e.tensor_tensor_scan(out=cflat, data0=rmask[:, 0:8 * C], data1=sflat,
                                                                                       initial=0.0, op0=ALU.mult, op1=ALU.add),
                        [rmk, sigk], [csk])
                    act(G[:], cs[:], AF.Exp, [csk], [Gk], scale=-C0)
                    act(iG[:], cs[:], AF.Exp, [csk], [iGk], scale=C0)
                    tt("pool", Gp[:], cs[:], sig[:], ALU.subtract, [csk, sigk], [Gpk])
                    act(Gp[:], Gp[:], AF.Exp, [Gpk], [Gpk], scale=-C0)
                    if P3 <= 1:
                        continue
                    for q in range(2):
                        j0 = 4 * q
                        Sq = slice(j0, j0 + 4)
                        lerp("pool", xr[:], j0, 4, si, c, xrk)
                        lerp("pool", xk_[:], 8 + j0, 4, si, c, xkk)
                        lerp("pool", xv[:], 16 + j0, 4, si, c, xvk)
                        tt("dve", t1[:], xk_[:], pvB(l, "kk", j0, 4, C), ALU.mult, [xkk, "pv"], [t1k])
                        act(t2[:], t1[:], AF.Square, [t1k], [t2k])
                        bs = nb()
                        for jj in range(4):
                            mm(bs.t[:, jj * C:(jj + 1) * C], bones[:, :], t2[:, jj, :], True, True, ["bones", t2k], [bs.k])
                        ts("dve", t2[:], bs.v3(128, 4, C), 1e-12, None, ALU.add, None, [bs.k], [t2k])
                        bs.rel()
                        act(t2[:], t2[:], AF.Sqrt, [t2k], [t2k])
                        recip(t2[:], t2[:], [t2k], [t2k])
                        tt("dve", kkn[:], t1[:], t2[:], ALU.mult, [t1k, t2k], [kknk])
                        tt("dve", t1[:], a_[:, Sq, :], pvB(l, "ka", j0, 4, C), ALU.mult, [ak, "pv"], [t1k])
                        tt("dve", t1[:], t1[:], omka[:, l, j0:j0 + 4].unsqueeze(2).broadcast_to([128, 4, C]), ALU.add, [t1k, "omka"], [t1k])
                        tt("dve", kmod[:], xk_[:], t1[:], ALU.mult, [xkk, t1k], [kmodk])
                        tt("pool", t1[:], xr[:], kmod[:], ALU.mult, [xrk, kmodk], [t1k])
                        tt("pool", t1[:], t1[:], pvB(l, "rk", j0, 4, C), ALU.mult, [t1k, "pv"], [t1k])
                        br = nb()
                        for jj in range(4):
                            mm(br.t[:, jj * C:(jj + 1) * C], bones[:, :], t1[:, jj, :], True, True, ["bones", t1k], [br.k])
                        tt("dve", bon[:], br.v3(128, 4, C), xv[:], ALU.mult, [br.k, xvk], [bonk])
                        br.rel()
                        tt("pool", bon[:], bon[:], pvB(l, "ln_b", j0, 4, C), ALU.add, [bonk, "pv"], [bonk])
                        if P3 <= 2:
                            continue
                        tt("pool", rt[:], xr[:], G[:, Sq, :], ALU.mult, [xrk, Gk], [rtk])
                        tt("pool", t1[:], kkn[:], a_[:, Sq, :], ALU.mult, [kknk, ak], [t1k])
                        for hp in range(2):
                            ps_ = slice(64 * hp, 64 * hp + 64)
                            GC = G[ps_, Sq, C - 1:C].broadcast_to([64, 4, C])
                            e1 = "dve" if hp == 0 else "pool"
                            tt(e1, blk["kap"][0][ps_, :, hp, :], kkn[ps_], Gp[ps_, Sq, :], ALU.mult, [kknk, Gpk], [blk["kap"][1]])
                            tt(e1, blk["bt"][0][ps_, :, hp, :], t1[ps_], iG[ps_, Sq, :], ALU.mult, [t1k, iGk], [blk["bt"][1]])
                            tt(e1, blk["kt"][0][ps_, :, hp, :], kmod[ps_], iG[ps_, Sq, :], ALU.mult, [kmodk, iGk], [blk["kt"][1]])
                            tt(e1, blk["kh"][0][ps_, :, hp, :], blk["kt"][0][ps_, :, hp, :], GC, ALU.mult, [blk["kt"][1], Gk], [blk["kh"][1]])
                            tt(e1, blk["bh"][0][ps_, :, hp, :], blk["bt"][0][ps_, :, hp, :], GC, ALU.mult, [blk["bt"][1], Gk], [blk["bh"][1]])
                            cp(e1, blk["vb"][0][ps_, :, hp, :], xv[ps_], [xvk], [blk["vb"][1]])

                        def B(nm, jj):
                            return blk[nm][0][:, jj, :, :].rearrange("p a b -> p (a b)")

                        def PR(nm, jj, rows=W):
                            return prod[nm][0][0:rows, jj, 0:128]

                        def PRW(nm, jj):
                            return prod[nm][0][0:W, jj, 0:W]

                        if P3 <= 3:
                            continue
                        for nm_in, nm_out, neg in (("vb", "Vb", False), ("kh", "Kt", False), ("bh", "Bt", True)):
                            b = nb()
                            for jj in range(4):
                                tr(b.t[0:W, jj * 128:(jj + 1) * 128], B(nm_in, jj), ident[:, :], [blk[nm_in][1], "ident"], [b.k])
                            dst = prod[nm_out][0][0:W, :, :]
                            if neg:
                                ts("dve", dst, b.v3(W, 4, 128), -1.0, None, ALU.mult, None, [b.k], [prod[nm_out][1]])
                            else:
                                cp("act", dst, b.v3(W, 4, 128), [b.k], [prod[nm_out][1]])
                            b.rel()
                        for (lh, rh, outnm, mask, mk) in (("bt", "kap", "X0", MSU, mkeys[0]), ("kap", "bt", "A0", MSL, mkeys[1]),
                                                          ("kt", "kap", "A2t", MSU, mkeys[0])):
                            b = nb()
                            for jj in range(4):
                                mm(b.t[0:W, jj * W:(jj + 1) * W], B(lh, jj), B(rh, jj), True, True, [blk[lh][1], blk[rh][1]], [b.k])
                            tt("dve", prod[outnm][0][0:W, :, 0:W], b.v3(W, 4, W), mask[0:W, 0:W].unsqueeze(1).broadcast_to([W, 4, W]), ALU.mult,
                               [b.k, mk], [prod[outnm][1]])
                            b.rel()
                        for (lh, out_ap, outk, mask, mk) in (("kt", A2R, A2Rk, MUP, mkeys[2]), ("bt", ARn, ARnk, MUN, mkeys[3])):
                            b = nb()
                            for jj in range(4):
                                mm(b.t[0:W, jj * C:(jj + 1) * C], B(lh, jj), rt[:, jj, :], True, True, [blk[lh][1], rtk], [b.k])
                            tt("dve", out_ap[0:W, :, :], b.v3(W, 4, C), mask[0:W, 0:C].unsqueeze(1).broadcast_to([W, 4, C]), ALU.mult,
                               [b.k, mk], [outk])
                            b.rel()
                        if P3 <= 4:
                            continue
                        tt("pool", prod["Q"][0][0:W, :, 0:W], ident[0:W, 0:W].unsqueeze(1).broadcast_to([W, 4, W]), prod["X0"][0][0:W, :, 0:W],
                           ALU.subtract, ["ident", prod["X0"][1]], [prod["Q"][1]])
                        cur = ("X0", "A0"); nxt = ("X1", "A1")
                        for lev in range(nlev):
                            last = (lev == nlev - 1)
                            if not last:
                                b1 = nb()
                                for jj in range(4):
                                    mm(b1.t[0:W, jj * W:(jj + 1) * W], PRW(cur[1], jj), PRW(cur[0], jj), True, True,
                                       [prod[cur[0]][1], prod[cur[1]][1]], [b1.k])
                            b2 = nb()
                            for jj in range(4):
                                mm(b2.t[0:W, jj * W:(jj + 1) * W], PRW(cur[0], jj), PRW(cur[1], jj), True, True,
                                   [prod[cur[0]][1], prod[cur[1]][1]], [b2.k])
                            if not last:
                                cp("act", prod[nxt[0]][0][0:W, :, 0:W], b1.v3(W, 4, W), [b1.k], [prod[nxt[0]][1]])
                                b1.rel()
                            cp("dve", prod[nxt[1]][0][0:W, :, 0:W], b2.v3(W, 4, W), [b2.k], [prod[nxt[1]][1]])
                            b2.rel()
                            b3 = nb()
                            for jj in range(4):
                                mm(b3.t[0:W, jj * W:(jj + 1) * W], PRW(nxt[1], jj), PRW("Q", jj), True, True,
                                   [prod[nxt[1]][1], prod["Q"][1]], [b3.k])
                            tt("dve", prod["Q"][0][0:W, :, 0:W], b3.v3(W, 4, W), prod["Q"][0][0:W, :, 0:W], ALU.add,
                               [b3.k, prod["Q"][1]], [prod["Q"][1]])
                            b3.rel()
                            cur, nxt = nxt, cur
                        if P3 <= 5:
                            continue
                        RHSn, Un = cur[0], nxt[0]
                        b = nb()
                        for jj in range(4):
                            mm(b.t[0:W, jj * 128:(jj + 1) * 128], B("kap", jj), M[:, j0 + jj, :], True, False, [blk["kap"][1], Mk], [b.k])
                            mm(b.t[0:W, jj * 128:(jj + 1) * 128], PRW("A2t", jj), PR("Vb", jj), False, True, [prod["A2t"][1], prod["Vb"][1]], [b.k])
                        cp("act", prod[RHSn][0][0:W, :, :], b.v3(W, 4, 128), [b.k], [prod[RHSn][1]])
                        b.rel()
                        b = nb()
                        for jj in range(4):
                            mm(b.t[0:W, jj * 128:(jj + 1) * 128], PRW("Q", jj), PR(RHSn, jj), True, True, [prod["Q"][1], prod[RHSn][1]], [b.k])
                        cp("dve", prod[Un][0][0:W, :, :], b.v3(W, 4, 128), [b.k], [prod[Un][1]])
                        b.rel()
                        bO = nb()
                        for jj in range(4):
                            mm(bO.t[0:C, jj * 128:(jj + 1) * 128], rt[:, jj, :], M[:, j0 + jj, :], True, False, [rtk, Mk], [bO.k])
                            mm(bO.t[0:C, jj * 128:(jj + 1) * 128], A2R[0:W, jj, :], PR("Vb", jj), False, False, [A2Rk, prod["Vb"][1]], [bO.k])
                            mm(bO.t[0:C, jj * 128:(jj + 1) * 128], ARn[0:W, jj, :], PR(Un, jj), False, True, [ARnk, prod[Un][1]], [bO.k])
                        cp("act", Oq[0:C, :], bO.t[0:C, 0:512], [bO.k], [Oqk])
                        bO.rel()
                        bM = nb()
                        for jj in range(4):
                            mm(bM.t[:, jj * 128:(jj + 1) * 128], PR("Kt", jj), PR("Vb", jj), True, False, [prod["Kt"][1], prod["Vb"][1]], [bM.k])
                            mm(bM.t[:, jj * 128:(jj + 1) * 128], PR("Bt", jj), PR(Un, jj), False, True, [prod["Bt"][1], prod[Un][1]], [bM.k])
                        tt("pool", M[:, Sq, :], M[:, Sq, :], G[:, Sq, C - 1:C].broadcast_to([128, 4, 128]), ALU.mult, [Mk, Gk], [Mk])
                        tt("dve", M[:, Sq, :], M[:, Sq, :], bM.v3(128, 4, 128), ALU.add, [Mk, bM.k], [Mk])
                        bM.rel()
                        if P3 <= 6:
                            continue
                        O3 = Oq[0:C, :].rearrange("p (a b) -> p a b", a=8)
                        c3 = cen[0:C, :].rearrange("p (a b) -> p a b", a=8)
                        s3 = sqv[0:C, :].rearrange("p (a b) -> p a b", a=8)
                        red(st8[0:C, 0:8], O3, [Oqk], [st8k])
                        ts("dve", st8[0:C, 0:8], st8[0:C, 0:8], 1.0 / 64, None, ALU.mult, None, [st8k], [st8k])
                        tt("dve", c3, O3, st8[0:C, 0:8].unsqueeze(2).broadcast_to([C, 8, 64]), ALU.subtract, [Oqk, st8k], [cenk])
                        act(sqv[0:C, :], cen[0:C, :], AF.Square, [cenk], [sqvk])
                        red(st8[0:C, 8:16], s3, [sqvk], [st8k])
                        ts("dve", st8[0:C, 8:16], st8[0:C, 8:16], 1.0 / 64, GN_EPS, ALU.mult, ALU.add, [st8k], [st8k])
                        act(st8[0:C, 8:16], st8[0:C, 8:16], AF.Sqrt, [st8k], [st8k])
                        recip(st8[0:C, 8:16], st8[0:C, 8:16], [st8k], [st8k])
                        tt("dve", c3, c3, st8[0:C, 8:16].unsqueeze(2).broadcast_to([C, 8, 64]), ALU.mult, [cenk, st8k], [cenk])
                        bT = nb()
                        for jj in range(4):
                            tr(bT.t[:, jj * C:(jj + 1) * C], cen[0:C, jj * 128:(jj + 1) * 128], ident[0:C, 0:C], [cenk, "ident"], [bT.k])
                        tt("dve", t2[:], bT.v3(128, 4, C), pvB(l, "ln_g", j0, 4, C), ALU.mult, [bT.k, "pv"], [t2k])
                        bT.rel()
                        tt("pool", mixT[:, j0:j0 + 4, tok0:tok0 + C], t2[:], bon[:], ALU.add, [t2k, bonk], ["mixT"])

            def cons_gate(j, wd, b):
                act(sgt[:, 0:NT], b.t[:, 0:NT], AF.Silu, [b.k], [sgtk])
                tt("dve", mixT[:, j, 0:NT], mixT[:, j, 0:NT], sgt[:, 0:NT], ALU.mult, ["mixT", sgtk], ["mixT"])
            in_proj_fm(l, t, OFF["rw_gate"], 1024, cons_gate)

        def phase4(l, t):
            NT = t.NT
            A = new_phase("p4")
            cqb, cqbk = A.get("cqb", [8, NT], BF16)
            cosT, cosk = A.get("cos", [NT]); sinT, sink = A.get("sin", [NT])
            sq = [A.get(f"sq{i}", [NTP]) for i in range(2)]
            rs = [A.get(f"rs{i}", [NTP]) for i in range(2)]
            xrq, xrqk = A.get("xrq", [NT])
            mark = A.off
            ncommon = len(A.keys)
            latT, latTk = A.get("latT", [4, NTP])
            latb, latbk = A.get("latb", [4, NTP], BF16)
            krb, krbk = A.get("krb", [NTP], BF16)
            wuk, wukk = A.get("wuk", [4, 2048], BF16)
            wuvb = [A.get(f"wuv{i}", [4, 512], BF16) for i in range(2)]
            Vp = [A.get(f"Vp{i}", [512], BF16) for i in range(2)]
            knb = [A.get(f"knb{i}", [NTP], BF16) for i in range(2)]
            lato = [A.get(f"lato{i}", [512]) for i in range(2)]
            kro, krok = A.get("kro", [64])
            ctm, ctmk = A.get("ctm", [4, 512])
            ktm, ktmk = A.get("ktm", [4, 64])
            yy, yyk = A.get("yy", [NT]); kf, kfk = A.get("kf", [NT]); ki, kik = A.get("ki", [NT], I32)
            seal(A)
            sqi = [0]

            def next_sq():
                sqi[0] += 1
                return sq[sqi[0] % 2], rs[sqi[0] % 2]

            def rms_stats(b, rows, n, div, eps):
                (s_, sk), (r_, rk) = next_sq()
                act(s_[0:rows, 0:n], b.t[0:rows, 0:n], AF.Square, [b.k], [sk])
                b2 = nb()
                mm(b2.t[0:rows, 0:n], onesf[0:rows, 0:rows], s_[0:rows, 0:n], True, True, ["onesf", sk], [b2.k])
                ts("dve", r_[0:rows, 0:n], b2.t[0:rows, 0:n], 1.0 / div, eps, ALU.mult, ALU.add, [b2.k], [rk])
                b2.rel()
                act(r_[0:rows, 0:n], r_[0:rows, 0:n], AF.Sqrt, [rk], [rk])
                recip(r_[0:rows, 0:n], r_[0:rows, 0:n], [rk], [rk])
                return r_, rk

            pos, posk = yy, yyk
            dma("sp", pos[0:64, :], c_pos[:, t.pos0:t.pos0 + NT], (), [posk])
            ts("dve", yy[0:64, :], pos[0:64, :], freq[:, 0:1], 1.0 / (2 * math.pi), ALU.mult, ALU.mult, [posk, "freq"], [yyk])
            cp("dve", ki[0:64, :], yy[0:64, :], [yyk], [kik])
            cp("dve", kf[0:64, :], ki[0:64, :], [kik], [kfk])
            tt("dve", yy[0:64, :], yy[0:64, :], kf[0:64, :], ALU.subtract, [yyk, kfk], [yyk])
            act(sinT[0:64, :], yy[0:64, :], AF.Sin, [yyk], [sink], scale=math.pi)
            act(kf[0:64, :], yy[0:64, :], AF.Sin, [yyk], [kfk], scale=math.pi / 2)
            tt("dve", kf[0:64, :], kf[0:64, :], kf[0:64, :], ALU.mult, [kfk], [kfk])
            ts("dve", kf[0:64, :], kf[0:64, :], -2.0, 1.0, ALU.mult, ALU.add, [kfk], [kfk])
            tt("dve", cosT[0:64, :], sinT[0:64, :], sinT[0:64, :], ALU.mult, [sink], [cosk])
            ts("dve", cosT[0:64, :], cosT[0:64, :], -2.0, 1.0, ALU.mult, ALU.add, [cosk], [cosk])
            tt("dve", sinT[0:64, :], sinT[0:64, :], kf[0:64, :], ALU.mult, [sink, kfk], [sink])
            ts("dve", sinT[0:32, :], sinT[0:32, :], -2.0, None, ALU.mult, None, [sink], [sink])
            ts("dve", sinT[32:64, :], sinT[32:64, :], 2.0, None, ALU.mult, None, [sink], [sink])

            def rotary(out_ap, outk, x_ap, xk, n):
                b = nb()
                mm(b.t[0:64, 0:n], pswap[:, :], x_ap, True, True, ["pswap", xk], [b.k])
                (s_, sk), _ = next_sq()
                tt("dve", s_[0:64, 0:n], b.t[0:64, 0:n], sinT[0:64, 0:n], ALU.mult, [b.k, sink], [sk])
                b.rel()
                tt("dve", x_ap, x_ap, cosT[0:64, 0:n], ALU.mult, [xk, cosk], [xk])
                tt("dve", out_ap, x_ap, s_[0:64, 0:n], ALU.add, [xk, sk], [outk])

            def cons_cq(j, wd, b):
                cp("dve", cqb[:, j, :], b.t[:, 0:NT], [b.k], [cqbk])
                (s_, sk), _ = next_sq()
                act(s_[:, 0:NT], b.t[:, 0:NT], AF.Square, [b.k], [sk])
                mm(ACC0.t[:, 0:NT], onesf[:, :], s_[:, 0:NT], j == 0, j == 7, ["onesf", sk], [ACC0.k])
            in_proj_fm(l, t, OFF["cq"], 1024, cons_cq)
            _, (r_, rk) = next_sq()
            ts("dve", r_[:, 0:NT], ACC0.t[:, 0:NT], 1.0 / 1024, NORM_EPS, ALU.mult, ALU.add, [ACC0.k], [rk])
            act(r_[:, 0:NT], r_[:, 0:NT], AF.Sqrt, [rk], [rk])
            recip(r_[:, 0:NT], r_[:, 0:NT], [rk], [rk])
            for j in range(8):
                stt(cqb[:, j, :], cqb[:, j, :], pvc(l, "qng", j), r_[:, 0:NT], ALU.mult, ALU.mult, [cqbk, "pv", rk], [cqbk])

            def cons_ckv(j, wd, b):
                cp("dve", latT[:, j, 0:NT], b.t[:, 0:NT], [b.k], [latTk])
                (s_, sk), _ = next_sq()
                act(s_[:, 0:NT], b.t[:, 0:NT], AF.Square, [b.k], [sk])
                mm(ACC1.t[:, 0:NT], onesf[:, :], s_[:, 0:NT], j == 0, j == 3, ["onesf", sk], [ACC1.k])
            in_proj_fm(l, t, OFF["ckv"], 512, cons_ckv)
            _, (r_, rk) = next_sq()
            ts("dve", r_[:, 0:NT], ACC1.t[:, 0:NT], 1.0 / 512, NORM_EPS, ALU.mult, ALU.add, [ACC1.k], [rk])
            act(r_[:, 0:NT], r_[:, 0:NT], AF.Sqrt, [rk], [rk])
            recip(r_[:, 0:NT], r_[:, 0:NT], [rk], [rk])
            for j in range(4):
                stt(latT[:, j, 0:NT], latT[:, j, 0:NT], pvc(l, "kvg", j), r_[:, 0:NT], ALU.mult, ALU.mult, [latTk, "pv", rk], [latTk])
                cp("pool", latb[:, j, 0:NT], latT[:, j, 0:NT], [latTk], [latbk])
            lat_dst = lat_s if t.sample else lat_p
            kr_dst = kr_s if t.sample else kr_p
            for si, (r0, rows) in enumerate(t.subs):
                g0 = r0 if t.sample else t.idx * NTP + r0
                b = nb()
                for j in range(4):
                    tr(b.t[0:rows, j * 128:(j + 1) * 128], latT[:, j, r0:r0 + rows], ident[:, :], [latTk, "ident"], [b.k])
                lo_, lok = lato[si % 2]
                cp("act", lo_[0:rows, :], b.t[0:rows, 0:512], [b.k], [lok])
                b.rel()
                dma("pool", lat_dst[l, g0:g0 + rows, :], lo_[0:rows, :], [lok], [])

            def cons_kr(j, wd, b):
                r_, rk = rms_stats(b, 64, NT, 64, NORM_EPS)
                stt(xrq[0:64, :], b.t[0:64, 0:NT], pvc(l, "kn_rope", 0, 1, 64), r_[0:64, 0:NT], ALU.mult, ALU.mult, [b.k, "pv", rk], [xrqk])
            in_proj_fm(l, t, OFF["kr"], 64, cons_kr)
            rotary(xrq[0:64, :], xrqk, xrq[0:64, :], xrqk, NT)
            cp("pool", krb[0:64, 0:NT], xrq[0:64, :], [xrqk], [krbk])
            for si, (r0, rows) in enumerate(t.subs):
                g0 = r0 if t.sample else t.idx * NTP + r0
                b = nb()
                tr(b.t[0:rows, 0:64], xrq[0:64, r0:r0 + rows], ident[0:64, 0:64], [xrqk, "ident"], [b.k])
                cp("act", kro[0:rows, :], b.t[0:rows, 0:64], [b.k], [krok])
                b.rel()
                dma("pool", kr_dst[l, g0:g0 + rows, :], kro[0:rows, :], [krok], [])

            dma("sp", wuk[:], wuk_s[l].rearrange("p (c n) -> p c n", c=4), [f"wuk_s{l}"], [wukk])

            def emit_kv(s, lat_ap, n, key0):
                kd = f"kT_{l}_{s}"
                vd = f"V_{l}_{s}"
                for h in range(H):
                    b = nb()
                    for ch in range(4):
                        mm(b.t[:, 0:n], wuk[:, ch, h * 128:(h + 1) * 128], lat_ap[:, ch, :], ch == 0, ch == 3, [wukk, latbk], [b.k])
                    r_, rk = rms_stats(b, 128, n, 128, NORM_EPS)
                    kb_, kbk = knb[h % 2]
                    stt(kb_[:, 0:n], b.t[:, 0:n], pvc(l, "kn_nope"), r_[:, 0:n], ALU.mult, ALU.mult, [b.k, "pv", rk], [kbk])
                    b.rel()
                    dma("pool", kT_d[l][s][h, :, key0:key0 + n], kb_[:, 0:n], [kbk], [f"{kd}_{h}"])
                for vbk in range(4):
                    wv, wvk = wuvb[vbk % 2]
                    dma("sp", wv[:], wuv_s[l][vbk].rearrange("p (c n) -> p c n", c=4), [f"wuv_s{l}"], [wvk])
                    for tb in range((n + 127) // 128):
                        m = min(128, n - tb * 128)
                        b = nb()
                        for ch in range(4):
                            mm(b.t[0:m, 0:512], lat_ap[:, ch, tb * 128:tb * 128 + m], wv[:, ch, :], ch == 0, ch == 3, [latbk, wvk], [b.k])
                        vp, vpk = Vp[(vbk * 4 + tb) % 2]
                        cp(ev_eng(), vp[0:m, :], b.t[0:m, 0:512], [b.k], [vpk])
                        b.rel()
                        dma("pool", V_d[l][s][key0 + tb * 128:key0 + tb * 128 + m, vbk * 512:(vbk + 1) * 512], vp[0:m, :], [vpk], [f"{vd}_{vbk}_{tb}"])

            if t.sample:
                for (s, o, n, p0) in t.segs:
                    for blk_ in range(PAST // 512):
                        for sub in range(4):
                            r0 = blk_ * 512 + sub * 128
                            dma("sp", ctm[:, sub, :], cache_lat[l, s - 1, r0:r0 + 128, :], (), [ctmk])
                            dma("sp", ktm[:, sub, :], cache_kr[l, s - 1, r0:r0 + 128, :], (), [ktmk])
                        for sub in range(4):
                            b = nb()
                            for j in range(4):
                                tr(b.t[:, j * 128:(j + 1) * 128], ctm[:, sub, j * 128:(j + 1) * 128], ident[:, :], [ctmk, "ident"], [b.k])
                            for j in range(4):
                                cp(ev_eng(), latb[:, j, sub * 128:(sub + 1) * 128], b.t[:, j * 128:(j + 1) * 128], [b.k], [latbk])
                            b.rel()
                        b = nb()
                        for sub in range(4):
                            tr(b.t[0:64, sub * 128:(sub + 1) * 128], ktm[:, sub, :], ident[:, :], [ktmk, "ident"], [b.k])
                        cp("act", krb[0:64, 0:512], b.t[0:64, 0:512], [b.k], [krbk])
                        b.rel()
                        dma("pool", krT_d[l][s][:, blk_ * 512:(blk_ + 1) * 512], krb[0:64, 0:512], [krbk], [f"krT_{l}_{s}"])
                        emit_kv(s, latb[:, :, 0:512], 512, blk_ * 512)
                for j in range(4):
                    cp("pool", latb[:, j, 0:NT], latT[:, j, 0:NT], [latTk], [latbk])
                cp("pool", krb[0:64, 0:NT], xrq[0:64, :], [xrqk], [krbk])
            for (s, o, n, p0) in t.segs:
                dma("pool", krT_d[l][s][:, p0:p0 + n], krb[0:64, o:o + n], [krbk], [f"krT_{l}_{s}"])
                emit_kv(s, latb[:, :, o:o + n], n, p0)

            old_keys = list(A.keys)
            A.off = mark
            nk_max = max(p0 + n for (s, o, n, p0) in t.segs)
            nkb_max = (nk_max + 127) // 128
            nseg = len(t.segs)
            k0 = len(A.keys)
            knA = [A.get(f"kn{i}", [nk_max], BF16) for i in range(2)]
            VA = [A.get(f"V{i}", [nkb_max, 128], BF16) for i in range(2)]
            krA, krAk = A.get("krA", [nseg, nk_max], BF16)
            wuqh = [A.get(f"wuq{i}", [8, 192], BF16) for i in range(2)]
            qnA = [A.get(f"qn{i}", [NT], BF16) for i in range(2)]
            qrA = [A.get(f"qr{i}", [NT], BF16) for i in range(2)]
            PT = [A.get(f"PT{i}", [NTP], BF16) for i in range(4)]
            rec, reck = A.get("rec", [NT]); tq, tqk = A.get("tq", [NT])
            sgm, sgmk = A.get("sgm", [2, NT], BF16)
            xq2, xq2k = A.get("xq2", [NT])
            if not t.sample:
                matt, mattk = A.get("matt", [4, NTP], BF16)
            newk = A.keys[k0:]
            P.rekey(old_keys[ncommon:], newk)
            prev_keys[0] = list(A.keys[:ncommon]) + newk
            if not t.sample:
                dma("pool", matt[:], c_matt, (), [mattk])
            for si, (s, o, n, p0) in enumerate(t.segs):
                dma("sp", krA[0:64, si, 0:p0 + n], krT_d[l][s][:, 0:p0 + n], [f"krT_{l}_{s}"], [krAk])
            pti = [0]

            def emit_q(h):
                wq_, wqk = wuqh[h % 2]
                qn_, qnk_ = qnA[h % 2]
                qr_, qrk_ = qrA[h % 2]
                if h == 0:
                    dma("sp", wq_[:], wuq_s[l][0].rearrange("p (c n) -> p c n", c=8), [f"wuq_s{l}"], [wqk])
                b = nb()
                for ch in range(8):
                    mm(b.t[:, 0:NT], wq_[:, ch, 0:128], cqb[:, ch, :], ch == 0, ch == 7, [wqk, cqbk], [b.k])
                r_, rk = rms_stats(b, 128, NT, 128, NORM_EPS)
                stt(qn_[:, :], b.t[:, 0:NT], gq[:, l, 0:1], r_[:, 0:NT], ALU.mult, ALU.mult, [b.k, "gq", rk], [qnk_])
                b.rel()
                b = nb()
                for ch in range(8):
                    mm(b.t[0:64, 0:NT], wq_[:, ch, 128:192], cqb[:, ch, :], ch == 0, ch == 7, [wqk, cqbk], [b.k])
                r_, rk = rms_stats(b, 64, NT, 64, NORM_EPS)
                stt(xq2[0:64, :], b.t[0:64, 0:NT], gq[0:64, l, 1:2], r_[0:64, 0:NT], ALU.mult, ALU.mult, [b.k, "gq", rk], [xq2k])
                b.rel()
                rotary(qr_[0:64, :], qrk_, xq2[0:64, :], xq2k, NT)
                if h + 2 < H:
                    dma("sp", wq_[:], wuq_s[l][h + 2].rearrange("p (c n) -> p c n", c=8), [f"wuq_s{l}"], [wqk])

            dma("sp", wuqh[1][0][:], wuq_s[l][1].rearrange("p (c n) -> p c n", c=8), [f"wuq_s{l}"], [wuqh[1][1]])
            emit_q(0)
            for h in range(H):
                if h % 2 == 0:
                    def cons_mg(j, wd, b):
                        act(sgm[:, j, :], b.t[:, 0:NT], AF.Silu, [b.k], [sgmk])
                    in_proj_fm(l, t, OFF["mla_gate"] + h * 128, 256, cons_mg)
                if h + 1 < H:
                    emit_q(h + 1)
                qn, qnk = qnA[h % 2]
                qr, qrk = qrA[h % 2]
                for si, (s, o, n, p0) in enumerate(t.segs):
                    nk = p0 + n
                    nkb = (nk + 127) // 128
                    nfull = nk // 128
                    kn, knk = knA[(h * nseg + si) % 2]
                    V, Vk = VA[(h * nseg + si) % 2]
                    dma("sp", kn[:, 0:nk], kT_d[l][s][h, :, 0:nk], [f"kT_{l}_{s}_{h}"], [knk])
                    if nfull > 0:
                        dma("sp", V[:, 0:nfull, :], V_d[l][s][0:nfull * 128, h * 128:(h + 1) * 128].rearrange("(b p) d -> p b d", p=128),
                            [f"V_{l}_{s}_{h // 4}_{tbx}" for tbx in range(4)], [Vk])
                    if nkb > nfull:
                        m_ = nk - nfull * 128
                        dma("sp", V[0:m_, nfull, :], V_d[l][s][nfull * 128:nk, h * 128:(h + 1) * 128], [f"V_{l}_{s}_{h // 4}_{tbx}" for tbx in range(4)], [Vk])
                    def geom(kb):
                        m = min(128, nk - kb * 128)
                        diag = (not t.sample) and kb >= t.idx * 4
                        jd = kb - t.idx * 4 if diag else 0
                        qlo = jd * 128 if diag else 0
                        return m, diag, jd, qlo, n - qlo

                    def scores(kb):
                        m, diag, jd, qlo, nq = geom(kb)
                        b = nb()
                        mm(b.t[0:m, 0:nq], kn[:, kb * 128:kb * 128 + m], qn[:, o + qlo:o + n], True, False, [knk, qnk], [b.k])
                        mm(b.t[0:m, 0:nq], krA[0:64, si, kb * 128:kb * 128 + m], qr[0:64, o + qlo:o + n], False, True, [krAk, qrk], [b.k])
                        return b
                    pend = []
                    nxt = [0]

                    def push():
                        if nxt[0] < nkb:
                            pend.append(scores(nxt[0]))
                            nxt[0] += 1
                    push(); push()
                    for kb in range(nkb):
                        m, diag, jd, qlo, nq = geom(kb)
                        b = pend.pop(0)
                        push()
                        pt, ptk = PT[pti[0] % 4]
                        pti[0] += 1
                        act(pt[0:m, 0:nq], b.t[0:m, 0:nq], AF.Exp, [b.k], [ptk])
                        b.rel()
                        if diag:
                            tt("pool", pt[0:m, 0:nq], pt[0:m, 0:nq], matt[0:m, jd, qlo:NTP], ALU.mult, [ptk, mattk], [ptk])
                        mm(ACC0.t[:, qlo:n], V[0:m, kb, :], pt[0:m, 0:nq], kb == 0, kb == nkb - 1, [Vk, ptk], [ACC0.k])
                        mm(ACC1.t[:, qlo:n], onesb[0:m, :], pt[0:m, 0:nq], kb == 0, kb == nkb - 1, ["onesb", ptk], [ACC1.k])
                    recip(rec[:, 0:n], ACC1.t[:, 0:n], [ACC1.k], [reck])
                    tt("dve", tq[:, 0:n], ACC0.t[:, 0:n], rec[:, 0:n], ALU.mult, [ACC0.k, reck], [tqk])
                    tt("pool", mixT[:, 8 + h, o:o + n], tq[:, 0:n], sgm[:, h % 2, o:o + n], ALU.mult, [tqk, sgmk], ["mixT"])

        def phase5(l, t):
            NT = t.NT
            nsub = len(t.subs)
            rows = t.subs[0][1]
            A = new_phase("p5")
            gbc, gbck = A.get("gbc", [D])
            xq = [A.get(f"xq{i}", [nsub, 256]) for i in range(2)]
            oq = [A.get(f"oq{i}", [nsub, 256]) for i in range(2)]
            seal(A)
            last = (l == NL - 1)
            src = (x1_s if l > 0 else xs_in) if t.sample else (x1_p if l > 0 else xp)
            dst = (y_s if last else x1_s) if t.sample else (y_p if last else x1_p)
            srck = ["x1_s" if t.sample else "x1_p"] if l > 0 else []
            dstk = [] if last else ["x1_s" if t.sample else "x1_p"]
            if t.sample:
                for (s, o, n, p0) in t.segs:
                    dma("sp", gbc[o:o + n, :], gate_d[s:s + 1, :].broadcast_to([n, D]), ["gate_d"], [gbck])
            else:
                dma("sp", gbc[:, :], gate_d[0:1, :].broadcast_to([128, D]), ["gate_d"], [gbck])
            g0 = 0 if t.sample else t.idx * NTP
            for cb in range(16):
                c0 = cb * 256
                wb, wk = load_ws(wout_s[l][cb], f"wout_s{l}_{cb}")
                xt, xk = xq[cb % 2]
                ot, ok = oq[cb % 2]
                dma("sp", xt[0:rows, :, :], src[g0:g0 + NT, c0:c0 + 256].rearrange("(s p) c -> p s c", p=rows), srck, [xk])
                for si, (r0, rws) in enumerate(t.subs):
                    b = nb()
                    for ch in range(DC):
                        mm(b.t[0:rows, 0:256], mixT[:, ch, r0:r0 + rows], wb[:, ch, :], ch == 0, ch == DC - 1, [wk, "mixT"], [b.k])
                    tt("dve", ot[0:rows, si, :], b.t[0:rows, 0:256], gbc[0:rows, c0:c0 + 256], ALU.mult, [b.k, gbck], [ok])
                    b.rel()
                tt("pool", ot[0:rows, :, :], ot[0:rows, :, :], xt[0:rows, :, :], ALU.add, [ok, xk], [ok])
                dma("pool", dst[g0:g0 + NT, c0:c0 + 256].rearrange("(s p) c -> p s c", p=rows), ot[0:rows, :, :], [ok], dstk)

        def finish_seq(l, s):
            A = new_phase("lf")
            Ts, Tsk = A.get("Ts", [8, 128])
            shs, shsk = A.get("shs", [128])
            cvs, cvsk = A.get("cvs", [2, 128])
            seal(A)
            if True:
                b = nb()
                tr(b.t[0:25, 0:128], rwprev[s][:, 0:25], ident[:, :], [f"rwprev{s}", "ident"], [b.k])
                cp("act", shs[0:25, :], b.t[0:25, 0:128], [b.k], [shsk])
                b.rel()
                dsh = sh_p[l] if s == 0 else sh_s[l, s - 1]
                dma("pool", dsh.rearrange("(c p) -> c p", p=128), shs[0:25, :], [shsk], [])
                b = nb()
                for j in range(2):
                    tr(b.t[0:8, j * 128:(j + 1) * 128], convh[s][:, :, j], ident[:, :], [f"convh{s}", "ident"], [b.k])
                cp("act", cvs[0:8, :, :], b.v3(8, 2, 128), [b.k], [cvsk])
                b.rel()
                dcv = cv_p[l] if s == 0 else cv_s[l, s - 1]
                dma("pool", dcv.rearrange("j (c p) -> c j p", p=128), cvs[0:8, :, :], [cvsk], [])
                for q in range(2):
                    b = nb()
                    for jj in range(4):
                        tr(b.t[:, jj * 128:(jj + 1) * 128], Mblk[s][:, q * 4 + jj, :], ident[:, :], [MKEY[s], "ident"], [b.k])
                    cp("act", Ts[:, q * 4:q * 4 + 4, :], b.v3(128, 4, 128), [b.k], [Tsk])
                    b.rel()
                drw = rw_p[l] if s == 0 else rw_s[l, s - 1]
                dv = drw.rearrange("(j h) v k -> h v j k", h=2)
                dma("pool", dv[0], Ts[0:64, :, 0:64], [Tsk], [])
                dma("pool", dv[1], Ts[64:128, :, 64:128], [Tsk], [])

        for l in range(NL):
            layer_setup(l)
            for t in tiles:
                if t.sample:
                    finish_seq(l, 0)
                    load_state_M(l, 1)
                phase1(l, t)
                if 2 in phases:
                    phase2(l, t)
                if 3 in phases:
                    phase3(l, t)
                if 4 in phases:
                    phase4(l, t)
                phase5(l, t)
            finish_seq(l, 1)
            finish_seq(l, 2)
        nops = {e: len(P.ops[e]) for e in ENGS}
        print("ops per engine:", nops, flush=True)
        P.emit()
    return nc


def _cols(v):
    v = np.asarray(v, np.float32).reshape(-1)
    return np.ascontiguousarray(v.reshape(-1, 128).T)


def _consts(SEQ):
    c = {}
    c["c_ident"] = np.eye(128, dtype=np.float32)
    k = np.arange(128)[:, None]
    q = np.arange(NTP)[None, :]
    matt = np.zeros((128, 4, NTP), np.float32)
    for j in range(4):
        matt[:, j, :] = ((j * 128 + k) // 64 <= q // 64)
    c["c_matt"] = matt

    def blockdiag(m):
        n = m.shape[0]
        z = np.zeros((2 * n, 2 * n), np.float32)
        z[:n, :n] = m
        z[n:, n:] = m
        return z
    for C, suf in ((64, ""), (32, "32")):
        s = np.arange(C)[:, None]
        t = np.arange(C)[None, :]
        su = (s < t).astype(np.float32)
        sl = (s > t).astype(np.float32)
        mu = (s <= t).astype(np.float32)
        c["c_msu" + suf] = blockdiag(su)
        c["c_msl" + suf] = blockdiag(sl)
        c["c_mu" + suf] = np.concatenate([mu, mu], 0)
    bo = np.zeros((128, 128), np.float32)
    bo[:64, :64] = 1
    bo[64:, 64:] = 1
    c["c_bones"] = bo
    ps = np.zeros((64, 64), np.float32)
    for i in range(64):
        ps[(i + 32) % 64, i] = 1
    c["c_pswap"] = ps
    pos = np.concatenate([np.arange(SEQ), PAST + np.arange(SS), PAST + np.arange(SS)]).astype(np.float32)
    c["c_pos"] = np.ascontiguousarray(np.broadcast_to(pos[None, :], (64, SEQ + 2 * SS)))
    c["c_fidx"] = (np.arange(64) % 32).astype(np.float32)[:, None]
    return c


def _pack_pv(inp, NL):
    pv = np.zeros((NL, 128, NPV), np.float32)
    for l in range(NL):
        def put(name, arr, rows=128):
            a = np.asarray(arr, np.float32)
            pv[l, :a.shape[0], PV[name]:PV[name] + a.shape[1]] = a
        put("norm_g", _cols(inp["norm_g"][l]))
        put("b_shift", _cols(inp["b_ada"][l][0:D]))
        put("b_scale", _cols(inp["b_ada"][l][D:2 * D]))
        put("mu", _cols(inp["rw_mu"][l]))
        put("w0", _cols(inp["rw_w0"][l]))
        put("a0", _cols(inp["rw_a0"][l]))
        put("kk", _cols(inp["rw_kk"][l]))
        put("ka", _cols(inp["rw_ka"][l]))
        put("rk", _cols(inp["rw_rk"][l].reshape(-1)))
        put("qng", _cols(inp["mla_qnorm_g"][l]))
        put("kvg", _cols(inp["mla_kvnorm_g"][l]))
        put("qn_nope", _cols(inp["mla_qn_nope"][l]))
        put("qn_rope", np.asarray(inp["mla_qn_rope"][l], np.float32)[:, None])
        put("kn_nope", _cols(inp["mla_kn_nope"][l]))
        put("kn_rope", np.asarray(inp["mla_kn_rope"][l], np.float32)[:, None])
        cw = np.concatenate([_cols(inp["conv_w"][l][j]) for j in range(3)], 1)
        put("conv_w", cw)
        put("conv_b", _cols(inp["conv_b"][l]))
        put("ln_g", _cols(inp["rw_ln_g"][l]))
        put("ln_b", _cols(inp["rw_ln_b"][l]))
    return pv


_NC_CACHE = {}
RUNNER = None
TRACE = False


def run(inputs, SEQ=SEQ_FULL, NL=L_FULL, phases=(1, 2, 3, 4, 5), n_cores=8):
    inp = {k: np.asarray(v) for k, v in inputs.items()}
    key = (SEQ, NL, tuple(phases))
    if key not in _NC_CACHE:
        _NC_CACHE[key] = build(SEQ, NL, phases)
    nc = _NC_CACHE[key]
    consts = _consts(SEQ)
    pv = _pack_pv(inp, NL)
    shared = dict(consts)
    shared["pv"] = pv
    f32 = lambda a: np.ascontiguousarray(a, dtype=np.float32)
    shared["w_ada"] = f32(inp["w_ada"][:NL])
    shared["b_gate"] = f32(inp["b_ada"][:NL, None, 2 * D:3 * D])
    shared["w_in"] = f32(inp["w_in"][:NL])
    shared["w_out"] = f32(inp["w_out"][:NL])
    shared["w_uq"] = f32(inp["mla_w_uq"][:NL])
    shared["w_uk"] = f32(inp["mla_w_uk"][:NL])
    shared["w_uv"] = f32(inp["mla_w_uv"][:NL])
    shared["rw_w2"] = f32(inp["rw_w2"][:NL])
    shared["rw_a2"] = f32(inp["rw_a2"][:NL])
    in_maps = []
    for i in range(n_cores):
        b = i % 4
        sbs = [2 * i, 2 * i + 1]
        m = dict(shared)
        m["xp"] = f32(inp["x_prompt"][b, :SEQ])
        m["xs"] = f32(inp["x_sample"][sbs].reshape(2 * SS, D))
        cs = np.stack([inp["c_prompt"][b], inp["c_sample"][sbs[0]], inp["c_sample"][sbs[1]]], 0)
        m["cT"] = f32(cs.reshape(3, DC, 128).transpose(2, 1, 0))
        m["cache_lat"] = f32(inp["cache_mla_latent"][:NL, sbs])
        m["cache_kr"] = f32(inp["cache_mla_krope"][:NL, sbs])
        S = inp["state_rwkv"][:NL, sbs]
        Mst = S.reshape(NL, 2, 8, 2, 64, 64).transpose(0, 1, 3, 5, 2, 4)
        m["st_M"] = f32(Mst.reshape(NL, 2, 128, 8, 64))
        sh = inp["state_rwkv_shift"][:NL, sbs]
        m["st_shift"] = f32(sh.reshape(NL, 2, 25, 128).transpose(0, 1, 3, 2))
        cv = inp["state_conv"][:NL, sbs]
        m["st_conv"] = f32(cv.reshape(NL, 2, 2, 8, 128).transpose(0, 1, 4, 3, 2))
        in_maps.append(m)
    if RUNNER is not None:
        return RUNNER(nc, in_maps)
    if TRACE:
        res = run_bass_kernel_spmd(nc, in_maps, core_ids=list(range(n_cores)), trace=True)
        print('EXEC_NS', res.exec_time_ns, flush=True)
        return res.results
    res = run_bass_kernel_spmd(nc, in_maps, core_ids=list(range(n_cores)))
    return res.results


def kernel(**inputs):
    r = run(inputs)
    NL = L_FULL
    B = 4
    f = lambda a: np.asarray(a, dtype=np.float32)
    y_p = np.stack([f(r[b]["y_p"]) for b in range(B)], 0)
    y_s = np.concatenate([f(r[i]["y_s"]).reshape(2, SS, D) for i in range(8)], 0)
    lat_p = np.stack([f(r[b]["lat_p"]) for b in range(B)], 1)
    kr_p = np.stack([f(r[b]["kr_p"]) for b in range(B)], 1)
    rw_p = np.stack([f(r[b]["rw_p"]) for b in range(B)], 1)
    sh_p = np.stack([f(r[b]["sh_p"]) for b in range(B)], 1)
    cv_p = np.stack([f(r[b]["cv_p"]) for b in range(B)], 1)
    lat_s = np.concatenate([f(r[i]["lat_s"]).reshape(NL, 2, SS, 512) for i in range(8)], 1)
    kr_s = np.concatenate([f(r[i]["kr_s"]).reshape(NL, 2, SS, 64) for i in range(8)], 1)
    rw_s = np.concatenate([f(r[i]["rw_s"]) for i in range(8)], 1)
    sh_s = np.concatenate([f(r[i]["sh_s"]) for i in range(8)], 1)
    cv_s = np.concatenate([f(r[i]["cv_s"]) for i in range(8)], 1)
    return (y_p, y_s, lat_p, kr_p, rw_p, sh_p, cv_p, lat_s, kr_s, rw_s, sh_s, cv_s)
```

```python
import contextlib
import math
import numpy as np
import concourse.bass as bass
import concourse.mybir as mybir
from concourse.bass_utils import run_bass_kernel_spmd

F32 = mybir.dt.float32
BF16 = mybir.dt.bfloat16
I32 = mybir.dt.int32
AF = mybir.ActivationFunctionType
ALU = mybir.AluOpType
AX = mybir.AxisListType

D = 4096
DC = 32
H = 16
L_FULL = 2
SEQ_FULL = 4096
PAST = 2048
SS = 32
NTP = 512
IN_COLS = 11968
OFF = dict(rw_pre=0, rw_gate=3200, cq=4224, ckv=5248, kr=5760, mla_gate=5824,
           cv_b=7872, cv_c=8896, cv_x=9920, cv_gate=10944)
QSCALE = 192.0 ** -0.5
C0 = math.exp(-0.5)
NORM_EPS = 1e-6
GN_EPS = 64e-5

PV = {}
_c = 0
for _n, _w in (("norm_g", 32), ("b_shift", 32), ("b_scale", 32), ("mu", 25), ("w0", 8), ("a0", 8),
               ("kk", 8), ("ka", 8), ("rk", 8), ("qng", 8), ("kvg", 4), ("qn_nope", 1), ("qn_rope", 1),
               ("kn_nope", 1), ("kn_rope", 1), ("conv_w", 24), ("conv_b", 8), ("ln_g", 8), ("ln_b", 8)):
    PV[_n] = _c
    _c += _w
NPV = _c

ENGS = ("pe", "act", "dve", "pool", "sp")
EPOCH = 20000
NDMA = 8
import os as _os
SAME_ENGINE_SYNC = _os.environ.get('SES', '1') == '1'


class Op:
    __slots__ = ("eng", "fn", "deps", "is_dma", "sig", "dsem", "dval", "prevd")

    def __init__(self, eng, fn, is_dma):
        self.eng = eng
        self.fn = fn
        self.is_dma = is_dma
        self.deps = []
        self.sig = None
        self.dsem = None
        self.dval = 0
        self.prevd = None


class Prog:
    def __init__(self, nc):
        self.nc = nc
        self.ops = {e: [] for e in ENGS}
        self.lastw = {}
        self.readers = {}
        self.nsig = {e: 0 for e in ENGS}
        self.ndma = {e: 0 for e in ENGS}
        self.dma_last = {}

    def rekey(self, old_keys, new_keys):
        ops = []
        for k in old_keys:
            w = self.lastw.get(k)
            if w is not None:
                ops.append(w)
            ops.extend(self.readers.get(k, ()))
        for k in new_keys:
            self.lastw.pop(k, None)
            self.readers[k] = list(ops)

    def op(self, eng, fn, reads=(), writes=(), dma=False):
        o = Op(eng, fn, dma)
        psr = [k for k in reads if k.startswith("ps") and k not in writes]
        if psr:
            writes = list(writes) + psr
        deps = []
        for k in reads:
            w = self.lastw.get(k)
            if w is not None:
                deps.append(w)
        for k in writes:
            w = self.lastw.get(k)
            if w is not None:
                deps.append(w)
            deps.extend(self.readers.get(k, ()))
        seen = set()
        for d in deps:
            if id(d) in seen or d is o:
                continue
            seen.add(id(d))
            if d.is_dma:
                o.deps.append(d)
            elif d.eng == eng:
                if eng == "pe" or not SAME_ENGINE_SYNC:
                    continue
                o.deps.append(d)
                if d.sig is None:
                    d.sig = -1
            else:
                o.deps.append(d)
                if d.sig is None:
                    d.sig = -1
        for k in reads:
            self.readers.setdefault(k, []).append(o)
        for k in writes:
            self.lastw[k] = o
            self.readers[k] = []
        if dma:
            j = self.ndma[eng]
            self.ndma[eng] = j + 1
            slot = j % NDMA
            o.dsem = (eng, slot)
            o.dval = 16 * (j // NDMA + 1)
            o.prevd = self.dma_last.get((eng, slot))
            self.dma_last[(eng, slot)] = o
        self.ops[eng].append(o)
        return o

    def emit(self):
        nc = self.nc
        for e in ENGS:
            n = 0
            for o in self.ops[e]:
                if o.sig is not None and not o.is_dma:
                    o.sig = n
                    n += 1
            self.nsig[e] = n
        with contextlib.ExitStack() as st:
            esems = {}
            for e in ENGS:
                ne = (self.nsig[e] + EPOCH - 1) // EPOCH
                esems[e] = [st.enter_context(nc.semaphore(f"s_{e}_{i}")) for i in range(ne)]
            dsems = {}
            for e in ENGS:
                for s in range(min(NDMA, self.ndma[e])):
                    dsems[(e, s)] = st.enter_context(nc.semaphore(f"d_{e}_{s}"))
            block = st.enter_context(nc.Block())
            final_dma = dict(self.dma_last)

            def make(e):
                def body(eng):
                    seen_sig = {}
                    seen_dma = {}
                    for o in self.ops[e]:
                        waits = []
                        for d in o.deps:
                            if d.is_dma:
                                if seen_dma.get(d.dsem, 0) < d.dval:
                                    seen_dma[d.dsem] = d.dval
                                    waits.append((dsems[d.dsem], d.dval))
                            else:
                                if seen_sig.get(d.eng, -1) < d.sig:
                                    seen_sig[d.eng] = d.sig
                                    waits.append((esems[d.eng][d.sig // EPOCH], d.sig % EPOCH + 1))
                        if o.is_dma and o.prevd is not None:
                            d = o.prevd
                            if seen_dma.get(d.dsem, 0) < d.dval:
                                seen_dma[d.dsem] = d.dval
                                waits.append((dsems[d.dsem], d.dval))
                        for (s, v) in waits:
                            eng.wait_ge(s, v)
                        ins = o.fn(eng)
                        if o.is_dma:
                            ins.then_inc(dsems[o.dsem], 16)
                        elif o.sig is not None:
                            ins.then_inc(esems[e][o.sig // EPOCH], 1)
                    if e == "sp":
                        for k, d in final_dma.items():
                            if seen_dma.get(d.dsem, 0) < d.dval:
                                eng.wait_ge(dsems[d.dsem], d.dval)
                return body

            block.tensor(make("pe"))
            block.scalar(make("act"))
            block.vector(make("dve"))
            block.gpsimd(make("pool"))
            block.sync(make("sp"))


class Ctx:
    pass


def build(SEQ=SEQ_FULL, NL=L_FULL, phases=(1, 2, 3, 4, 5)):
    nc = bass.Bass("TRN2", target_bir_lowering=False)
    NPT = SEQ // NTP
    LKS = PAST + SS

    def din(name, shape, dt=F32):
        return nc.dram_tensor(name, list(shape), dt, kind="ExternalInput").ap()

    def dout(name, shape, dt=F32):
        return nc.dram_tensor(name, list(shape), dt, kind="ExternalOutput").ap()

    def dscr(name, shape, dt=F32):
        return nc.dram_tensor(name, list(shape), dt).ap()

    xp = din("xp", [SEQ, D])
    xs_in = din("xs", [2 * SS, D])
    cT_in = din("cT", [128, DC, 3])
    cache_lat = din("cache_lat", [NL, 2, PAST, 512])
    cache_kr = din("cache_kr", [NL, 2, PAST, 64])
    st_M = din("st_M", [NL, 2, 128, 8, 64])
    st_shift = din("st_shift", [NL, 2, 128, 25])
    st_conv = din("st_conv", [NL, 2, 128, 8, 2])
    w_ada = din("w_ada", [NL, D, 3 * D])
    b_gate = din("b_gate", [NL, 1, D])
    w_in = din("w_in", [NL, D, IN_COLS])
    w_out = din("w_out", [NL, D, D])
    w_uq = din("w_uq", [NL, 1024, 3072])
    w_uk = din("w_uk", [NL, 512, 2048])
    w_uv = din("w_uv", [NL, 512, 2048])
    w_w2 = din("rw_w2", [NL, 64, 1024])
    w_a2 = din("rw_a2", [NL, 64, 1024])
    pv_in = din("pv", [NL, 128, NPV])
    c_ident = din("c_ident", [128, 128])
    c_matt = din("c_matt", [128, 4, NTP])
    c_msu = din("c_msu", [128, 128])
    c_msl = din("c_msl", [128, 128])
    c_mu = din("c_mu", [128, 64])
    c_msu32 = din("c_msu32", [64, 64])
    c_msl32 = din("c_msl32", [64, 64])
    c_mu32 = din("c_mu32", [64, 32])
    c_bones = din("c_bones", [128, 128])
    c_pswap = din("c_pswap", [64, 64])
    c_pos = din("c_pos", [64, SEQ + 2 * SS])
    c_fidx = din("c_fidx", [64, 1])

    y_p = dout("y_p", [SEQ, D])
    y_s = dout("y_s", [2 * SS, D])
    lat_p = dout("lat_p", [NL, SEQ, 512])
    kr_p = dout("kr_p", [NL, SEQ, 64])
    rw_p = dout("rw_p", [NL, H, 64, 64])
    sh_p = dout("sh_p", [NL, 3200])
    cv_p = dout("cv_p", [NL, 2, 1024])
    lat_s = dout("lat_s", [NL, 2 * SS, 512])
    kr_s = dout("kr_s", [NL, 2 * SS, 64])
    rw_s = dout("rw_s", [NL, 2, H, 64, 64])
    sh_s = dout("sh_s", [NL, 2, 3200])
    cv_s = dout("cv_s", [NL, 2, 2, 1024])

    x1_p = dscr("x1_p", [SEQ, D])
    x1_s = dscr("x1_s", [2 * SS, D])
    gate_d = dscr("gate_d", [3, D])
    LK = [SEQ, LKS, LKS]
    kT_d = [[dscr(f"kT_{l}_{s}", [H, 128, LK[s]], BF16) for s in range(3)] for l in range(NL)]
    krT_d = [[dscr(f"krT_{l}_{s}", [64, LK[s]], BF16) for s in range(3)] for l in range(NL)]
    V_d = [[dscr(f"V_{l}_{s}", [LK[s], 2048], BF16) for s in range(3)] for l in range(NL)]

    def win_blocks():
        out = []
        for cc2 in range(4):
            for g in ("cv_c", "cv_x", "cv_b", "cv_gate"):
                out.append((OFF[g] + cc2 * 256, 256))
        for i in range(12):
            out.append((i * 256, 256))
        out.append((3072, 128))
        for i in range(4):
            out.append((OFF["cq"] + i * 256, 256))
        for i in range(2):
            out.append((OFF["ckv"] + i * 256, 256))
        out.append((OFF["kr"], 64))
        for h in range(0, H, 2):
            out.append((OFF["mla_gate"] + h * 128, 256))
        for i in range(4):
            out.append((OFF["rw_gate"] + i * 256, 256))
        return out
    WINB = win_blocks()
    WIN_IDX = {b: i for i, b in enumerate(WINB)}
    win_s = [dscr(f"win_s{l}", [len(WINB), 128, DC * 256], BF16) for l in range(NL)]
    wout_s = [dscr(f"wout_s{l}", [16, 128, DC * 256], BF16) for l in range(NL)]
    wuq_s = [dscr(f"wuq_s{l}", [H, 128, 8 * 192], BF16) for l in range(NL)]
    wuk_s = [dscr(f"wuk_s{l}", [128, 4 * 2048], BF16) for l in range(NL)]
    wuv_s = [dscr(f"wuv_s{l}", [4, 128, 4 * 512], BF16) for l in range(NL)]

    st = contextlib.ExitStack()
    with st:
        def sb(name, shape, dt=F32):
            return st.enter_context(nc.sbuf_tensor("sb_" + name, list(shape), dt))

        P = Prog(nc)

        def _op(eng, fn, r, w, dma=False):
            return P.op(eng, fn, reads=r, writes=w, dma=dma)

        def mm(out, lhsT, rhs, start, stop, r, w):
            _op("pe", lambda e: e.matmul(out, lhsT=lhsT, rhs=rhs, start=start, stop=stop), r, w)

        def tr(out, in_, ident_ap, r, w):
            _op("pe", lambda e: e.transpose(out, in_, ident_ap), r, w)

        def act(out, in_, func, r, w, bias=None, scale=None, accum_out=None):
            kw = {}
            if bias is not None:
                kw["bias"] = bias
            if scale is not None:
                kw["scale"] = scale
            if accum_out is not None:
                kw["accum_out"] = accum_out
            _op("act", lambda e: e.activation(out=out, in_=in_, func=func, **kw), r, w)

        def ts(eng, out, in0, s1, s2, op0, op1, r, w):
            if op1 is None:
                _op(eng, lambda e: e.tensor_scalar(out=out, in0=in0, scalar1=s1, scalar2=None, op0=op0), r, w)
            else:
                _op(eng, lambda e: e.tensor_scalar(out=out, in0=in0, scalar1=s1, scalar2=s2, op0=op0, op1=op1), r, w)

        def tt(eng, out, in0, in1, op, r, w):
            _op(eng, lambda e: e.tensor_tensor(out=out, in0=in0, in1=in1, op=op), r, w)

        def stt(out, in0, scalar, in1, op0, op1, r, w):
            _op("dve", lambda e: e.scalar_tensor_tensor(out=out, in0=in0, scalar=scalar, in1=in1, op0=op0, op1=op1), r, w)

        def cp(eng, out, in_, r, w):
            if eng == "act":
                _op("act", lambda e: e.activation(out=out, in_=in_, func=AF.Copy), r, w)
            else:
                _op(eng, lambda e: e.tensor_copy(out=out, in_=in_), r, w)

        def memset(eng, ap, val, w):
            _op(eng, lambda e: e.memset(ap, val), (), w)

        def dma(eng, out, in_, r, w):
            _op(eng, lambda e: e.dma_start(out=out, in_=in_), r, w, dma=True)

        def red(out, in_, r, w):
            _op("dve", lambda e: e.tensor_reduce(out=out, in_=in_, axis=AX.X, op=ALU.add), r, w)

        def recip(out, in_, r, w):
            _op("dve", lambda e: e.reciprocal(out=out, in_=in_), r, w)

        banks = [st.enter_context(nc.psum_tensor(f"ps{i}", [128, 512], F32)) for i in range(8)]
        NROT = 6
        rot = [0]
        busy = [False] * 8

        class Bank:
            def __init__(self, i):
                self.i = i
                self.t = banks[i]
                self.k = f"ps{i}"

            def rel(self):
                busy[self.i] = False

            def v3(self, rows, a, b):
                return self.t[0:rows, 0:a * b].rearrange("p (a b) -> p a b", a=a)

        def nb():
            i = rot[0] % NROT
            rot[0] += 1
            assert not busy[i], f"psum bank {i} still busy"
            busy[i] = True
            return Bank(i)

        ACC0 = Bank(6)
        ACC1 = Bank(7)

        evq = [0]

        def ev_eng():
            evq[0] += 1
            return "act" if evq[0] % 2 else "dve"

        hT = sb("hT", [128, DC, NTP], BF16)
        mixT = sb("mixT", [128, DC, NTP], BF16)
        wbuf = [sb(f"wbuf{i}", [128, DC, 256], BF16) for i in range(2)]
        wq = [0]
        ident = sb("ident", [128, 128])
        onesb = sb("onesb", [128, 128], BF16)
        onesf = sb("onesf", [128, 128])
        bones = sb("bones", [128, 128])
        msu = sb("msu", [128, 128]); msl = sb("msl", [128, 128]); mup = sb("mup", [128, 64]); mun = sb("mun", [128, 64])
        msu32 = sb("msu32", [64, 64]); msl32 = sb("msl32", [64, 64]); mup32 = sb("mup32", [64, 32]); mun32 = sb("mun32", [64, 32])
        pswap = sb("pswap", [64, 64])
        fidx = sb("fidx", [64, 1]); freq = sb("freq", [64, 1])
        rmask64 = sb("rmask64", [128, 512]); rmask32 = sb("rmask32", [128, 256])
        pv = sb("pv", [128, NL, NPV])
        omka = sb("omka", [128, NL, 8])
        gq = sb("gq", [128, NL, 2])
        cTb = sb("cTb", [128, DC, 3], BF16)
        modT = sb("modT", [128, 64, 3])
        s1T = sb("s1T", [128, DC, 3])
        MblkA = sb("MblkA", [128, 8, 128]); MblkB = sb("MblkB", [128, 8, 128])
        Mblk = [MblkA, MblkA, MblkB]
        MKEY = ["Mblk0", "Mblk0", "Mblk2"]
        rwprev = [sb(f"rwprev{s}", [128, 25]) for s in range(3)]
        convh = [sb(f"convh{s}", [128, 8, 2]) for s in range(3)]
        ARENA_W = 23744
        arena = sb("arena", [128, ARENA_W])

        class Carver:
            def __init__(self, tag, prev):
                self.off = 0
                self.tag = tag
                self.keys = []
                self._prev = prev

            def get(self, name, free_shape, dt=F32):
                n = int(np.prod(free_shape))
                words = (n + 1) // 2 if dt == BF16 else n
                words = (words + 15) // 16 * 16
                assert self.off + words <= ARENA_W, f"arena overflow {self.tag}:{name} {self.off + words}"
                v = arena[:, self.off:self.off + words]
                self.off += words
                if dt == BF16:
                    v = v.bitcast(BF16)
                elif dt == I32:
                    v = v.bitcast(I32)
                v = v[:, 0:n]
                if len(free_shape) == 2:
                    v = v.rearrange("p (a b) -> p a b", a=free_shape[0])
                elif len(free_shape) == 3:
                    v = v.rearrange("p (a b c) -> p a b c", a=free_shape[0], b=free_shape[1])
                k = f"{self.tag}:{name}"
                self.keys.append(k)
                return v, k

        prev_keys = [[]]
        phase_ctr = [0]

        def new_phase(tag):
            phase_ctr[0] += 1
            return Carver(f"{tag}_{phase_ctr[0]}", list(prev_keys[0]))

        def seal(c):
            P.rekey(c._prev, c.keys)
            prev_keys[0] = list(c.keys)

        dma("sp", ident[:], c_ident, (), ["ident"])
        dma("sp", bones[:], c_bones, (), ["bones"])
        dma("sp", msu[:], c_msu, (), ["msu"]); dma("sp", msl[:], c_msl, (), ["msl"]); dma("sp", mup[:], c_mu, (), ["mup"])
        dma("sp", msu32[:], c_msu32, (), ["msu32"]); dma("sp", msl32[:], c_msl32, (), ["msl32"]); dma("sp", mup32[:], c_mu32, (), ["mup32"])
        dma("sp", pswap[:], c_pswap, (), ["pswap"])
        dma("sp", fidx[:], c_fidx, (), ["fidx"])
        dma("sp", pv[:], pv_in.rearrange("l p n -> p l n"), (), ["pv"])
        dma("pool", cTb[:], cT_in, (), ["cTb"])
        memset("pool", onesb[:], 1.0, ["onesb"])
        if len(phases) < 5:
            memset("pool", mixT[:], 0.0, ["mixT"])
        memset("pool", onesf[:], 1.0, ["onesf"])
        memset("pool", rmask64[:], 1.0, ["rmask64"])
        memset("pool", rmask32[:], 1.0, ["rmask32"])
        _op("pool", lambda e: e.memset(rmask64[:].rearrange("p (a c) -> p a c", c=64)[:, :, 0:1], 0.0), ["rmask64"], ["rmask64"])
        _op("pool", lambda e: e.memset(rmask32[:].rearrange("p (a c) -> p a c", c=32)[:, :, 0:1], 0.0), ["rmask32"], ["rmask32"])
        ts("dve", mun[:], mup[:], -1.0, None, ALU.mult, None, ["mup"], ["mun"])
        ts("dve", mun32[:], mup32[:], -1.0, None, ALU.mult, None, ["mup32"], ["mun32"])
        act(freq[:], fidx[:], AF.Exp, ["fidx"], ["freq"], scale=-math.log(10000.0) / 32.0)
        for l in range(NL):
            ts("dve", omka[:, l, :], pv[:, l, PV["ka"]:PV["ka"] + 8], -1.0, 1.0, ALU.mult, ALU.add, ["pv"], ["omka"])
            ts("dve", gq[:, l, 0:1], pv[:, l, PV["qn_nope"]:PV["qn_nope"] + 1], QSCALE, None, ALU.mult, None, ["pv"], ["gq"])
            ts("dve", gq[:, l, 1:2], pv[:, l, PV["qn_rope"]:PV["qn_rope"] + 1], QSCALE, None, ALU.mult, None, ["pv"], ["gq"])

        def convert_weights(l):
            jobs = []
            for i, (c, bw) in enumerate(WINB):
                jobs.append((win_s[l][i][:, 0:DC * bw].rearrange("p (c n) -> p c n", c=DC),
                             w_in[l][:, c:c + bw].rearrange("(c p) n -> p c n", p=128), f"win_s{l}_{i}"))
            for h in range(H):
                jobs.append((wuq_s[l][h].rearrange("p (c n) -> p c n", c=8),
                             w_uq[l][:, h * 192:(h + 1) * 192].rearrange("(c p) n -> p c n", p=128), f"wuq_s{l}"))
            jobs.append((wuk_s[l].rearrange("p (c n) -> p c n", c=4), w_uk[l].rearrange("(c p) n -> p c n", p=128), f"wuk_s{l}"))
            for i in range(4):
                jobs.append((wuv_s[l][i].rearrange("p (c n) -> p c n", c=4),
                             w_uv[l][:, i * 512:(i + 1) * 512].rearrange("(c p) n -> p c n", p=128), f"wuv_s{l}"))
            for i in range(16):
                jobs.append((wout_s[l][i].rearrange("p (c n) -> p c n", c=DC),
                             w_out[l][:, i * 256:(i + 1) * 256].rearrange("(c p) n -> p c n", p=128), f"wout_s{l}_{i}"))
            return jobs

        def issue_conv(jobs, n):
            for _ in range(min(n, len(jobs))):
                o_, i_, k_ = jobs.pop(0)
                dma("pool", o_, i_, (), [k_])

        def pvc(l, name, j=0, n=1, rows=128):
            c0 = PV[name] + j
            return pv[0:rows, l, c0:c0 + n]

        def pvB(l, name, j, n, C, rows=128, p0=0):
            c0 = PV[name] + j
            return pv[p0:p0 + rows, l, c0:c0 + n].unsqueeze(2).broadcast_to([rows, n, C])

        conv_jobs = [convert_weights(l) for l in range(NL)]

        def load_w(src_ap, nch=DC, ncols=256):
            i = wq[0] % 2
            wq[0] += 1
            wb = wbuf[i]
            dma("pool", wb[:, 0:nch, 0:ncols], src_ap.rearrange("(c p) n -> p c n", p=128), (), [f"wbuf{i}"])
            return wb, f"wbuf{i}"

        def load_ws(scr_ap, scr_key, ncols=256):
            i = wq[0] % 2
            wq[0] += 1
            wb = wbuf[i]
            dma("sp", wb[:, 0:DC, 0:ncols], scr_ap[:, 0:DC * ncols].rearrange("p (c n) -> p c n", c=DC), [scr_key], [f"wbuf{i}"])
            return wb, f"wbuf{i}"

        class Tile:
            pass

        tiles = []
        for i in range(NPT):
            t = Tile()
            t.NT = NTP; t.C = 64; t.sample = False; t.idx = i
            t.segs = [(0, 0, NTP, i * NTP)]
            t.subs = [(j * 128, 128) for j in range(NTP // 128)]
            t.pos0 = i * NTP
            tiles.append(t)
        t = Tile()
        t.NT = 2 * SS; t.C = 32; t.sample = True; t.idx = NPT
        t.segs = [(1, 0, SS, PAST), (2, SS, SS, PAST)]
        t.subs = [(0, 2 * SS)]
        t.pos0 = SEQ
        tiles.append(t)

        def layer_setup(l):
            A = new_phase("ls")
            grow, growk = A.get("grow", [D])
            bgrow, bgrowk = A.get("bgrow", [D])
            seal(A)
            dma("sp", bgrow[0:3, :], b_gate[l].broadcast_to([3, D]), (), [bgrowk])
            for cb in range(48):
                wb, wk = load_w(w_ada[l][:, cb * 256:(cb + 1) * 256])
                if cb < 32:
                    for half in range(2):
                        cc = cb * 2 + half
                        b = nb()
                        for ch in range(DC):
                            mm(b.t[:, 0:3], wb[:, ch, half * 128:(half + 1) * 128], cTb[:, ch, :], ch == 0, ch == DC - 1,
                               [wk, "cTb"], [b.k])
                        bname = "b_shift" if cc < 32 else "b_scale"
                        ts("dve", modT[:, cc, :], b.t[:, 0:3], pvc(l, bname, cc % 32), None, ALU.add, None,
                           [b.k, "pv"], ["modT"])
                        b.rel()
                else:
                    g0 = (cb - 32) * 256
                    b = nb()
                    for ch in range(DC):
                        mm(b.t[0:3, 0:256], cTb[:, ch, :], wb[:, ch, :], ch == 0, ch == DC - 1, [wk, "cTb"], [b.k])
                    tt("dve", grow[0:3, g0:g0 + 256], b.t[0:3, 0:256], bgrow[0:3, g0:g0 + 256], ALU.add, [b.k, bgrowk], [growk])
                    b.rel()
            dma("pool", gate_d, grow[0:3, :], [growk], ["gate_d"])
            ts("dve", s1T[:], modT[:, 32:64, :], 1.0, None, ALU.add, None, ["modT"], ["s1T"])
            tt("dve", s1T[:], s1T[:], pvB(l, "norm_g", 0, 32, 3), ALU.mult, ["s1T", "pv"], ["s1T"])
            for s in range(3):
                if s == 0:
                    memset("pool", Mblk[0][:], 0.0, [MKEY[0]])
                    memset("pool", rwprev[0][:], 0.0, ["rwprev0"])
                    memset("pool", convh[0][:], 0.0, ["convh0"])
                else:
                    if s == 2:
                        load_state_M(l, s)
                    dma("sp", rwprev[s][:], st_shift[l, s - 1], (), [f"rwprev{s}"])
                    dma("sp", convh[s][:], st_conv[l, s - 1], (), [f"convh{s}"])

        def load_state_M(l, s):
            memset("pool", Mblk[s][:], 0.0, [MKEY[s]])
            dma("sp", Mblk[s][0:64, :, 0:64], st_M[l, s - 1, 0:64], [MKEY[s]], [MKEY[s]])
            dma("sp", Mblk[s][64:128, :, 64:128], st_M[l, s - 1, 64:128], [MKEY[s]], [MKEY[s]])

        def in_proj_fm(l, t, col0, ncols, consume):
            NT = t.NT
            j = 0
            c = col0
            while c < col0 + ncols:
                bw = min(256, col0 + ncols - c)
                bi = WIN_IDX[(c, bw)]
                wb, wk = load_ws(win_s[l][bi], f"win_s{l}_{bi}", bw)
                for h0 in range(0, bw, 128):
                    wd = min(128, bw - h0)
                    b = nb()
                    for ch in range(DC):
                        mm(b.t[0:wd, 0:NT], wb[:, ch, h0:h0 + wd], hT[:, ch, 0:NT], ch == 0, ch == DC - 1,
                           [wk, "hT"], [b.k])
                    consume(j, wd, b)
                    b.rel()
                    j += 1
                c += bw

        def phase1(l, t):
            A = new_phase("p1")
            xb = [A.get(f"x{i}", [D]) for i in range(2)]
            junk, junkk = A.get("junk", [D], BF16)
            ssq, ssqk = A.get("ssq", [4])
            rstd, rstdk = A.get("rstd", [4])
            seal(A)
            src = (x1_s if l > 0 else xs_in) if t.sample else (x1_p if l > 0 else xp)
            srck = ["x1_s" if t.sample else "x1_p"] if l > 0 else []
            for si, (r0, rows) in enumerate(t.subs):
                xt, xk = xb[si % 2]
                g0 = r0 if t.sample else t.idx * NTP + r0
                dma("sp", xt[0:rows, :], src[g0:g0 + rows, :], srck, [xk])
                act(junk[0:rows, :], xt[0:rows, :], AF.Square, [xk], [junkk, ssqk], accum_out=ssq[0:rows, si:si + 1])
                ts("dve", rstd[0:rows, si:si + 1], ssq[0:rows, si:si + 1], 1.0 / D, NORM_EPS, ALU.mult, ALU.add, [ssqk], [rstdk])
                act(rstd[0:rows, si:si + 1], rstd[0:rows, si:si + 1], AF.Sqrt, [rstdk], [rstdk])
                recip(rstd[0:rows, si:si + 1], rstd[0:rows, si:si + 1], [rstdk], [rstdk])
                ts("dve", xt[0:rows, :], xt[0:rows, :], rstd[0:rows, si:si + 1], None, ALU.mult, None, [xk, rstdk], [xk])
                for c4 in range(DC // 4):
                    b = nb()
                    for q in range(4):
                        ch = c4 * 4 + q
                        tr(b.t[:, q * 128:q * 128 + rows], xt[0:rows, ch * 128:(ch + 1) * 128], ident[0:rows, 0:rows],
                           [xk, "ident"], [b.k])
                    e = ev_eng()
                    for q in range(4):
                        ch = c4 * 4 + q
                        for (s, o, n, p0) in t.segs:
                            lo = max(o, r0); hi = min(o + n, r0 + rows)
                            if lo >= hi:
                                continue
                            src_ps = b.t[:, q * 128 + lo - r0:q * 128 + hi - r0]
                            if e == "act":
                                act(hT[:, ch, lo:hi], src_ps, AF.Identity, [b.k, "s1T", "modT"], ["hT"],
                                    scale=s1T[:, ch, s:s + 1], bias=modT[:, ch, s:s + 1])
                            else:
                                ts("dve", hT[:, ch, lo:hi], src_ps, s1T[:, ch, s:s + 1], modT[:, ch, s:s + 1], ALU.mult, ALU.add,
                                   [b.k, "s1T", "modT"], ["hT"])
                    b.rel()

        def phase2(l, t):
            NT = t.NT
            A = new_phase("p2")
            nseg = len(t.segs)
            Ls = t.segs[0][2]
            csb = [A.get(f"csb{i}", [NT]) for i in range(2)]
            ub = [A.get(f"ub{i}", [nseg, Ls + 2]) for i in range(2)]
            yb = [A.get(f"yb{i}", [NT]) for i in range(2)]
            sg = [A.get(f"sg{i}", [NT]) for i in range(2)]
            seal(A)
            cw = PV["conv_w"]
            for cc2 in range(4):
                def cons_c(j, wd, b):
                    cp("act", csb[j][0][:, :], b.t[:, 0:NT], [b.k], [csb[j][1]])
                in_proj_fm(l, t, OFF["cv_c"] + cc2 * 256, 256, cons_c)

                def cons_x(j, wd, b):
                    cc = cc2 * 2 + j
                    u, uk = ub[j]
                    y, yk = yb[j]
                    for si, (s, o, n, p0) in enumerate(t.segs):
                        cp("pool", u[:, si, 0:2], convh[s][:, cc, :], [f"convh{s}"], [uk])
                        tt("dve", u[:, si, 2:2 + n], csb[j][0][:, o:o + n], b.t[:, o:o + n], ALU.mult, [b.k, csb[j][1], uk], [uk])
                        cp("pool", convh[s][:, cc, :], u[:, si, n:n + 2], [uk], [f"convh{s}"])
                        ts("dve", y[:, o:o + n], u[:, si, 2:2 + n], pv[:, l, cw + 16 + cc:cw + 17 + cc], pvc(l, "conv_b", cc), ALU.mult, ALU.add,
                           [uk, "pv"], [yk])
                        stt(y[:, o:o + n], u[:, si, 1:1 + n], pv[:, l, cw + 8 + cc:cw + 9 + cc], y[:, o:o + n], ALU.mult, ALU.add, [uk, yk, "pv"], [yk])
                        stt(y[:, o:o + n], u[:, si, 0:n], pv[:, l, cw + cc:cw + 1 + cc], y[:, o:o + n], ALU.mult, ALU.add, [uk, yk, "pv"], [yk])
                in_proj_fm(l, t, OFF["cv_x"] + cc2 * 256, 256, cons_x)

                def cons_b(j, wd, b):
                    y, yk = yb[j]
                    tt("dve", y[:, 0:NT], y[:, 0:NT], b.t[:, 0:NT], ALU.mult, [b.k, yk], [yk])
                in_proj_fm(l, t, OFF["cv_b"] + cc2 * 256, 256, cons_b)

                def cons_g(j, wd, b):
                    cc = cc2 * 2 + j
                    y, yk = yb[j]
                    act(sg[j][0][:, :], b.t[:, 0:NT], AF.Silu, [b.k], [sg[j][1]])
                    tt("pool", mixT[:, 24 + cc, 0:NT], y[:, 0:NT], sg[j][0][:, :], ALU.mult, [yk, sg[j][1]], ["mixT"])
                in_proj_fm(l, t, OFF["cv_gate"] + cc2 * 256, 256, cons_g)

        def phase3(l, t):
            NT, C = t.NT, t.C
            W = 2 * C
            nseg = len(t.segs)
            Ls = t.segs[0][2]
            nchk = Ls // C
            nlev = int(round(math.log2(C))) - 1
            MSU, MSL, MUP, MUN = (msu, msl, mup, mun) if C == 64 else (msu32, msl32, mup32, mun32)
            mkeys = ["msu", "msl", "mup", "mun"] if C == 64 else ["msu32", "msl32", "mup32", "mun32"]
            rmask, rmk = (rmask64, "rmask64") if C == 64 else (rmask32, "rmask32")
            A = new_phase("p3")
            pre, prek = A.get("pre", [25, nseg, Ls + 2], BF16)
            w2a2, w2k = A.get("w2a2", [1024])
            x24, x24k = A.get("x24", [C])
            sig, sigk = A.get("sig", [8, C])
            a_, ak = A.get("a", [8, C])
            cs, csk = A.get("cs", [8, C])
            G, Gk = A.get("G", [8, C])
            iG, iGk = A.get("iG", [8, C])
            Gp, Gpk = A.get("Gp", [8, C])
            xr, xrk = A.get("xr", [4, C]); xk_, xkk = A.get("xk", [4, C]); xv, xvk = A.get("xv", [4, C])
            kkn, kknk = A.get("kkn", [4, C]); kmod, kmodk = A.get("kmod", [4, C]); rt, rtk = A.get("rt", [4, C])
            t1, t1k = A.get("t1", [4, C]); t2, t2k = A.get("t2", [4, C]); bon, bonk = A.get("bon", [4, C])
            blk = {}
            for nm in ("kap", "bt", "kt", "kh", "bh", "vb"):
                blk[nm] = A.get("blk_" + nm, [4, 2, C])
            prod = {}
            for nm in ("Vb", "Kt", "Bt", "X0", "A0", "A2t", "X1", "A1", "Q"):
                prod[nm] = A.get("pr_" + nm, [4, 128])
            A2R, A2Rk = A.get("A2R", [4, C]); ARn, ARnk = A.get("ARn", [4, C])
            Oq, Oqk = A.get("Oq", [512]); cen, cenk = A.get("cen", [512]); sqv, sqvk = Oq, Oqk
            st8, st8k = A.get("st8", [16])
            sgt, sgtk = cen, cenk
            seal(A)
            import os
            P3 = int(os.environ.get("P3STOP", "99"))
            if P3 <= -3:
                return
            dma("sp", w2a2[0:64, :], w_w2[l], (), [w2k])
            dma("sp", w2a2[64:128, :], w_a2[l], (), [w2k])
            if P3 <= -2:
                return
            for nm in blk:
                memset("pool", blk[nm][0][:], 0.0, [blk[nm][1]])
            if P3 <= -1:
                return

            def cons_pre(j, wd, b):
                for si, (s, o, n, p0) in enumerate(t.segs):
                    CPF = int(os.environ.get("CPF", "7"))
                    if CPF & 1:
                        cp("pool", pre[:, j, si, 1:2], rwprev[s][:, j:j + 1], [f"rwprev{s}"], [prek])
                    if CPF & 2:
                        cp(ev_eng(), pre[:, j, si, 2:2 + n], b.t[:, o:o + n], [b.k], [prek])
                    if CPF & 4:
                        cp("dve", rwprev[s][:, j:j + 1], b.t[:, o + n - 1:o + n], [b.k], [f"rwprev{s}"])
            in_proj_fm(l, t, OFF["rw_pre"], 3200, cons_pre)
            if P3 <= 0:
                return

            def lerp(eng, out, j0, nj, si, c, outk):
                cur = pre[:, j0:j0 + nj, si, 2 + c * C:2 + (c + 1) * C]
                prv = pre[:, j0:j0 + nj, si, 1 + c * C:1 + (c + 1) * C]
                tt(eng, out, prv, cur, ALU.subtract, [prek], [outk])
                tt(eng, out, out, pvB(l, "mu", j0, nj, C), ALU.mult, [outk, "pv"], [outk])
                tt(eng, out, out, cur, ALU.add, [outk, prek], [outk])

            def v3(ap2d, rows, a, b_):
                return ap2d[0:rows, 0:a * b_].rearrange("p (a b) -> p a b", a=a)

            for si, (s, o, n, p0) in enumerate(t.segs):
                Mk = MKEY[s]
                M = Mblk[s]
                for c in range(nchk):
                    tok0 = o + c * C
                    lerp("pool", x24.unsqueeze(1), 24, 1, si, c, x24k)
                    act(x24[0:64, :], x24[0:64, :], AF.Tanh, [x24k], [x24k])
                    bw = nb(); ba = nb()
                    for j in range(8):
                        mm(bw.t[:, j * C:(j + 1) * C], w2a2[0:64, j * 128:(j + 1) * 128], x24[0:64, :], True, True, [w2k, x24k], [bw.k])
                    for j in range(8):
                        mm(ba.t[:, j * C:(j + 1) * C], w2a2[64:128, j * 128:(j + 1) * 128], x24[64:128, :], True, True, [w2k, x24k], [ba.k])
                    tt("dve", sig[:], bw.v3(128, 8, C), pvB(l, "w0", 0, 8, C), ALU.add, [bw.k, "pv"], [sigk])
                    bw.rel()
                    act(sig[:], sig[:], AF.Sigmoid, [sigk], [sigk])
                    tt("dve", a_[:], ba.v3(128, 8, C), pvB(l, "a0", 0, 8, C), ALU.add, [ba.k, "pv"], [ak])
                    ba.rel()
                    act(a_[:], a_[:], AF.Sigmoid, [ak], [ak])
                    sflat = sig[:].rearrange("p a b -> p (a b)")
                    cflat = cs[:].rearrange("p a b -> p (a b)")
                    _op("dve", lambda e, cflat=cflat, sflat=sflat: e.tensor_tensor_scan(out=cflat, data0=rmask[:, 0:8 * C], data1=sflat,
                                                                                       initial=0.0, op0=ALU.mult, op1=ALU.add),
                        [rmk, sigk], [csk])
                    act(G[:], cs[:], AF.Exp, [csk], [Gk], scale=-C0)
                    act(iG[:], cs[:], AF.Exp, [csk], [iGk], scale=C0)
                    tt("pool", Gp[:], cs[:], sig[:], ALU.subtract, [csk, sigk], [Gpk])
                    act(Gp[:], Gp[:], AF.Exp, [Gpk], [Gpk], scale=-C0)
                    if P3 <= 1:
                        continue
                    for q in range(2):
                        j0 = 4 * q
                        Sq = slice(j0, j0 + 4)
                        lerp("pool", xr[:], j0, 4, si, c, xrk)
                        lerp("pool", xk_[:], 8 + j0, 4, si, c, xkk)
                        lerp("pool", xv[:], 16 + j0, 4, si, c, xvk)
                        tt("dve", t1[:], xk_[:], pvB(l, "kk", j0, 4, C), ALU.mult, [xkk, "pv"], [t1k])
                        act(t2[:], t1[:], AF.Square, [t1k], [t2k])
                        bs = nb()
                        for jj in range(4):
                            mm(bs.t[:, jj * C:(jj + 1) * C], bones[:, :], t2[:, jj, :], True, True, ["bones", t2k], [bs.k])
                        ts("dve", t2[:], bs.v3(128, 4, C), 1e-12, None, ALU.add, None, [bs.k], [t2k])
                        bs.rel()
                        act(t2[:], t2[:], AF.Sqrt, [t2k], [t2k])
                        recip(t2[:], t2[:], [t2k], [t2k])
                        tt("dve", kkn[:], t1[:], t2[:], ALU.mult, [t1k, t2k], [kknk])
                        tt("dve", t1[:], a_[:, Sq, :], pvB(l, "ka", j0, 4, C), ALU.mult, [ak, "pv"], [t1k])
                        tt("dve", t1[:], t1[:], omka[:, l, j0:j0 + 4].unsqueeze(2).broadcast_to([128, 4, C]), ALU.add, [t1k, "omka"], [t1k])
                        tt("dve", kmod[:], xk_[:], t1[:], ALU.mult, [xkk, t1k], [kmodk])
                        tt("pool", t1[:], xr[:], kmod[:], ALU.mult, [xrk, kmodk], [t1k])
                        tt("pool", t1[:], t1[:], pvB(l, "rk", j0, 4, C), ALU.mult, [t1k, "pv"], [t1k])
                        br = nb()
                        for jj in range(4):
                            mm(br.t[:, jj * C:(jj + 1) * C], bones[:, :], t1[:, jj, :], True, True, ["bones", t1k], [br.k])
                        tt("dve", bon[:], br.v3(128, 4, C), xv[:], ALU.mult, [br.k, xvk], [bonk])
                        br.rel()
                        tt("pool", bon[:], bon[:], pvB(l, "ln_b", j0, 4, C), ALU.add, [bonk, "pv"], [bonk])
                        if P3 <= 2:
                            continue
                        tt("pool", rt[:], xr[:], G[:, Sq, :], ALU.mult, [xrk, Gk], [rtk])
                        tt("pool", t1[:], kkn[:], a_[:, Sq, :], ALU.mult, [kknk, ak], [t1k])
                        for hp in range(2):
                            ps_ = slice(64 * hp, 64 * hp + 64)
                            GC = G[ps_, Sq, C - 1:C].broadcast_to([64, 4, C])
                            e1 = "dve" if hp == 0 else "pool"
                            tt(e1, blk["kap"][0][ps_, :, hp, :], kkn[ps_], Gp[ps_, Sq, :], ALU.mult, [kknk, Gpk], [blk["kap"][1]])
                            tt(e1, blk["bt"][0][ps_, :, hp, :], t1[ps_], iG[ps_, Sq, :], ALU.mult, [t1k, iGk], [blk["bt"][1]])
                            tt(e1, blk["kt"][0][ps_, :, hp, :], kmod[ps_], iG[ps_, Sq, :], ALU.mult, [kmodk, iGk], [blk["kt"][1]])
                            tt(e1, blk["kh"][0][ps_, :, hp, :], blk["kt"][0][ps_, :, hp, :], GC, ALU.mult, [blk["kt"][1], Gk], [blk["kh"][1]])
                            tt(e1, blk["bh"][0][ps_, :, hp, :], blk["bt"][0][ps_, :, hp, :], GC, ALU.mult, [blk["bt"][1], Gk], [blk["bh"][1]])
                            cp(e1, blk["vb"][0][ps_, :, hp, :], xv[ps_], [xvk], [blk["vb"][1]])

                        def B(nm, jj):
                            return blk[nm][0][:, jj, :, :].rearrange("p a b -> p (a b)")

                        def PR(nm, jj, rows=W):
                            return prod[nm][0][0:rows, jj, 0:128]

                        def PRW(nm, jj):
                            return prod[nm][0][0:W, jj, 0:W]

                        if P3 <= 3:
                            continue
                        for nm_in, nm_out, neg in (("vb", "Vb", False), ("kh", "Kt", False), ("bh", "Bt", True)):
                            b = nb()
                            for jj in range(4):
                                tr(b.t[0:W, jj * 128:(jj + 1) * 128], B(nm_in, jj), ident[:, :], [blk[nm_in][1], "ident"], [b.k])
                            dst = prod[nm_out][0][0:W, :, :]
                            if neg:
                                ts("dve", dst, b.v3(W, 4, 128), -1.0, None, ALU.mult, None, [b.k], [prod[nm_out][1]])
                            else:
                                cp("act", dst, b.v3(W, 4, 128), [b.k], [prod[nm_out][1]])
                            b.rel()
                        for (lh, rh, outnm, mask, mk) in (("bt", "kap", "X0", MSU, mkeys[0]), ("kap", "bt", "A0", MSL, mkeys[1]),
                                                          ("kt", "kap", "A2t", MSU, mkeys[0])):
                            b = nb()
                            for jj in range(4):
                                mm(b.t[0:W, jj * W:(jj + 1) * W], B(lh, jj), B(rh, jj), True, True, [blk[lh][1], blk[rh][1]], [b.k])
                            tt("dve", prod[outnm][0][0:W, :, 0:W], b.v3(W, 4, W), mask[0:W, 0:W].unsqueeze(1).broadcast_to([W, 4, W]), ALU.mult,
                               [b.k, mk], [prod[outnm][1]])
                            b.rel()
                        for (lh, out_ap, outk, mask, mk) in (("kt", A2R, A2Rk, MUP, mkeys[2]), ("bt", ARn, ARnk, MUN, mkeys[3])):
                            b = nb()
                            for jj in range(4):
                                mm(b.t[0:W, jj * C:(jj + 1) * C], B(lh, jj), rt[:, jj, :], True, True, [blk[lh][1], rtk], [b.k])
                            tt("dve", out_ap[0:W, :, :], b.v3(W, 4, C), mask[0:W, 0:C].unsqueeze(1).broadcast_to([W, 4, C]), ALU.mult,
                               [b.k, mk], [outk])
                            b.rel()
                        if P3 <= 4:
                            continue
                        tt("pool", prod["Q"][0][0:W, :, 0:W], ident[0:W, 0:W].unsqueeze(1).broadcast_to([W, 4, W]), prod["X0"][0][0:W, :, 0:W],
                           ALU.subtract, ["ident", prod["X0"][1]], [prod["Q"][1]])
                        cur = ("X0", "A0"); nxt = ("X1", "A1")
                        for lev in range(nlev):
                            last = (lev == nlev - 1)
                            if not last:
                                b1 = nb()
                                for jj in range(4):
                                    mm(b1.t[0:W, jj * W:(jj + 1) * W], PRW(cur[1], jj), PRW(cur[0], jj), True, True,
                                       [prod[cur[0]][1], prod[cur[1]][1]], [b1.k])
                            b2 = nb()
                            for jj in range(4):
                                mm(b2.t[0:W, jj * W:(jj + 1) * W], PRW(cur[0], jj), PRW(cur[1], jj), True, True,
                                   [prod[cur[0]][1], prod[cur[1]][1]], [b2.k])
                            if not last:
                                cp("act", prod[nxt[0]][0][0:W, :, 0:W], b1.v3(W, 4, W), [b1.k], [prod[nxt[0]][1]])
                                b1.rel()
                            cp("dve", prod[nxt[1]][0][0:W, :, 0:W], b2.v3(W, 4, W), [b2.k], [prod[nxt[1]][1]])
                            b2.rel()
                            b3 = nb()
                            for jj in range(4):
                                mm(b3.t[0:W, jj * W:(jj + 1) * W], PRW(nxt[1], jj), PRW("Q", jj), True, True,
                                   [prod[nxt[1]][1], prod["Q"][1]], [b3.k])
                            tt("dve", prod["Q"][0][0:W, :, 0:W], b3.v3(W, 4, W), prod["Q"][0][0:W, :, 0:W], ALU.add,
                               [b3.k, prod["Q"][1]], [prod["Q"][1]])
                            b3.rel()
                            cur, nxt = nxt, cur
                        if P3 <= 5:
                            continue
                        RHSn, Un = cur[0], nxt[0]
                        b = nb()
                        for jj in range(4):
                            mm(b.t[0:W, jj * 128:(jj + 1) * 128], B("kap", jj), M[:, j0 + jj, :], True, False, [blk["kap"][1], Mk], [b.k])
                            mm(b.t[0:W, jj * 128:(jj + 1) * 128], PRW("A2t", jj), PR("Vb", jj), False, True, [prod["A2t"][1], prod["Vb"][1]], [b.k])
                        cp("act", prod[RHSn][0][0:W, :, :], b.v3(W, 4, 128), [b.k], [prod[RHSn][1]])
                        b.rel()
                        b = nb()
                        for jj in range(4):
                            mm(b.t[0:W, jj * 128:(jj + 1) * 128], PRW("Q", jj), PR(RHSn, jj), True, True, [prod["Q"][1], prod[RHSn][1]], [b.k])
                        cp("dve", prod[Un][0][0:W, :, :], b.v3(W, 4, 128), [b.k], [prod[Un][1]])
                        b.rel()
                        bO = nb()
                        for jj in range(4):
                            mm(bO.t[0:C, jj * 128:(jj + 1) * 128], rt[:, jj, :], M[:, j0 + jj, :], True, False, [rtk, Mk], [bO.k])
                            mm(bO.t[0:C, jj * 128:(jj + 1) * 128], A2R[0:W, jj, :], PR("Vb", jj), False, False, [A2Rk, prod["Vb"][1]], [bO.k])
                            mm(bO.t[0:C, jj * 128:(jj + 1) * 128], ARn[0:W, jj, :], PR(Un, jj), False, True, [ARnk, prod[Un][1]], [bO.k])
                        cp("act", Oq[0:C, :], bO.t[0:C, 0:512], [bO.k], [Oqk])
                        bO.rel()
                        bM = nb()
                        for jj in range(4):
                            mm(bM.t[:, jj * 128:(jj + 1) * 128], PR("Kt", jj), PR("Vb", jj), True, False, [prod["Kt"][1], prod["Vb"][1]], [bM.k])
                            mm(bM.t[:, jj * 128:(jj + 1) * 128], PR("Bt", jj), PR(Un, jj), False, True, [prod["Bt"][1], prod[Un][1]], [bM.k])
                        tt("pool", M[:, Sq, :], M[:, Sq, :], G[:, Sq, C - 1:C].broadcast_to([128, 4, 128]), ALU.mult, [Mk, Gk], [Mk])
                        tt("dve", M[:, Sq, :], M[:, Sq, :], bM.v3(128, 4, 128), ALU.add, [Mk, bM.k], [Mk])
                        bM.rel()
                        if P3 <= 6:
                            continue
                        O3 = Oq[0:C, :].rearrange("p (a b) -> p a b", a=8)
                        c3 = cen[0:C, :].rearrange("p (a b) -> p a b", a=8)
                        s3 = sqv[0:C, :].rearrange("p (a b) -> p a b", a=8)
                        red(st8[0:C, 0:8], O3, [Oqk], [st8k])
                        ts("dve", st8[0:C, 0:8], st8[0:C, 0:8], 1.0 / 64, None, ALU.mult, None, [st8k], [st8k])
                        tt("dve", c3, O3, st8[0:C, 0:8].unsqueeze(2).broadcast_to([C, 8, 64]), ALU.subtract, [Oqk, st8k], [cenk])
                        act(sqv[0:C, :], cen[0:C, :], AF.Square, [cenk], [sqvk])
                        red(st8[0:C, 8:16], s3, [sqvk], [st8k])
                        ts("dve", st8[0:C, 8:16], st8[0:C, 8:16], 1.0 / 64, GN_EPS, ALU.mult, ALU.add, [st8k], [st8k])
                        act(st8[0:C, 8:16], st8[0:C, 8:16], AF.Sqrt, [st8k], [st8k])
                        recip(st8[0:C, 8:16], st8[0:C, 8:16], [st8k], [st8k])
                        tt("dve", c3, c3, st8[0:C, 8:16].unsqueeze(2).broadcast_to([C, 8, 64]), ALU.mult, [cenk, st8k], [cenk])
                        bT = nb()
                        for jj in range(4):
                            tr(bT.t[:, jj * C:(jj + 1) * C], cen[0:C, jj * 128:(jj + 1) * 128], ident[0:C, 0:C], [cenk, "ident"], [bT.k])
                        tt("dve", t2[:], bT.v3(128, 4, C), pvB(l, "ln_g", j0, 4, C), ALU.mult, [bT.k, "pv"], [t2k])
                        bT.rel()
                        tt("pool", mixT[:, j0:j0 + 4, tok0:tok0 + C], t2[:], bon[:], ALU.add, [t2k, bonk], ["mixT"])

            def cons_gate(j, wd, b):
                act(sgt[:, 0:NT], b.t[:, 0:NT], AF.Silu, [b.k], [sgtk])
                tt("dve", mixT[:, j, 0:NT], mixT[:, j, 0:NT], sgt[:, 0:NT], ALU.mult, ["mixT", sgtk], ["mixT"])
            in_proj_fm(l, t, OFF["rw_gate"], 1024, cons_gate)

        def phase4(l, t):
            NT = t.NT
            A = new_phase("p4")
            cqb, cqbk = A.get("cqb", [8, NT], BF16)
            cosT, cosk = A.get("cos", [NT]); sinT, sink = A.get("sin", [NT])
            sq = [A.get(f"sq{i}", [NTP]) for i in range(2)]
            rs = [A.get(f"rs{i}", [NTP]) for i in range(2)]
            xrq, xrqk = A.get("xrq", [NT])
            mark = A.off
            ncommon = len(A.keys)
            latT, latTk = A.get("latT", [4, NTP])
            latb, latbk = A.get("latb", [4, NTP], BF16)
            krb, krbk = A.get("krb", [NTP], BF16)
            wuk, wukk = A.get("wuk", [4, 2048], BF16)
            wuvb = [A.get(f"wuv{i}", [4, 512], BF16) for i in range(2)]
            Vp = [A.get(f"Vp{i}", [512], BF16) for i in range(2)]
            knb = [A.get(f"knb{i}", [NTP], BF16) for i in range(2)]
            lato = [A.get(f"lato{i}", [512]) for i in range(2)]
            kro, krok = A.get("kro", [64])
            ctm, ctmk = A.get("ctm", [4, 512])
            ktm, ktmk = A.get("ktm", [4, 64])
            yy, yyk = A.get("yy", [NT]); kf, kfk = A.get("kf", [NT]); ki, kik = A.get("ki", [NT], I32)
            seal(A)
            sqi = [0]

            def next_sq():
                sqi[0] += 1
                return sq[sqi[0] % 2], rs[sqi[0] % 2]

            def rms_stats(b, rows, n, div, eps):
                (s_, sk), (r_, rk) = next_sq()
                act(s_[0:rows, 0:n], b.t[0:rows, 0:n], AF.Square, [b.k], [sk])
                b2 = nb()
                mm(b2.t[0:rows, 0:n], onesf[0:rows, 0:rows], s_[0:rows, 0:n], True, True, ["onesf", sk], [b2.k])
                ts("dve", r_[0:rows, 0:n], b2.t[0:rows, 0:n], 1.0 / div, eps, ALU.mult, ALU.add, [b2.k], [rk])
                b2.rel()
                act(r_[0:rows, 0:n], r_[0:rows, 0:n], AF.Sqrt, [rk], [rk])
                recip(r_[0:rows, 0:n], r_[0:rows, 0:n], [rk], [rk])
                return r_, rk

            pos, posk = yy, yyk
            dma("sp", pos[0:64, :], c_pos[:, t.pos0:t.pos0 + NT], (), [posk])
            ts("dve", yy[0:64, :], pos[0:64, :], freq[:, 0:1], 1.0 / (2 * math.pi), ALU.mult, ALU.mult, [posk, "freq"], [yyk])
            cp("dve", ki[0:64, :], yy[0:64, :], [yyk], [kik])
            cp("dve", kf[0:64, :], ki[0:64, :], [kik], [kfk])
            tt("dve", yy[0:64, :], yy[0:64, :], kf[0:64, :], ALU.subtract, [yyk, kfk], [yyk])
            act(sinT[0:64, :], yy[0:64, :], AF.Sin, [yyk], [sink], scale=math.pi)
            act(kf[0:64, :], yy[0:64, :], AF.Sin, [yyk], [kfk], scale=math.pi / 2)
            tt("dve", kf[0:64, :], kf[0:64, :], kf[0:64, :], ALU.mult, [kfk], [kfk])
            ts("dve", kf[0:64, :], kf[0:64, :], -2.0, 1.0, ALU.mult, ALU.add, [kfk], [kfk])
            tt("dve", cosT[0:64, :], sinT[0:64, :], sinT[0:64, :], ALU.mult, [sink], [cosk])
            ts("dve", cosT[0:64, :], cosT[0:64, :], -2.0, 1.0, ALU.mult, ALU.add, [cosk], [cosk])
            tt("dve", sinT[0:64, :], sinT[0:64, :], kf[0:64, :], ALU.mult, [sink, kfk], [sink])
            ts("dve", sinT[0:32, :], sinT[0:32, :], -2.0, None, ALU.mult, None, [sink], [sink])
            ts("dve", sinT[32:64, :], sinT[32:64, :], 2.0, None, ALU.mult, None, [sink], [sink])

            def rotary(out_ap, outk, x_ap, xk, n):
                b = nb()
                mm(b.t[0:64, 0:n], pswap[:, :], x_ap, True, True, ["pswap", xk], [b.k])
                (s_, sk), _ = next_sq()
                tt("dve", s_[0:64, 0:n], b.t[0:64, 0:n], sinT[0:64, 0:n], ALU.mult, [b.k, sink], [sk])
                b.rel()
                tt("dve", x_ap, x_ap, cosT[0:64, 0:n], ALU.mult, [xk, cosk], [xk])
                tt("dve", out_ap, x_ap, s_[0:64, 0:n], ALU.add, [xk, sk], [outk])

            def cons_cq(j, wd, b):
                cp("dve", cqb[:, j, :], b.t[:, 0:NT], [b.k], [cqbk])
                (s_, sk), _ = next_sq()
                act(s_[:, 0:NT], b.t[:, 0:NT], AF.Square, [b.k], [sk])
                mm(ACC0.t[:, 0:NT], onesf[:, :], s_[:, 0:NT], j == 0, j == 7, ["onesf", sk], [ACC0.k])
            in_proj_fm(l, t, OFF["cq"], 1024, cons_cq)
            _, (r_, rk) = next_sq()
            ts("dve", r_[:, 0:NT], ACC0.t[:, 0:NT], 1.0 / 1024, NORM_EPS, ALU.mult, ALU.add, [ACC0.k], [rk])
            act(r_[:, 0:NT], r_[:, 0:NT], AF.Sqrt, [rk], [rk])
            recip(r_[:, 0:NT], r_[:, 0:NT], [rk], [rk])
            for j in range(8):
                stt(cqb[:, j, :], cqb[:, j, :], pvc(l, "qng", j), r_[:, 0:NT], ALU.mult, ALU.mult, [cqbk, "pv", rk], [cqbk])

            def cons_ckv(j, wd, b):
                cp("dve", latT[:, j, 0:NT], b.t[:, 0:NT], [b.k], [latTk])
                (s_, sk), _ = next_sq()
                act(s_[:, 0:NT], b.t[:, 0:NT], AF.Square, [b.k], [sk])
                mm(ACC1.t[:, 0:NT], onesf[:, :], s_[:, 0:NT], j == 0, j == 3, ["onesf", sk], [ACC1.k])
            in_proj_fm(l, t, OFF["ckv"], 512, cons_ckv)
            _, (r_, rk) = next_sq()
            ts("dve", r_[:, 0:NT], ACC1.t[:, 0:NT], 1.0 / 512, NORM_EPS, ALU.mult, ALU.add, [ACC1.k], [rk])
            act(r_[:, 0:NT], r_[:, 0:NT], AF.Sqrt, [rk], [rk])
            recip(r_[:, 0:NT], r_[:, 0:NT], [rk], [rk])
            for j in range(4):
                stt(latT[:, j, 0:NT], latT[:, j, 0:NT], pvc(l, "kvg", j), r_[:, 0:NT], ALU.mult, ALU.mult, [latTk, "pv", rk], [latTk])
                cp("pool", latb[:, j, 0:NT], latT[:, j, 0:NT], [latTk], [latbk])
            lat_dst = lat_s if t.sample else lat_p
            kr_dst = kr_s if t.sample else kr_p
            for si, (r0, rows) in enumerate(t.subs):
                g0 = r0 if t.sample else t.idx * NTP + r0
                b = nb()
                for j in range(4):
                    tr(b.t[0:rows, j * 128:(j + 1) * 128], latT[:, j, r0:r0 + rows], ident[:, :], [latTk, "ident"], [b.k])
                lo_, lok = lato[si % 2]
                cp("act", lo_[0:rows, :], b.t[0:rows, 0:512], [b.k], [lok])
                b.rel()
                dma("pool", lat_dst[l, g0:g0 + rows, :], lo_[0:rows, :], [lok], [])

            def cons_kr(j, wd, b):
                r_, rk = rms_stats(b, 64, NT, 64, NORM_EPS)
                stt(xrq[0:64, :], b.t[0:64, 0:NT], pvc(l, "kn_rope", 0, 1, 64), r_[0:64, 0:NT], ALU.mult, ALU.mult, [b.k, "pv", rk], [xrqk])
            in_proj_fm(l, t, OFF["kr"], 64, cons_kr)
            rotary(xrq[0:64, :], xrqk, xrq[0:64, :], xrqk, NT)
            cp("pool", krb[0:64, 0:NT], xrq[0:64, :], [xrqk], [krbk])
            for si, (r0, rows) in enumerate(t.subs):
                g0 = r0 if t.sample else t.idx * NTP + r0
                b = nb()
                tr(b.t[0:rows, 0:64], xrq[0:64, r0:r0 + rows], ident[0:64, 0:64], [xrqk, "ident"], [b.k])
                cp("act", kro[0:rows, :], b.t[0:rows, 0:64], [b.k], [krok])
                b.rel()
                dma("pool", kr_dst[l, g0:g0 + rows, :], kro[0:rows, :], [krok], [])

            dma("sp", wuk[:], wuk_s[l].rearrange("p (c n) -> p c n", c=4), [f"wuk_s{l}"], [wukk])

            def emit_kv(s, lat_ap, n, key0):
                kd = f"kT_{l}_{s}"
                vd = f"V_{l}_{s}"
                for h in range(H):
                    b = nb()
                    for ch in range(4):
                        mm(b.t[:, 0:n], wuk[:, ch, h * 128:(h + 1) * 128], lat_ap[:, ch, :], ch == 0, ch == 3, [wukk, latbk], [b.k])
                    r_, rk = rms_stats(b, 128, n, 128, NORM_EPS)
                    kb_, kbk = knb[h % 2]
                    stt(kb_[:, 0:n], b.t[:, 0:n], pvc(l, "kn_nope"), r_[:, 0:n], ALU.mult, ALU.mult, [b.k, "pv", rk], [kbk])
                    b.rel()
                    dma("pool", kT_d[l][s][h, :, key0:key0 + n], kb_[:, 0:n], [kbk], [f"{kd}_{h}"])
                for vbk in range(4):
                    wv, wvk = wuvb[vbk % 2]
                    dma("sp", wv[:], wuv_s[l][vbk].rearrange("p (c n) -> p c n", c=4), [f"wuv_s{l}"], [wvk])
                    for tb in range((n + 127) // 128):
                        m = min(128, n - tb * 128)
                        b = nb()
                        for ch in range(4):
                            mm(b.t[0:m, 0:512], lat_ap[:, ch, tb * 128:tb * 128 + m], wv[:, ch, :], ch == 0, ch == 3, [latbk, wvk], [b.k])
                        vp, vpk = Vp[(vbk * 4 + tb) % 2]
                        cp(ev_eng(), vp[0:m, :], b.t[0:m, 0:512], [b.k], [vpk])
                        b.rel()
                        dma("pool", V_d[l][s][key0 + tb * 128:key0 + tb * 128 + m, vbk * 512:(vbk + 1) * 512], vp[0:m, :], [vpk], [f"{vd}_{vbk}_{tb}"])

            if t.sample:
                for (s, o, n, p0) in t.segs:
                    for blk_ in range(PAST // 512):
                        for sub in range(4):
                            r0 = blk_ * 512 + sub * 128
                            dma("sp", ctm[:, sub, :], cache_lat[l, s - 1, r0:r0 + 128, :], (), [ctmk])
                            dma("sp", ktm[:, sub, :], cache_kr[l, s - 1, r0:r0 + 128, :], (), [ktmk])
                        for sub in range(4):
                            b = nb()
                            for j in range(4):
                                tr(b.t[:, j * 128:(j + 1) * 128], ctm[:, sub, j * 128:(j + 1) * 128], ident[:, :], [ctmk, "ident"], [b.k])
                            for j in range(4):
                                cp(ev_eng(), latb[:, j, sub * 128:(sub + 1) * 128], b.t[:, j * 128:(j + 1) * 128], [b.k], [latbk])
                            b.rel()
                        b = nb()
                        for sub in range(4):
                            tr(b.t[0:64, sub * 128:(sub + 1) * 128], ktm[:, sub, :], ident[:, :], [ktmk, "ident"], [b.k])
                        cp("act", krb[0:64, 0:512], b.t[0:64, 0:512], [b.k], [krbk])
                        b.rel()
                        dma("pool", krT_d[l][s][:, blk_ * 512:(blk_ + 1) * 512], krb[0:64, 0:512], [krbk], [f"krT_{l}_{s}"])
                        emit_kv(s, latb[:, :, 0:512], 512, blk_ * 512)
                for j in range(4):
                    cp("pool", latb[:, j, 0:NT], latT[:, j, 0:NT], [latTk], [latbk])
                cp("pool", krb[0:64, 0:NT], xrq[0:64, :], [xrqk], [krbk])
            for (s, o, n, p0) in t.segs:
                dma("pool", krT_d[l][s][:, p0:p0 + n], krb[0:64, o:o + n], [krbk], [f"krT_{l}_{s}"])
                emit_kv(s, latb[:, :, o:o + n], n, p0)

            old_keys = list(A.keys)
            A.off = mark
            nk_max = max(p0 + n for (s, o, n, p0) in t.segs)
            nkb_max = (nk_max + 127) // 128
            nseg = len(t.segs)
            k0 = len(A.keys)
            knA = [A.get(f"kn{i}", [nk_max], BF16) for i in range(2)]
            VA = [A.get(f"V{i}", [nkb_max, 128], BF16) for i in range(2)]
            krA, krAk = A.get("krA", [nseg, nk_max], BF16)
            wuqh = [A.get(f"wuq{i}", [8, 192], BF16) for i in range(2)]
            qnA = [A.get(f"qn{i}", [NT], BF16) for i in range(2)]
            qrA = [A.get(f"qr{i}", [NT], BF16) for i in range(2)]
            PT = [A.get(f"PT{i}", [NTP], BF16) for i in range(4)]
            rec, reck = A.get("rec", [NT]); tq, tqk = A.get("tq", [NT])
            sgm, sgmk = A.get("sgm", [2, NT], BF16)
            xq2, xq2k = A.get("xq2", [NT])
            if not t.sample:
                matt, mattk = A.get("matt", [4, NTP], BF16)
            newk = A.keys[k0:]
            P.rekey(old_keys[ncommon:], newk)
            prev_keys[0] = list(A.keys[:ncommon]) + newk
            if not t.sample:
                dma("pool", matt[:], c_matt, (), [mattk])
            for si, (s, o, n, p0) in enumerate(t.segs):
                dma("sp", krA[0:64, si, 0:p0 + n], krT_d[l][s][:, 0:p0 + n], [f"krT_{l}_{s}"], [krAk])
            pti = [0]

            def emit_q(h):
                wq_, wqk = wuqh[h % 2]
                qn_, qnk_ = qnA[h % 2]
                qr_, qrk_ = qrA[h % 2]
                if h == 0:
                    dma("sp", wq_[:], wuq_s[l][0].rearrange("p (c n) -> p c n", c=8), [f"wuq_s{l}"], [wqk])
                b = nb()
                for ch in range(8):
                    mm(b.t[:, 0:NT], wq_[:, ch, 0:128], cqb[:, ch, :], ch == 0, ch == 7, [wqk, cqbk], [b.k])
                r_, rk = rms_stats(b, 128, NT, 128, NORM_EPS)
                stt(qn_[:, :], b.t[:, 0:NT], gq[:, l, 0:1], r_[:, 0:NT], ALU.mult, ALU.mult, [b.k, "gq", rk], [qnk_])
                b.rel()
                b = nb()
                for ch in range(8):
                    mm(b.t[0:64, 0:NT], wq_[:, ch, 128:192], cqb[:, ch, :], ch == 0, ch == 7, [wqk, cqbk], [b.k])
                r_, rk = rms_stats(b, 64, NT, 64, NORM_EPS)
                stt(xq2[0:64, :], b.t[0:64, 0:NT], gq[0:64, l, 1:2], r_[0:64, 0:NT], ALU.mult, ALU.mult, [b.k, "gq", rk], [xq2k])
                b.rel()
                rotary(qr_[0:64, :], qrk_, xq2[0:64, :], xq2k, NT)
                if h + 2 < H:
                    dma("sp", wq_[:], wuq_s[l][h + 2].rearrange("p (c n) -> p c n", c=8), [f"wuq_s{l}"], [wqk])

            dma("sp", wuqh[1][0][:], wuq_s[l][1].rearrange("p (c n) -> p c n", c=8), [f"wuq_s{l}"], [wuqh[1][1]])
            emit_q(0)
            for h in range(H):
                if h % 2 == 0:
                    def cons_mg(j, wd, b):
                        act(sgm[:, j, :], b.t[:, 0:NT], AF.Silu, [b.k], [sgmk])
                    in_proj_fm(l, t, OFF["mla_gate"] + h * 128, 256, cons_mg)
                if h + 1 < H:
                    emit_q(h + 1)
                qn, qnk = qnA[h % 2]
                qr, qrk = qrA[h % 2]
                for si, (s, o, n, p0) in enumerate(t.segs):
                    nk = p0 + n
                    nkb = (nk + 127) // 128
                    nfull = nk // 128
                    kn, knk = knA[(h * nseg + si) % 2]
                    V, Vk = VA[(h * nseg + si) % 2]
                    dma("sp", kn[:, 0:nk], kT_d[l][s][h, :, 0:nk], [f"kT_{l}_{s}_{h}"], [knk])
                    if nfull > 0:
                        dma("sp", V[:, 0:nfull, :], V_d[l][s][0:nfull * 128, h * 128:(h + 1) * 128].rearrange("(b p) d -> p b d", p=128),
                            [f"V_{l}_{s}_{h // 4}_{tbx}" for tbx in range(4)], [Vk])
                    if nkb > nfull:
                        m_ = nk - nfull * 128
                        dma("sp", V[0:m_, nfull, :], V_d[l][s][nfull * 128:nk, h * 128:(h + 1) * 128], [f"V_{l}_{s}_{h // 4}_{tbx}" for tbx in range(4)], [Vk])
                    def geom(kb):
                        m = min(128, nk - kb * 128)
                        diag = (not t.sample) and kb >= t.idx * 4
                        jd = kb - t.idx * 4 if diag else 0
                        qlo = jd * 128 if diag else 0
                        return m, diag, jd, qlo, n - qlo

                    def scores(kb):
                        m, diag, jd, qlo, nq = geom(kb)
                        b = nb()
                        mm(b.t[0:m, 0:nq], kn[:, kb * 128:kb * 128 + m], qn[:, o + qlo:o + n], True, False, [knk, qnk], [b.k])
                        mm(b.t[0:m, 0:nq], krA[0:64, si, kb * 128:kb * 128 + m], qr[0:64, o + qlo:o + n], False, True, [krAk, qrk], [b.k])
                        return b
                    pend = []
                    nxt = [0]

                    def push():
                        if nxt[0] < nkb:
                            pend.append(scores(nxt[0]))
                            nxt[0] += 1
                    push(); push()
                    for kb in range(nkb):
                        m, diag, jd, qlo, nq = geom(kb)
                        b = pend.pop(0)
                        push()
                        pt, ptk = PT[pti[0] % 4]
                        pti[0] += 1
                        act(pt[0:m, 0:nq], b.t[0:m, 0:nq], AF.Exp, [b.k], [ptk])
                        b.rel()
                        if diag:
                            tt("pool", pt[0:m, 0:nq], pt[0:m, 0:nq], matt[0:m, jd, qlo:NTP], ALU.mult, [ptk, mattk], [ptk])
                        mm(ACC0.t[:, qlo:n], V[0:m, kb, :], pt[0:m, 0:nq], kb == 0, kb == nkb - 1, [Vk, ptk], [ACC0.k])
                        mm(ACC1.t[:, qlo:n], onesb[0:m, :], pt[0:m, 0:nq], kb == 0, kb == nkb - 1, ["onesb", ptk], [ACC1.k])
                    recip(rec[:, 0:n], ACC1.t[:, 0:n], [ACC1.k], [reck])
                    tt("dve", tq[:, 0:n], ACC0.t[:, 0:n], rec[:, 0:n], ALU.mult, [ACC0.k, reck], [tqk])
                    tt("pool", mixT[:, 8 + h, o:o + n], tq[:, 0:n], sgm[:, h % 2, o:o + n], ALU.mult, [tqk, sgmk], ["mixT"])

        def phase5(l, t):
            NT = t.NT
            nsub = len(t.subs)
            rows = t.subs[0][1]
            A = new_phase("p5")
            gbc, gbck = A.get("gbc", [D])
            xq = [A.get(f"xq{i}", [nsub, 256]) for i in range(2)]
            oq = [A.get(f"oq{i}", [nsub, 256]) for i in range(2)]
            seal(A)
            last = (l == NL - 1)
            src = (x1_s if l > 0 else xs_in) if t.sample else (x1_p if l > 0 else xp)
            dst = (y_s if last else x1_s) if t.sample else (y_p if last else x1_p)
            srck = ["x1_s" if t.sample else "x1_p"] if l > 0 else []
            dstk = [] if last else ["x1_s" if t.sample else "x1_p"]
            if t.sample:
                for (s, o, n, p0) in t.segs:
                    dma("sp", gbc[o:o + n, :], gate_d[s:s + 1, :].broadcast_to([n, D]), ["gate_d"], [gbck])
            else:
                dma("sp", gbc[:, :], gate_d[0:1, :].broadcast_to([128, D]), ["gate_d"], [gbck])
            g0 = 0 if t.sample else t.idx * NTP
            for cb in range(16):
                c0 = cb * 256
                wb, wk = load_ws(wout_s[l][cb], f"wout_s{l}_{cb}")
                xt, xk = xq[cb % 2]
                ot, ok = oq[cb % 2]
                dma("sp", xt[0:rows, :, :], src[g0:g0 + NT, c0:c0 + 256].rearrange("(s p) c -> p s c", p=rows), srck, [xk])
                for si, (r0, rws) in enumerate(t.subs):
                    b = nb()
                    for ch in range(DC):
                        mm(b.t[0:rows, 0:256], mixT[:, ch, r0:r0 + rows], wb[:, ch, :], ch == 0, ch == DC - 1, [wk, "mixT"], [b.k])
                    tt("dve", ot[0:rows, si, :], b.t[0:rows, 0:256], gbc[0:rows, c0:c0 + 256], ALU.mult, [b.k, gbck], [ok])
                    b.rel()
                tt("pool", ot[0:rows, :, :], ot[0:rows, :, :], xt[0:rows, :, :], ALU.add, [ok, xk], [ok])
                dma("pool", dst[g0:g0 + NT, c0:c0 + 256].rearrange("(s p) c -> p s c", p=rows), ot[0:rows, :, :], [ok], dstk)

        def finish_seq(l, s):
            A = new_phase("lf")
            Ts, Tsk = A.get("Ts", [8, 128])
            shs, shsk = A.get("shs", [128])
            cvs, cvsk = A.get("cvs", [2, 128])
            seal(A)
            if True:
                b = nb()
                tr(b.t[0:25, 0:128], rwprev[s][:, 0:25], ident[:, :], [f"rwprev{s}", "ident"], [b.k])
                cp("act", shs[0:25, :], b.t[0:25, 0:128], [b.k], [shsk])
                b.rel()
                dsh = sh_p[l] if s == 0 else sh_s[l, s - 1]
                dma("pool", dsh.rearrange("(c p) -> c p", p=128), shs[0:25, :], [shsk], [])
                b = nb()
                for j in range(2):
                    tr(b.t[0:8, j * 128:(j + 1) * 128], convh[s][:, :, j], ident[:, :], [f"convh{s}", "ident"], [b.k])
                cp("act", cvs[0:8, :, :], b.v3(8, 2, 128), [b.k], [cvsk])
                b.rel()
                dcv = cv_p[l] if s == 0 else cv_s[l, s - 1]
                dma("pool", dcv.rearrange("j (c p) -> c j p", p=128), cvs[0:8, :, :], [cvsk], [])
                for q in range(2):
                    b = nb()
                    for jj in range(4):
                        tr(b.t[:, jj * 128:(jj + 1) * 128], Mblk[s][:, q * 4 + jj, :], ident[:, :], [MKEY[s], "ident"], [b.k])
                    cp("act", Ts[:, q * 4:q * 4 + 4, :], b.v3(128, 4, 128), [b.k], [Tsk])
                    b.rel()
                drw = rw_p[l] if s == 0 else rw_s[l, s - 1]
                dv = drw.rearrange("(j h) v k -> h v j k", h=2)
                dma("pool", dv[0], Ts[0:64, :, 0:64], [Tsk], [])
                dma("pool", dv[1], Ts[64:128, :, 64:128], [Tsk], [])

        for l in range(NL):
            layer_setup(l)
            if l == 0:
                issue_conv(conv_jobs[0], len(conv_jobs[0]))
            per = 0
            if l + 1 < NL:
                per = (len(conv_jobs[l + 1]) + NPT - 1) // NPT
            for t in tiles:
                if per:
                    issue_conv(conv_jobs[l + 1], per)
                if t.sample:
                    finish_seq(l, 0)
                    load_state_M(l, 1)
                phase1(l, t)
                if 2 in phases:
                    phase2(l, t)
                if 3 in phases:
                    phase3(l, t)
                if 4 in phases:
                    phase4(l, t)
                phase5(l, t)
            finish_seq(l, 1)
            finish_seq(l, 2)
            if l + 1 < NL:
                issue_conv(conv_jobs[l + 1], len(conv_jobs[l + 1]))
        nops = {e: len(P.ops[e]) for e in ENGS}
        print("ops per engine:", nops, flush=True)
        P.emit()
    return nc


def _cols(v):
    v = np.asarray(v, np.float32).reshape(-1)
    return np.ascontiguousarray(v.reshape(-1, 128).T)


def _consts(SEQ):
    c = {}
    c["c_ident"] = np.eye(128, dtype=np.float32)
    k = np.arange(128)[:, None]
    q = np.arange(NTP)[None, :]
    matt = np.zeros((128, 4, NTP), np.float32)
    for j in range(4):
        matt[:, j, :] = ((j * 128 + k) // 64 <= q // 64)
    c["c_matt"] = matt

    def blockdiag(m):
        n = m.shape[0]
        z = np.zeros((2 * n, 2 * n), np.float32)
        z[:n, :n] = m
        z[n:, n:] = m
        return z
    for C, suf in ((64, ""), (32, "32")):
        s = np.arange(C)[:, None]
        t = np.arange(C)[None, :]
        su = (s < t).astype(np.float32)
        sl = (s > t).astype(np.float32)
        mu = (s <= t).astype(np.float32)
        c["c_msu" + suf] = blockdiag(su)
        c["c_msl" + suf] = blockdiag(sl)
        c["c_mu" + suf] = np.concatenate([mu, mu], 0)
    bo = np.zeros((128, 128), np.float32)
    bo[:64, :64] = 1
    bo[64:, 64:] = 1
    c["c_bones"] = bo
    ps = np.zeros((64, 64), np.float32)
    for i in range(64):
        ps[(i + 32) % 64, i] = 1
    c["c_pswap"] = ps
    pos = np.concatenate([np.arange(SEQ), PAST + np.arange(SS), PAST + np.arange(SS)]).astype(np.float32)
    c["c_pos"] = np.ascontiguousarray(np.broadcast_to(pos[None, :], (64, SEQ + 2 * SS)))
    c["c_fidx"] = (np.arange(64) % 32).astype(np.float32)[:, None]
    return c


def _pack_pv(inp, NL):
    pv = np.zeros((NL, 128, NPV), np.float32)
    for l in range(NL):
        def put(name, arr, rows=128):
            a = np.asarray(arr, np.float32)
            pv[l, :a.shape[0], PV[name]:PV[name] + a.shape[1]] = a
        put("norm_g", _cols(inp["norm_g"][l]))
        put("b_shift", _cols(inp["b_ada"][l][0:D]))
        put("b_scale", _cols(inp["b_ada"][l][D:2 * D]))
        put("mu", _cols(inp["rw_mu"][l]))
        put("w0", _cols(inp["rw_w0"][l]))
        put("a0", _cols(inp["rw_a0"][l]))
        put("kk", _cols(inp["rw_kk"][l]))
        put("ka", _cols(inp["rw_ka"][l]))
        put("rk", _cols(inp["rw_rk"][l].reshape(-1)))
        put("qng", _cols(inp["mla_qnorm_g"][l]))
        put("kvg", _cols(inp["mla_kvnorm_g"][l]))
        put("qn_nope", _cols(inp["mla_qn_nope"][l]))
        put("qn_rope", np.asarray(inp["mla_qn_rope"][l], np.float32)[:, None])
        put("kn_nope", _cols(inp["mla_kn_nope"][l]))
        put("kn_rope", np.asarray(inp["mla_kn_rope"][l], np.float32)[:, None])
        cw = np.concatenate([_cols(inp["conv_w"][l][j]) for j in range(3)], 1)
        put("conv_w", cw)
        put("conv_b", _cols(inp["conv_b"][l]))
        put("ln_g", _cols(inp["rw_ln_g"][l]))
        put("ln_b", _cols(inp["rw_ln_b"][l]))
    return pv


_NC_CACHE = {}
RUNNER = None
TRACE = False


def run(inputs, SEQ=SEQ_FULL, NL=L_FULL, phases=(1, 2, 3, 4, 5), n_cores=8):
    inp = {k: np.asarray(v) for k, v in inputs.items()}
    key = (SEQ, NL, tuple(phases))
    if key not in _NC_CACHE:
        _NC_CACHE[key] = build(SEQ, NL, phases)
    nc = _NC_CACHE[key]
    consts = _consts(SEQ)
    pv = _pack_pv(inp, NL)
    shared = dict(consts)
    shared["pv"] = pv
    f32 = lambda a: np.ascontiguousarray(a, dtype=np.float32)
    shared["w_ada"] = f32(inp["w_ada"][:NL])
    shared["b_gate"] = f32(inp["b_ada"][:NL, None, 2 * D:3 * D])
    shared["w_in"] = f32(inp["w_in"][:NL])
    shared["w_out"] = f32(inp["w_out"][:NL])
    shared["w_uq"] = f32(inp["mla_w_uq"][:NL])
    shared["w_uk"] = f32(inp["mla_w_uk"][:NL])
    shared["w_uv"] = f32(inp["mla_w_uv"][:NL])
    shared["rw_w2"] = f32(inp["rw_w2"][:NL])
    shared["rw_a2"] = f32(inp["rw_a2"][:NL])
    in_maps = []
    for i in range(n_cores):
        b = i % 4
        sbs = [2 * i, 2 * i + 1]
        m = dict(shared)
        m["xp"] = f32(inp["x_prompt"][b, :SEQ])
        m["xs"] = f32(inp["x_sample"][sbs].reshape(2 * SS, D))
        cs = np.stack([inp["c_prompt"][b], inp["c_sample"][sbs[0]], inp["c_sample"][sbs[1]]], 0)
        m["cT"] = f32(cs.reshape(3, DC, 128).transpose(2, 1, 0))
        m["cache_lat"] = f32(inp["cache_mla_latent"][:NL, sbs])
        m["cache_kr"] = f32(inp["cache_mla_krope"][:NL, sbs])
        S = inp["state_rwkv"][:NL, sbs]
        Mst = S.reshape(NL, 2, 8, 2, 64, 64).transpose(0, 1, 3, 5, 2, 4)
        m["st_M"] = f32(Mst.reshape(NL, 2, 128, 8, 64))
        sh = inp["state_rwkv_shift"][:NL, sbs]
        m["st_shift"] = f32(sh.reshape(NL, 2, 25, 128).transpose(0, 1, 3, 2))
        cv = inp["state_conv"][:NL, sbs]
        m["st_conv"] = f32(cv.reshape(NL, 2, 2, 8, 128).transpose(0, 1, 4, 3, 2))
        in_maps.append(m)
    if RUNNER is not None:
        return RUNNER(nc, in_maps)
    if TRACE:
        res = run_bass_kernel_spmd(nc, in_maps, core_ids=list(range(n_cores)), trace=True)
        print('EXEC_NS', res.exec_time_ns, flush=True)
        return res.results
    res = run_bass_kernel_spmd(nc, in_maps, core_ids=list(range(n_cores)))
    return res.results


def kernel(**inputs):
    r = run(inputs)
    NL = L_FULL
    B = 4
    f = lambda a: np.asarray(a, dtype=np.float32)
    y_p = np.stack([f(r[b]["y_p"]) for b in range(B)], 0)
    y_s = np.concatenate([f(r[i]["y_s"]).reshape(2, SS, D) for i in range(8)], 0)
    lat_p = np.stack([f(r[b]["lat_p"]) for b in range(B)], 1)
    kr_p = np.stack([f(r[b]["kr_p"]) for b in range(B)], 1)
    rw_p = np.stack([f(r[b]["rw_p"]) for b in range(B)], 1)
    sh_p = np.stack([f(r[b]["sh_p"]) for b in range(B)], 1)
    cv_p = np.stack([f(r[b]["cv_p"]) for b in range(B)], 1)
    lat_s = np.concatenate([f(r[i]["lat_s"]).reshape(NL, 2, SS, 512) for i in range(8)], 1)
    kr_s = np.concatenate([f(r[i]["kr_s"]).reshape(NL, 2, SS, 64) for i in range(8)], 1)
    rw_s = np.concatenate([f(r[i]["rw_s"]) for i in range(8)], 1)
    sh_s = np.concatenate([f(r[i]["sh_s"]) for i in range(8)], 1)
    cv_s = np.concatenate([f(r[i]["cv_s"]) for i in range(8)], 1)
    return (y_p, y_s, lat_p, kr_p, rw_p, sh_p, cv_p, lat_s, kr_s, rw_s, sh_s, cv_s)
```
